# Optimizing a Trainium2 kernel written in Bass

```python
import jax, jax.numpy as jnp
from jax import lax
import numpy as np

D_MODEL = 1024
BATCH = 1
SEQ = 16384
DEPTH = 1
DEC_BATCH = 32
DEC_SEQ = 64
PAST_LEN = 2048

CHUNK = 64
RET_HEADS = 4
RET_DK = 128
RET_DV = 128
ATTN_HEADS = 8
ATTN_DH = 64
BAND_PAST_CHUNKS = 8
BAND_PAST = BAND_PAST_CHUNKS * CHUNK
REL_CLIP = 256
D_FF = -(-8 * D_MODEL // (3 * 256)) * 256
PLE_DIM = 256
ROPE_THETA = 10000.0
EPS = 1e-6
COL_SIZES = (RET_HEADS * RET_DK, RET_HEADS * RET_DK, RET_HEADS * RET_DV, RET_HEADS * RET_DV,
             ATTN_HEADS * ATTN_DH, ATTN_HEADS * ATTN_DH, ATTN_HEADS * ATTN_DH)
IN_COLS = sum(COL_SIZES)

kernel_name = "hybrid_retention_chunkband_stream_step"


def rmsnorm(x, g):
    xf = x.astype(jnp.float32)
    y = xf * lax.rsqrt(jnp.mean(xf * xf, axis=-1, keepdims=True) + EPS) * g.astype(jnp.float32)
    return y.astype(x.dtype)


def rope(x, pos):
    half = x.shape[-1] // 2
    inv = ROPE_THETA ** (-jnp.arange(half, dtype=jnp.float32) / half)
    ang = pos[:, None] * inv[None, :]
    cos = jnp.cos(ang)[None, :, None, :]
    sin = jnp.sin(ang)[None, :, None, :]
    x1, x2 = x[..., :half], x[..., half:]
    return jnp.concatenate([x1 * cos - x2 * sin, x1 * sin + x2 * cos], axis=-1).astype(x.dtype)


def retention_log_decay():
    return jnp.log1p(-jnp.exp2(-5.0 - jnp.arange(RET_HEADS, dtype=jnp.float32)))


def retention_chunk(R, q, k, v, lg):
    T = q.shape[1]
    n = jnp.arange(T, dtype=jnp.float32)
    diff = n[:, None] - n[None, :]
    D = jnp.where(diff[None] >= 0, jnp.exp(jnp.maximum(diff, 0.0)[None] * lg[:, None, None]), 0.0)
    s = jnp.einsum('bnhd,bmhd->bhnm', q, k) * D[None]
    inner = jnp.einsum('bhnm,bmhe->bnhe', s, v)
    xi = jnp.exp((n + 1.0)[:, None] * lg[None, :])
    cross = jnp.einsum('bnhd,bhde->bnhe', q, R) * xi[None, :, :, None]
    zeta = jnp.exp((T - 1.0 - n)[:, None] * lg[None, :])
    R_new = jnp.exp(T * lg)[None, :, None, None] * R + jnp.einsum(
        'bmhd,bmhe->bhde', k * zeta[None, :, :, None], v)
    return R_new, inner + cross


def rel_bias_lookup(rel_bias, qpos, kpos):
    rel = qpos[:, None] - kpos[None, :]
    idx = jnp.clip(rel, -REL_CLIP, REL_CLIP) + REL_CLIP
    return rel_bias[:, idx].astype(jnp.float32)


def band_attention(q, k, v, valid, bias):
    s = jnp.einsum('bnthd,bnlhd->bnhtl', q, k).astype(jnp.float32) * (ATTN_DH ** -0.5) + bias[None, None]
    s = jnp.where(valid[None, :, None, None, :], s, jnp.finfo(jnp.float32).min)
    p = jax.nn.softmax(s, axis=-1)
    return jnp.einsum('bnhtl,bnlhd->bnthd', p.astype(v.dtype), v)


def project(u, w_in, g_q, g_k, pos):
    B, S, _ = u.shape
    z = u @ w_in
    splits = list(np.cumsum(COL_SIZES)[:-1])
    q_r, k_r, v_r, g_r, q_a, k_a, v_a = jnp.split(z, splits, axis=-1)
    q_r = rope(q_r.reshape(B, S, RET_HEADS, RET_DK), pos)
    k_r = rope(k_r.reshape(B, S, RET_HEADS, RET_DK), pos) * (RET_DK ** -0.5)
    v_r = v_r.reshape(B, S, RET_HEADS, RET_DV)
    q_a = rmsnorm(q_a.reshape(B, S, ATTN_HEADS, ATTN_DH), g_q)
    k_a = rmsnorm(k_a.reshape(B, S, ATTN_HEADS, ATTN_DH), g_k)
    v_a = v_a.reshape(B, S, ATTN_HEADS, ATTN_DH)
    return q_r, k_r, v_r, g_r, q_a, k_a, v_a


def merge_and_tail(x, p, o_r, g_r, o_a, g_ret_out, w_out, g_ffn, w_ffn_gate, w_ffn_up, w_ffn_down,
                   g_ple, w_ple_gate, w_ple_proj):
    B, S, _ = x.shape
    ret = jax.nn.silu(g_r) * rmsnorm(o_r, g_ret_out).reshape(B, S, RET_HEADS * RET_DV)
    mixed = jnp.concatenate([ret.astype(x.dtype), o_a.astype(x.dtype)], axis=-1)
    h = x + mixed @ w_out
    u = rmsnorm(h, g_ffn)
    h = h + (jax.nn.silu(u @ w_ffn_gate) * (u @ w_ffn_up)) @ w_ffn_down
    gate = jax.nn.sigmoid(rmsnorm(h, g_ple) @ w_ple_gate)
    return h + (p @ w_ple_proj) * gate


def mixers_prompt(u, w_in, g_q, g_k, rel_bias):
    B, S, _ = u.shape
    nC = S // CHUNK
    pos = jnp.arange(S, dtype=jnp.float32)
    q_r, k_r, v_r, g_r, q_a, k_a, v_a = project(u, w_in, g_q, g_k, pos)
    lg = retention_log_decay()

    def to_chunks(t):
        return jnp.moveaxis(t.reshape(B, nC, CHUNK, *t.shape[2:]), 1, 0)

    R0 = jnp.zeros((B, RET_HEADS, RET_DK, RET_DV), jnp.float32)
    R_fin, o_chunks = lax.scan(lambda R, xs: retention_chunk(R, xs[0], xs[1], xs[2], lg), R0,
                               (to_chunks(q_r), to_chunks(k_r), to_chunks(v_r)))
    o_r = jnp.moveaxis(o_chunks, 0, 1).reshape(B, S, RET_HEADS, RET_DV)

    pad = ((0, 0), (BAND_PAST_CHUNKS, 0), (0, 0), (0, 0), (0, 0))
    kp = jnp.pad(k_a.reshape(B, nC, CHUNK, ATTN_HEADS, ATTN_DH), pad)
    vp = jnp.pad(v_a.reshape(B, nC, CHUNK, ATTN_HEADS, ATTN_DH), pad)
    idx = jnp.arange(nC)[:, None] + jnp.arange(BAND_PAST_CHUNKS + 1)[None, :]
    L = (BAND_PAST_CHUNKS + 1) * CHUNK
    kb = kp[:, idx].reshape(B, nC, L, ATTN_HEADS, ATTN_DH)
    vb = vp[:, idx].reshape(B, nC, L, ATTN_HEADS, ATTN_DH)
    valid = jnp.repeat((idx - BAND_PAST_CHUNKS) >= 0, CHUNK, axis=1)
    bias = rel_bias_lookup(rel_bias, BAND_PAST + jnp.arange(CHUNK), jnp.arange(L))
    o_a = band_attention(q_a.reshape(B, nC, CHUNK, ATTN_HEADS, ATTN_DH), kb, vb, valid, bias)
    o_a = o_a.reshape(B, S, ATTN_HEADS * ATTN_DH)
    tail = min(BAND_PAST, S)
    return o_r, g_r, o_a, R_fin, k_a[:, S - tail:], v_a[:, S - tail:]


def mixers_sample(u, R0, ck, cv, w_in, g_q, g_k, rel_bias):
    B, T, _ = u.shape
    pos = PAST_LEN + jnp.arange(T, dtype=jnp.float32)
    q_r, k_r, v_r, g_r, q_a, k_a, v_a = project(u, w_in, g_q, g_k, pos)
    R_new, o_r = retention_chunk(R0.astype(jnp.float32), q_r, k_r, v_r, retention_log_decay())
    W = ck.shape[1]
    kb = jnp.concatenate([ck.astype(k_a.dtype), k_a], axis=1)[:, None]
    vb = jnp.concatenate([cv.astype(v_a.dtype), v_a], axis=1)[:, None]
    valid = jnp.ones((1, W + T), dtype=bool)
    bias = rel_bias_lookup(rel_bias, jnp.arange(T), jnp.arange(W + T) - W)
    o_a = band_attention(q_a[:, None], kb, vb, valid, bias)[:, 0].reshape(B, T, ATTN_HEADS * ATTN_DH)
    return o_r, g_r, o_a, R_new, k_a, v_a


def setup_inputs(seed: int = 0) -> dict:
    key = jax.random.key(seed)
    ks = jax.random.split(key, 24)
    nrm = jax.random.normal
    W = min(BAND_PAST, PAST_LEN)
    f = jnp.float32
    return {
        "x_prompt": nrm(ks[0], (BATCH, SEQ, D_MODEL), f),
        "x_sample": nrm(ks[1], (DEC_BATCH, DEC_SEQ, D_MODEL), f),
        "cache_attn_k": nrm(ks[2], (DEPTH, DEC_BATCH, W, ATTN_HEADS, ATTN_DH), f),
        "cache_attn_v": nrm(ks[3], (DEPTH, DEC_BATCH, W, ATTN_HEADS, ATTN_DH), f),
        "state_ret": 0.1 * nrm(ks[4], (DEPTH, DEC_BATCH, RET_HEADS, RET_DK, RET_DV), f),
        "p_prompt": nrm(ks[5], (DEPTH, BATCH, SEQ, PLE_DIM), f),
        "p_sample": nrm(ks[6], (DEPTH, DEC_BATCH, DEC_SEQ, PLE_DIM), f),
        "g_mix": 1.0 + 0.05 * nrm(ks[7], (DEPTH, D_MODEL), f),
        "w_in": nrm(ks[8], (DEPTH, D_MODEL, IN_COLS), f) * D_MODEL ** -0.5,
        "g_ret_out": 1.0 + 0.05 * nrm(ks[9], (DEPTH, RET_HEADS, RET_DV), f),
        "g_q_attn": 1.0 + 0.05 * nrm(ks[10], (DEPTH, ATTN_DH), f),
        "g_k_attn": 1.0 + 0.05 * nrm(ks[11], (DEPTH, ATTN_DH), f),
        "rel_bias": 0.1 * nrm(ks[12], (DEPTH, ATTN_HEADS, 2 * REL_CLIP + 1), f),
        "w_out": nrm(ks[13], (DEPTH, D_MODEL, D_MODEL), f) * D_MODEL ** -0.5,
        "g_ffn": 1.0 + 0.05 * nrm(ks[14], (DEPTH, D_MODEL), f),
        "w_ffn_gate": nrm(ks[15], (DEPTH, D_MODEL, D_FF), f) * D_MODEL ** -0.5,
        "w_ffn_up": nrm(ks[16], (DEPTH, D_MODEL, D_FF), f) * D_MODEL ** -0.5,
        "w_ffn_down": nrm(ks[17], (DEPTH, D_FF, D_MODEL), f) * D_FF ** -0.5,
        "g_ple": 1.0 + 0.05 * nrm(ks[18], (DEPTH, D_MODEL), f),
        "w_ple_gate": nrm(ks[19], (DEPTH, D_MODEL, D_MODEL), f) * D_MODEL ** -0.5,
        "w_ple_proj": nrm(ks[20], (DEPTH, PLE_DIM, D_MODEL), f) * PLE_DIM ** -0.5,
    }


def reference(x_prompt, x_sample, cache_attn_k, cache_attn_v, state_ret, p_prompt, p_sample,
              g_mix, w_in, g_ret_out, g_q_attn, g_k_attn, rel_bias, w_out, g_ffn,
              w_ffn_gate, w_ffn_up, w_ffn_down, g_ple, w_ple_gate, w_ple_proj):
    hp, hs = x_prompt, x_sample
    ret_p, kp_l, vp_l, ret_s, ks_l, vs_l = [], [], [], [], [], []
    for i in range(DEPTH):
        tail_w = (g_ret_out[i], w_out[i], g_ffn[i], w_ffn_gate[i], w_ffn_up[i], w_ffn_down[i],
                  g_ple[i], w_ple_gate[i], w_ple_proj[i])
        o_r, g_r, o_a, R_fin, k_tail, v_tail = mixers_prompt(
            rmsnorm(hp, g_mix[i]), w_in[i], g_q_attn[i], g_k_attn[i], rel_bias[i])
        hp = merge_and_tail(hp, p_prompt[i], o_r, g_r, o_a, *tail_w)
        o_r_s, g_r_s, o_a_s, R_new, k_new, v_new = mixers_sample(
            rmsnorm(hs, g_mix[i]), state_ret[i], cache_attn_k[i], cache_attn_v[i],
            w_in[i], g_q_attn[i], g_k_attn[i], rel_bias[i])
        hs = merge_and_tail(hs, p_sample[i], o_r_s, g_r_s, o_a_s, *tail_w)
        ret_p.append(R_fin); kp_l.append(k_tail); vp_l.append(v_tail)
        ret_s.append(R_new); ks_l.append(k_new); vs_l.append(v_new)
    new_ret_prompt = jnp.stack(ret_p)
    new_k_prompt = jnp.stack(kp_l)
    new_v_prompt = jnp.stack(vp_l)
    new_ret_sample = jnp.stack(ret_s)
    new_k_sample = jnp.stack(ks_l)
    new_v_sample = jnp.stack(vs_l)
    return (hp, hs, new_ret_prompt, new_k_prompt, new_v_prompt, new_ret_sample, new_k_sample, new_v_sample)
```

```python
import os
import numpy as np
from contextlib import ExitStack
import concourse.bass as bass
import concourse.mybir as mybir
from concourse.bass_utils import run_bass_kernel_spmd

F32 = mybir.dt.float32
BF16 = mybir.dt.bfloat16
ALU = mybir.AluOpType
AF = mybir.ActivationFunctionType
AX = mybir.AxisListType

NCORES = 8
D = 1024
SEQ = 16384
TPC = SEQ // NCORES
NPT = TPC // 128
NST = 2
NT = NPT + NST
DFF = 2816
NFC = DFF // 128
EPS = 1e-6
NEG = -30000.0
STAGE = int(os.environ.get("KSTAGE", "9"))
NPRE_T = 7 * 16
NPRE = int(os.environ.get("KNPRE", str(NPRE_T)))
KCUT = int(os.environ.get("KCUT", "0"))
PAST_LEN = 2048
LG = np.log1p(-np.exp2(-5.0 - np.arange(4, dtype=np.float64)))


class Op:
    __slots__ = ("eng", "fn", "deps", "signal", "val", "sem", "dma", "ext", "epoch")

    def __init__(self, eng, fn, dma):
        self.ext = None
        self.epoch = 0
        self.eng = eng
        self.fn = fn
        self.dma = dma
        self.deps = []
        self.signal = dma
        self.val = 0
        self.sem = None


class Buf:
    __slots__ = ("w", "r", "dr")

    def __init__(self):
        self.w = []
        self.r = {}
        self.dr = []


ENGS = ("pe", "act", "dve", "pool", "sp")


class Prog:
    def __init__(self, n_dma_sems):
        self.ops = {e: [] for e in ENGS}
        self.n_dma_sems = n_dma_sems
        self.dma_last = [None] * n_dma_sems
        self.dma_counts = [0, 0]
        self.epoch = 0
        self.waited = {e: {} for e in ENGS}
        self.final_waits = []

    def emit(self, eng, fn, reads=(), writes=(), dma=False, append=False, final=False):
        op = Op(eng, fn, dma)
        if getattr(self, 'muted', False):
            return op
        op.epoch = self.epoch
        deps = []
        for b in reads:
            deps += b.w
        for b in writes:
            if not append:
                deps += b.w
            deps += list(b.r.values()) + b.dr
        if dma:
            half = self.n_dma_sems // 2
            q = 0 if eng == "sp" else 1
            idx = q * half + self.dma_counts[q] % half
            self.dma_counts[q] += 1
            prev = self.dma_last[idx]
            op.sem = idx
            op.val = (prev.val if prev is not None else 0) + 16
            if prev is not None:
                deps.append(prev)
            self.dma_last[idx] = op
        seen = set()
        for d in deps:
            if id(d) in seen:
                continue
            seen.add(id(d))
            if (not d.dma) and (not dma) and d.eng == "pe" and eng == "pe":
                continue
            d.signal = True
            op.deps.append(d)
        for b in reads:
            if dma:
                b.dr.append(op)
            else:
                b.r[eng] = op
        for b in writes:
            if append:
                b.w.append(op)
            else:
                b.w = [op]
                b.r = {}
                b.dr = []
        self.ops[eng].append(op)
        if final:
            self.final_waits.append(op)
        return op

    def finalize(self):
        for e in ENGS:
            c = 0
            for op in self.ops[e]:
                if op.dma or op.ext is not None:
                    continue
                if op.signal:
                    c += 1
                    op.val = c
                    op.sem = e

    def new_epoch(self):
        self.epoch += 1

    def replay(self, eng, handle, sems, dma_sems, extra_final=False, epoch=None):
        waited = self.waited[eng]

        def wait(d):
            if d.ext is not None:
                if waited.get("ext", 0) < d.ext[1]:
                    handle.wait_ge(d.ext[0], d.ext[1])
                    waited["ext"] = d.ext[1]
                return
            key = ("d", d.sem) if d.dma else d.sem
            if waited.get(key, 0) >= d.val:
                return
            s = dma_sems[d.sem] if d.dma else sems[d.sem]
            self.wcount = getattr(self, 'wcount', {})
            self.wcount[eng] = self.wcount.get(eng, 0) + 1
            handle.wait_ge(s, d.val)
            waited[key] = d.val

        for op in self.ops[eng]:
            if epoch is not None and op.epoch != epoch:
                continue
            self.icount = getattr(self, 'icount', {})
            self.icount[eng] = self.icount.get(eng, 0) + 1
            for d in op.deps:
                wait(d)
            ins = op.fn(handle)
            if op.signal and op.ext is None:
                if op.dma:
                    ins.then_inc(dma_sems[op.sem], 16)
                else:
                    ins.then_inc(sems[op.sem], 1)
        if extra_final:
            for d in self.final_waits:
                wait(d)
            for d in self.dma_last:
                if d is not None:
                    wait(d)


def _rope_tables(core):
    half = 64
    inv = (np.float32(10000.0) ** (-np.arange(half, dtype=np.float32) / np.float32(half))).astype(np.float32)
    out = np.zeros((NT, 128, 256), np.float32)
    ks = np.float32(128.0 ** -0.5)
    for ti in range(NT):
        if ti < NPT:
            pos = (core * TPC + ti * 128 + np.arange(128)).astype(np.float32)
        else:
            pos = (PAST_LEN + (np.arange(128) % 64)).astype(np.float32)
        ang = (pos[:, None] * inv[None, :]).astype(np.float32)
        c = np.cos(ang).astype(np.float32)
        s = np.sin(ang).astype(np.float32)
        out[ti, :, 0:64] = c
        out[ti, :, 64:128] = s
        out[ti, :, 128:192] = c * ks
        out[ti, :, 192:256] = s * ks
    return out


def _prepass_tables(core):
    half = 64
    inv = (np.float32(10000.0) ** (-np.arange(half, dtype=np.float32) / np.float32(half))).astype(np.float32)
    ks = np.float32(128.0 ** -0.5)
    n0 = core * TPC
    r = np.arange(NPRE_T * 128)
    pos = (n0 - NPRE_T * 128 + r).astype(np.float32)
    ang = (pos[:, None] * inv[None, :]).astype(np.float32)
    tab = np.concatenate([np.cos(ang).astype(np.float32) * ks, np.sin(ang).astype(np.float32) * ks], axis=1)
    tab = tab.reshape(NPRE_T, 128, 128).astype(np.float32)
    zt = np.zeros((128, 4 * NPRE_T), np.float64)
    p = np.arange(128)
    for i in range(NPRE_T):
        for h in range(4):
            zt[:, 4 * i + h] = np.exp((NPRE_T * 128 - 1.0 - (i * 128 + p)) * LG[h])
    return np.ascontiguousarray(tab), zt.astype(np.float32)


def _decay_tables():
    n = np.arange(128)
    diff = n[None, :] - n[:, None]
    DT = np.zeros((2, 128, 4, 128), np.float64)
    xi = np.zeros((3, 128, 4, 128), np.float64)
    zt = np.zeros((128, 12 + 4 * NPT), np.float64)
    same = (n[None, :] // 64) == (n[:, None] // 64)
    for h in range(4):
        DT[0, :, h, :] = np.where(diff >= 0, np.exp(np.maximum(diff, 0) * LG[h]), 0.0)
        DT[1, :, h, :] = np.where((diff >= 0) & same, np.exp(np.maximum(diff, 0) * LG[h]), 0.0)
        xi[0, :, h, :] = np.exp((n + 1.0) * LG[h])[None, :]
        xs = np.exp(((n % 64) + 1.0) * LG[h])
        xi[1, :, h, :] = np.where(n < 64, xs, 0.0)[None, :]
        xi[2, :, h, :] = np.where(n >= 64, xs, 0.0)[None, :]
        zt[:, h] = np.exp((127.0 - n) * LG[h])
        zt[:, 4 + h] = np.where(n < 64, np.exp((63.0 - (n % 64)) * LG[h]), 0.0)
        zt[:, 8 + h] = np.where(n >= 64, np.exp((63.0 - (n % 64)) * LG[h]), 0.0)
        for i in range(NPT):
            zt[:, 12 + i * 4 + h] = np.exp((TPC - 1.0 - (i * 128 + n)) * LG[h])
    return (DT.reshape(2, 128, 512).astype(np.float32), xi.reshape(3, 128, 512).astype(np.float32),
            zt.astype(np.float32))


def _coef_table(core):
    co = np.zeros((2, 8, 4), np.float64)
    for cp in range(8):
        for h in range(4):
            if cp < core:
                co[0, cp, h] = np.exp(TPC * LG[h] * (core - 1 - cp))
            co[1, cp, h] = np.exp(TPC * LG[h] * (7 - cp))
    return np.broadcast_to(co.reshape(1, 64), (128, 64)).astype(np.float32).copy()


def _bias_tables(rel_bias):
    rb = np.asarray(rel_bias, np.float32).reshape(8, 513)
    l = np.arange(128)[:, None]
    t = np.arange(128)[None, :]
    bp = np.zeros((128, 5, 8, 128), np.float32)
    for j in range(5):
        d = 128 * (4 - j) + t - l
        idx = np.clip(d, -256, 256) + 256
        kc = 2 * j + (l >= 64)
        qc = 8 + (t >= 64)
        valid = (kc >= qc - 8) & (kc <= qc)
        for h in range(8):
            bp[:, j, h, :] = np.where(valid, rb[h][idx], NEG)
    bs = np.zeros((128, 9, 8, 128), np.float32)
    for j in range(4):
        da = np.clip(128 * (4 - j) + t - l, -256, 256) + 256
        db = np.clip(128 * (4 - j) + (t - 64) - l, -256, 256) + 256
        for h in range(8):
            bs[:, j, h, :] = np.where(t < 64, rb[h][da], NEG)
            bs[:, 4 + j, h, :] = np.where(t >= 64, rb[h][db], NEG)
    do = np.clip(t - l, -256, 256) + 256
    vo = ((t < 64) & (l < 64)) | ((t >= 64) & (l >= 64))
    for h in range(8):
        bs[:, 8, h, :] = np.where(vo, rb[h][do], NEG)
    return bp.reshape(128, 5 * 1024), bs.reshape(128, 9 * 1024)


def build_program():
    nc = bass.Bass("TRN2", target_bir_lowering=False)

    def din(name, shape):
        return nc.dram_tensor(name, list(shape), F32, kind="ExternalInput").ap()

    def dout(name, shape):
        return nc.dram_tensor(name, list(shape), F32, kind="ExternalOutput").ap()

    X = din("X", [512 + TPC + 256, D])
    PPd = din("PP", [NT * 128, 256])
    CK = din("CK", [4 * 512, 512])
    CV = din("CV", [4 * 512, 512])
    STd = din("ST", [4, 4, 128, 128])
    Win = din("w_in", [D, 3584])
    Wout = din("w_out", [D, D])
    Wg = din("w_gate", [D, DFF])
    Wu = din("w_up", [D, DFF])
    Wd = din("w_down", [DFF, D])
    Wpg = din("w_pg", [D, D])
    Wpp = din("w_pp", [256, D])
    gT_d = din("gT", [128, 24])
    gret_d = din("gret", [1, 512])
    gmix_d = din("gmixrow", [1, 1024])
    gqk_d = din("gqk", [1, 128])
    rope_d = din("rope", [NT, 128, 256])
    DT_d = din("DTt", [2, 128, 512])
    xi_d = din("xit", [3, 128, 512])
    zt_d = din("zt", [128, 12 + 4 * NPT])
    XP = din("XP", [NPRE_T * 128, D])
    ropeP_d = din("ropeP", [NPRE_T, 128, 128])
    ztp_d = din("ztp", [128, 4 * NPRE_T])
    bP_d = din("biasP", [128, 5 * 1024])
    bS_d = din("biasS", [128, 9 * 1024])
    hm_d = din("hmask", [128, 2])
    id_d = din("ident", [128, 128])

    Y = dout("Y", [NT * 128, D])
    KO = dout("KO", [6 * 128, 512])
    VO = dout("VO", [6 * 128, 512])
    RS = dout("RS", [4, 4, 128, 128])
    RP = dout("RP", [4, 128, 128])

    HS = nc.dram_tensor("HS", [NT * 128, D], F32)

    NDS = 40
    pg = Prog(NDS)
    E = pg.emit

    with ExitStack() as es:
        WR_t = es.enter_context(nc.sbuf_tensor("WR", [128, 77824], BF16))
        NF = 13200
        FR_t = es.enter_context(nc.sbuf_tensor("FR", [128, NF], F32))
        banks = [es.enter_context(nc.psum_tensor(f"bank{i}", [128, 512], F32)) for i in range(8)]
        bankB = [Buf() for _ in range(8)]
        sems = {e: es.enter_context(nc.semaphore(f"s_{e}")) for e in ("pe", "act", "dve", "pool")}
        dma_sems = [es.enter_context(nc.semaphore(f"d{i}")) for i in range(NDS)]

        class Carve:
            def __init__(self, t, lo, hi):
                self.t, self.off, self.hi = t, lo, hi

            def take(self, n):
                a = self.t[:, self.off:self.off + n]
                self.off += n
                assert self.off <= self.hi, (self.off, self.hi)
                return a

        w_in_sb = WR_t[:, 0:28672].rearrange("p (k n) -> p k n", k=8)
        w_out_sb = WR_t[:, 28672:36864].rearrange("p (k n) -> p k n", k=8)
        ca = Carve(WR_t, 36864, 77824)
        biasP = ca.take(5 * 1024)
        biasS = ca.take(9 * 1024)
        pT = [ca.take(1024) for _ in range(3)]
        NSLOT = 6
        knT = [ca.take(512).rearrange("p (j t) -> p j t", j=4) for _ in range(NSLOT)]
        vaug_flat = [ca.take(528) for _ in range(NSLOT)]
        vaug = [v.rearrange("p (h c) -> p h c", c=66) for v in vaug_flat]
        xb = ca.take(1024)
        uT = ca.take(1024).rearrange("p (k t) -> p k t", k=8)
        xb2 = [xb, ca.take(1024)]
        uT2 = [uT, ca.take(1024).rearrange("p (k t) -> p k t", k=8)]
        qrot = ca.take(512)
        krot = ca.take(512)
        kz = ca.take(512)
        vr = ca.take(512)
        qn = ca.take(512)
        kn = ca.take(512)
        qT = ca.take(512)
        kT = ca.take(512)
        qxT = [ca.take(512), ca.take(512)]
        sTm = ca.take(512)
        Rb = [ca.take(512), ca.take(512)]
        qnT = ca.take(512).rearrange("p (j t) -> p j t", j=4)
        qz = [ca.take(512).rearrange("p (j t) -> p j t", j=4) for _ in range(2)]
        kzb = ca.take(512)
        mixed = ca.take(1024)
        mixedT = ca.take(1024).rearrange("p (k t) -> p k t", k=8)
        kcb = ca.take(512)
        gbc = ca.take(1024)

        wg_sb = WR_t[:, 0:22528].rearrange("p (k n) -> p k n", k=8)
        wu_sb = WR_t[:, 22528:45056].rearrange("p (k n) -> p k n", k=8)
        wd_sb = WR_t[:, 45056:67584].rearrange("p (f n) -> p f n", f=NFC)
        wpg_sb = WR_t[:, 67584:75776].rearrange("p (k n) -> p k n", k=8)
        wpp_sb = WR_t[:, 75776:77824].rearrange("p (k n) -> p k n", k=2)

        cf = Carve(FR_t, 0, NF)
        xs = [cf.take(1024), cf.take(1024)]
        hbuf = cf.take(1024)
        scrA = cf.take(1024)
        gT = cf.take(24)
        gret = cf.take(512)
        gqk = cf.take(128)
        zt = cf.take(12 + 4 * NPT)
        ztp = cf.take(4 * NPRE_T)
        hmask = cf.take(2)
        ss1 = cf.take(2)
        rstd1 = cf.take(2)
        ss8 = cf.take(16)
        rstd8 = cf.take(16)
        ss4 = cf.take(4)
        rstd4 = cf.take(4)
        rec8 = cf.take(8)
        dmy = cf.take(2)
        epsc = cf.take(2)
        ident = cf.take(64).bitcast(BF16)
        junk = cf.take(512).bitcast(BF16)
        persist_end = cf.off
        ropeS = [cf.take(256), cf.take(256)]
        Gs = cf.take(512)
        scrB = cf.take(512)
        scrC = cf.take(512)
        kvout = cf.take(1024)
        Rf = [cf.take(512), cf.take(512)]
        DTs = [cf.take(512), cf.take(512)]
        xiS = [cf.take(512), cf.take(512), cf.take(512)]
        Sacc = cf.take(512)
        a_end = cf.off
        cb = Carve(FR_t, persist_end, NF)
        pS = [cb.take(256), cb.take(256)]
        gpl = cb.take(1024)
        bB = cb.take(2048 // 2 + 5632 // 2 + 1024 // 2 + 1024 // 2 + 256 // 2)
        bBb = bB.bitcast(BF16)
        o = 0
        uTB = bBb[:, o:o + 2048].rearrange("p (k t) -> p k t", k=8); o += 2048
        actT = bBb[:, o:o + 5632].rearrange("p (f t) -> p f t", f=NFC); o += 5632
        ubB = bBb[:, o:o + 1024]; o += 1024
        u2T = bBb[:, o:o + 1024].rearrange("p (k t) -> p k t", k=8); o += 1024
        ppT = bBb[:, o:o + 256].rearrange("p (k t) -> p k t", k=2); o += 256
        silu_s = cb.take(256)

        class NB:
            pass
        B = NB()
        for name in ("w_in", "w_out", "wg", "wu", "wd", "wpg", "wpp", "biasP", "biasS", "xb", "uT", "qrot", "krot",
                     "kz", "vr", "qn", "kn", "qT", "kT", "sTm", "qnT", "mixed", "mixedT", "ident", "kcb", "hbuf",
                     "scrA", "consts", "ss1", "rstd1", "ss8", "rstd8", "ss4", "rstd4", "rec8", "Gs", "scrB", "scrC",
                     "kvout", "Sacc", "gpl", "uTB", "actT", "ubB", "u2T", "ppT", "silu_s", "HS", "cc_in", "cc_out",
                     "vones", "dummy", "junk", "cc2", "qz", "kzb"):
            setattr(B, name, Buf())
        B.xs = [Buf(), Buf()]
        B.xb2 = [B.xb, Buf()]
        B.uT2 = [B.uT, Buf()]
        B.pT = [Buf() for _ in range(3)]
        B.knT = [Buf() for _ in range(NSLOT)]
        B.vaug = [Buf() for _ in range(NSLOT)]
        B.qxT = [Buf(), Buf()]
        B.Rb = [Buf(), Buf()]
        B.Rf = [Buf(), Buf()]
        B.ropeS = [Buf(), Buf()]
        B.pS = [Buf(), Buf()]
        B.HSt = [Buf() for _ in range(NT)]

        def bview(i):
            return banks[i][:, 0:512]

        def bview16(i):
            return banks[i][:, 0:512].bitcast(BF16)

        def load_const(dst, src, buf, eng="sp"):
            E(eng, lambda e, dst=dst, src=src: e.dma_start(out=dst, in_=src), writes=[buf], dma=True, append=True)

        load_const(gT, gT_d, B.consts)
        load_const(gret, gret_d.partition_broadcast(128), B.consts)
        load_const(gbc, gmix_d.partition_broadcast(128), B.consts, eng="pool")
        load_const(gqk, gqk_d.partition_broadcast(128), B.consts)
        load_const(zt, zt_d, B.consts)
        load_const(ztp, ztp_d, B.consts)
        load_const(hmask, hm_d, B.consts)
        for i in range(2):
            load_const(DTs[i], DT_d[i], B.consts)
        for i in range(3):
            load_const(xiS[i], xi_d[i], B.consts)
        load_const(ident, id_d, B.ident, eng="pool")
        Win_v = Win.rearrange("(k p) n -> p k n", p=128)
        for k in range(8):
            load_const(w_in_sb[:, k, :], Win_v[:, k, :], B.w_in, eng="pool")
        Wout_v = Wout.rearrange("(k p) n -> p k n", p=128)
        for k in range(8):
            load_const(w_out_sb[:, k, :], Wout_v[:, k, :], B.w_out, eng="pool")
        for j in range(5):
            load_const(biasP[:, j * 1024:(j + 1) * 1024], bP_d[:, j * 1024:(j + 1) * 1024], B.biasP, eng="pool")
        for j in range(9):
            load_const(biasS[:, j * 1024:(j + 1) * 1024], bS_d[:, j * 1024:(j + 1) * 1024], B.biasS, eng="pool")
        for s in range(NSLOT):
            E("pool", lambda e, s=s: e.memset(vaug_flat[s], 1.0), writes=[B.vaug[s]])

        E("pool", lambda e: e.memset(epsc, EPS), writes=[B.consts], append=True)
        for _q in range(2):
            E("pool", lambda e, _q=_q: e.memset(qz[_q], 0.0), writes=[B.qz], append=(_q == 1))
        gq_b = gqk[:, 0:64].unsqueeze(1).to_broadcast([128, 8, 64])
        gk_b = gqk[:, 64:128].unsqueeze(1).to_broadcast([128, 8, 64])

        def front(row0, slot, gcol, src_dram, xbuf_t, xbuf_B, xb_t=None, xb_B=None, uT_t=None, uT_B=None, tb=2,
                  ntok=128, tok_off=0, gfold=False):
            xb_t = xb if xb_t is None else xb_t
            xb_B = B.xb if xb_B is None else xb_B
            uT_t = uT if uT_t is None else uT_t
            uT_B = B.uT if uT_B is None else uT_B
            if src_dram is not None:
                E("sp", lambda e: e.dma_start(out=xbuf_t, in_=src_dram[row0:row0 + 128, :]), writes=[xbuf_B], dma=True)
            E("act", lambda e: e.activation(out=junk, in_=xbuf_t, func=AF.Square, accum_out=ss1[:, 0:1]),
              reads=[xbuf_B], writes=[B.junk, B.ss1])
            E("act", lambda e: e.activation(out=rstd1[:, 0:1], in_=ss1[:, 0:1], func=AF.Sqrt, bias=epsc[:, 0:1], scale=1.0 / D), reads=[B.ss1, B.consts], writes=[B.rstd1])
            E("dve", lambda e: e.reciprocal(out=rstd1[:, 0:1], in_=rstd1[:, 0:1]), reads=[B.rstd1], writes=[B.rstd1])
            if gfold:
                E("dve", lambda e: e.scalar_tensor_tensor(out=xb_t, in0=xbuf_t, scalar=rstd1[:, 0:1], in1=gbc,
                                                          op0=ALU.mult, op1=ALU.mult),
                  reads=[xbuf_B, B.rstd1, B.consts], writes=[xb_B])
            else:
                E("dve", lambda e: e.tensor_scalar(out=xb_t, in0=xbuf_t, scalar1=rstd1[:, 0:1], scalar2=None,
                                                   op0=ALU.mult), reads=[xbuf_B, B.rstd1], writes=[xb_B])
            tv = bview16(tb)
            for k in range(8):
                E("pe", lambda e, k=k: e.transpose(out=tv[:, k * 128:(k + 1) * 128], in_=xb_t[:, k * 128:(k + 1) * 128],
                                                   identity=ident), reads=[xb_B, B.ident], writes=[bankB[tb]])
            if gfold:
                E("act", lambda e: e.activation(out=uT_t[:, :, tok_off:tok_off + 128],
                                                in_=tv.rearrange("p (k t) -> p k t", k=8), func=AF.Copy),
                  reads=[bankB[tb]], writes=[uT_B])
            else:
                E("dve", lambda e: e.tensor_tensor(out=uT_t[:, :, tok_off:tok_off + 128],
                                                   in0=tv.rearrange("p (k t) -> p k t", k=8),
                                                   in1=gT[:, gcol:gcol + 8].unsqueeze(2).to_broadcast([128, 8, 128]),
                                                   op=ALU.mult),
                  reads=[bankB[tb], B.consts], writes=[uT_B])

        cur_par = [0]

        def zmm(bank_i, col):
            bv = bview(bank_i)
            uTp, uTB_ = uT2[cur_par[0]], B.uT2[cur_par[0]]
            for k in range(8):
                E("pe", lambda e, k=k: e.matmul(out=bv, lhsT=uTp[:, k, :], rhs=w_in_sb[:, k, col * 512:(col + 1) * 512],
                                                start=(k == 0), stop=(k == 7)),
                  reads=[uTB_, B.w_in], writes=[bankB[bank_i]])

        def rope(bank_i, rS, rB, coff, out_t, out_B):
            z = bview(bank_i).rearrange("p (h d) -> p h d", h=4)
            x1 = z[:, :, 0:64]
            x2 = z[:, :, 64:128]
            cosb = rS[:, coff:coff + 64].unsqueeze(1).to_broadcast([128, 4, 64])
            sinb = rS[:, coff + 64:coff + 128].unsqueeze(1).to_broadcast([128, 4, 64])
            t1 = scrB[:, 0:256].rearrange("p (h d) -> p h d", h=4)
            t2 = scrB[:, 256:512].rearrange("p (h d) -> p h d", h=4)
            t3 = scrC[:, 0:256].rearrange("p (h d) -> p h d", h=4)
            t4 = scrC[:, 256:512].rearrange("p (h d) -> p h d", h=4)
            ov = out_t.rearrange("p (h d) -> p h d", h=4)
            rd = [bankB[bank_i], rB]
            E("dve", lambda e: e.tensor_tensor(out=t1, in0=x1, in1=cosb, op=ALU.mult), reads=rd, writes=[B.scrB])
            E("dve", lambda e: e.tensor_tensor(out=t2, in0=x2, in1=sinb, op=ALU.mult), reads=rd, writes=[B.scrB],
              append=True)
            E("dve", lambda e: e.tensor_tensor(out=t3, in0=x1, in1=sinb, op=ALU.mult), reads=rd, writes=[B.scrC])
            E("dve", lambda e: e.tensor_tensor(out=t4, in0=x2, in1=cosb, op=ALU.mult), reads=rd, writes=[B.scrC],
              append=True)
            E("dve", lambda e: e.tensor_tensor(out=ov[:, :, 0:64], in0=t1, in1=t2, op=ALU.subtract),
              reads=[B.scrB], writes=[out_B])
            E("dve", lambda e: e.tensor_tensor(out=ov[:, :, 64:128], in0=t3, in1=t4, op=ALU.add),
              reads=[B.scrC], writes=[out_B], append=True)

        def load_rope(ti, slot):
            E("pool", lambda e: e.dma_start(out=ropeS[slot], in_=rope_d[ti]), writes=[B.ropeS[slot]], dma=True)

        SB = 7
        if KCUT == 1:
            pg.muted = True
        def pre_front(i):
            slot = i % 2
            E("pool", lambda e, i=i, slot=slot: e.dma_start(out=ropeS[slot][:, 0:128], in_=ropeP_d[i]),
              writes=[B.ropeS[slot]], dma=True)
            front(i * 128, slot, 0, XP, xs[slot], B.xs[slot], xb_t=xb2[slot], xb_B=B.xb2[slot], uT_t=uT2[slot],
                  uT_B=B.uT2[slot], tb=2 + slot, gfold=True)

        def pre_smm(i):
            for h in range(4):
                E("pe", lambda e, h=h, i=i: e.matmul(out=bview(SB)[:, h * 128:(h + 1) * 128],
                                                     lhsT=kz[:, h * 128:(h + 1) * 128],
                                                     rhs=vr[:, h * 128:(h + 1) * 128],
                                                     start=(i == 0 and h == 0), stop=(i == NPRE - 1),
                                                     skip_group_check=True),
                  reads=[B.kz, B.vr], writes=[bankB[SB]])

        pre_front(0)
        for i in range(NPRE):
            slot = i % 2
            if i + 1 < NPRE:
                pre_front(i + 1)
            cur_par[0] = slot
            zmm(0, 1)
            zmm(1, 2)
            if i > 0:
                pre_smm(i - 1)
            rope(0, ropeS[slot], B.ropeS[slot], 0, krot, B.krot)
            E("dve", lambda e, i=i: e.tensor_tensor(
                out=kz.rearrange("p (h d) -> p h d", h=4), in0=krot.rearrange("p (h d) -> p h d", h=4),
                in1=ztp[:, 4 * i:4 * i + 4].unsqueeze(2).to_broadcast([128, 4, 128]), op=ALU.mult),
              reads=[B.krot, B.consts], writes=[B.kz])
            E("dve", lambda e: e.tensor_copy(out=vr, in_=bview(1)), reads=[bankB[1]], writes=[B.vr])
        pre_smm(NPRE - 1)
        cur_par[0] = 0
        if KCUT == 3:
            pg.muted = True
        E("dve", lambda e: e.tensor_copy(out=Sacc, in_=bview(SB)), reads=[bankB[SB]], writes=[B.Sacc])

        def load_sample_state(sq, which):
            src = STd[sq].rearrange("h p v -> p h v")
            E("sp", lambda e: e.dma_start(out=Rf[which].rearrange("p (h v) -> p h v", h=4), in_=src),
              writes=[B.Rf[which]], dma=True)
            E("act", lambda e: e.activation(out=Rb[which], in_=Rf[which], func=AF.Copy),
              reads=[B.Rf[which]], writes=[B.Rb[which]])

        def kv_project(own_slot, want_out, out_row):
            zmm(0, 5)
            zmm(1, 6)
            kz_ = bview(0)
            E("act", lambda e: e.activation(out=scrA[:, 0:512], in_=kz_, func=AF.Square),
              reads=[bankB[0]], writes=[B.scrA])
            E("dve", lambda e: e.tensor_reduce(out=ss8[:, 8:16], in_=scrA[:, 0:512].rearrange("p (h d) -> p h d", h=8),
                                               axis=AX.X, op=ALU.add), reads=[B.scrA], writes=[B.ss8])
            E("act", lambda e: e.activation(out=rstd8[:, 8:16], in_=ss8[:, 8:16], func=AF.Sqrt, bias=epsc[:, 0:1], scale=1.0 / 64), reads=[B.ss8, B.consts], writes=[B.rstd8])
            E("dve", lambda e: e.reciprocal(out=rstd8[:, 8:16], in_=rstd8[:, 8:16]), reads=[B.rstd8], writes=[B.rstd8])
            E("dve", lambda e: e.tensor_tensor(out=scrA[:, 512:1024].rearrange("p (h d) -> p h d", h=8),
                                               in0=kz_.rearrange("p (h d) -> p h d", h=8),
                                               in1=rstd8[:, 8:16].unsqueeze(2).to_broadcast([128, 8, 64]),
                                               op=ALU.mult), reads=[bankB[0], B.rstd8], writes=[B.scrA])
            E("dve", lambda e: e.tensor_tensor(out=kvout[:, 0:512].rearrange("p (h d) -> p h d", h=8),
                                               in0=scrA[:, 512:1024].rearrange("p (h d) -> p h d", h=8),
                                               in1=gk_b, op=ALU.mult), reads=[B.scrA, B.consts], writes=[B.kvout])
            E("act", lambda e: e.activation(out=kn, in_=kvout[:, 0:512], func=AF.Copy),
              reads=[B.kvout], writes=[B.kn])
            E("act", lambda e: e.activation(out=kvout[:, 512:1024], in_=bview(1), func=AF.Copy),
              reads=[bankB[1]], writes=[B.kvout], append=True)
            E("act", lambda e: e.activation(out=vaug[own_slot][:, :, 0:64],
                                            in_=bview(1).rearrange("p (h d) -> p h d", h=8), func=AF.Copy),
              reads=[bankB[1]], writes=[B.vaug[own_slot]])
            tv = bview16(2)
            for j in range(4):
                E("pe", lambda e, j=j: e.transpose(out=tv[:, j * 128:(j + 1) * 128], in_=kn[:, j * 128:(j + 1) * 128],
                                                   identity=ident), reads=[B.kn, B.ident], writes=[bankB[2]])
            E("act", lambda e: e.activation(out=knT[own_slot], in_=tv[:, 0:512].rearrange("p (j t) -> p j t", j=4),
                                            func=AF.Copy), reads=[bankB[2]], writes=[B.knT[own_slot]])
            if want_out:
                E("pool", lambda e: e.dma_start(out=KO[out_row:out_row + 128, :], in_=kvout[:, 0:512]),
                  reads=[B.kvout], dma=True, final=True)
                E("pool", lambda e: e.dma_start(out=VO[out_row:out_row + 128, :], in_=kvout[:, 512:1024]),
                  reads=[B.kvout], dma=True, final=True)

        def load_cache_tile(sq, j, slot):
            r0 = sq * 512 + j * 128
            E("pool", lambda e: e.dma_start(out=kcb, in_=CK[r0:r0 + 128, :]), writes=[B.kcb], dma=True)
            E("pool", lambda e: e.dma_start(out=vaug[slot][:, :, 0:64],
                                            in_=CV[r0:r0 + 128, :].rearrange("p (h d) -> p h d", h=8)),
              writes=[B.vaug[slot]], dma=True)
            tv = bview16(2)
            for jj in range(4):
                E("pe", lambda e, jj=jj: e.transpose(out=tv[:, jj * 128:(jj + 1) * 128],
                                                     in_=kcb[:, jj * 128:(jj + 1) * 128], identity=ident),
                  reads=[B.kcb, B.ident], writes=[bankB[2]])
            E("act", lambda e: e.activation(out=knT[slot], in_=tv[:, 0:512].rearrange("p (j t) -> p j t", j=4),
                                            func=AF.Copy), reads=[bankB[2]], writes=[B.knT[slot]])

        pT_ctr = [0]

        def attention(keys):
            oA, oB = 4, 5
            nk = len(keys)
            def scores(jn):
                slot, bias_ap, bias_B, mcol, loader = keys[jn]
                if loader is not None:
                    loader()
                pair = (6, 7) if jn % 2 == 0 else (0, 1)
                for h in range(8):
                    bk = pair[h // 4]
                    E("pe", lambda e, h=h, bk=bk, slot=slot: e.matmul(
                        out=bview(bk)[:, (h % 4) * 128:(h % 4 + 1) * 128],
                        lhsT=knT[slot][:, h // 2, :], rhs=qz[h % 2][:, h // 2, :],
                        start=True, stop=True), reads=[B.knT[slot], B.qz], writes=[bankB[bk]])

            scores(0)
            for jn, (slot, bias_ap, bias_B, mcol, loader) in enumerate(keys):
                pair = (6, 7) if jn % 2 == 0 else (0, 1)
                ps = pT_ctr[0] % 3
                pT_ctr[0] += 1
                for half in range(2):
                    bk = pair[half]
                    E("dve", lambda e, bk=bk, half=half, bias_ap=bias_ap: e.scalar_tensor_tensor(
                        out=bview(bk), in0=bview(bk), scalar=0.125, in1=bias_ap[:, half * 512:(half + 1) * 512],
                        op0=ALU.mult, op1=ALU.add), reads=[bankB[bk], bias_B], writes=[bankB[bk]])
                    E("act", lambda e, bk=bk, half=half, ps=ps, mcol=mcol: e.activation(
                        out=pT[ps][:, half * 512:(half + 1) * 512], in_=bview(bk), func=AF.Exp, bias=mcol, scale=1.0),
                      reads=[bankB[bk], B.consts], writes=[B.pT[ps]], append=(half == 1))
                if jn + 1 < nk:
                    scores(jn + 1)
                for h in range(8):
                    ob = oA if h < 4 else oB
                    E("pe", lambda e, h=h, ob=ob, ps=ps, slot=slot, jn=jn: e.matmul(
                        out=bview(ob)[:, (h % 4) * 65:(h % 4) * 65 + 65], lhsT=pT[ps][:, h * 128:(h + 1) * 128],
                        rhs=vaug[slot][:, h, 0:65], start=(jn == 0 and h % 4 == 0), stop=(jn == nk - 1),
                        skip_group_check=True), reads=[B.pT[ps], B.vaug[slot]], writes=[bankB[ob]])
            cut(184)
            for half, ob in enumerate((oA, oB)):
                ov = bview(ob)[:, 0:260].rearrange("p (h c) -> p h c", c=65)
                E("dve", lambda e, ov=ov, half=half: e.reciprocal(out=rec8[:, half * 4:half * 4 + 4], in_=ov[:, :, 64]),
                  reads=[bankB[ob]], writes=[B.rec8], append=(half == 1))
            for half, ob in enumerate((oA, oB)):
                ov = bview(ob)[:, 0:260].rearrange("p (h c) -> p h c", c=65)
                E("dve", lambda e, ov=ov, half=half: e.tensor_tensor(
                    out=mixed[:, 512 + half * 256:512 + half * 256 + 256].rearrange("p (h d) -> p h d", h=4),
                    in0=ov[:, :, 0:64],
                    in1=rec8[:, half * 4:half * 4 + 4].unsqueeze(2).to_broadcast([128, 4, 64]), op=ALU.mult),
                  reads=[bankB[ob], B.rec8], writes=[B.mixed], append=True)

        tile_calls = [0]

        def cut(n):
            if KCUT == n and tile_calls[0] == int(os.environ.get('KTILE', '1')):
                pg.muted = True

        def tileA(ti, row0, slot, own_slot, keys, sample, want_out, out_row, seqs=None):
            tile_calls[0] += 1
            load_rope(ti, slot)
            if sample:
                load_sample_state(seqs[0], 0)
                load_sample_state(seqs[1], 1)
            front(row0, slot, 0, X, xs[slot], B.xs[slot], gfold=True)
            rS, rB = ropeS[slot], B.ropeS[slot]
            cut(10)
            zmm(0, 0)
            rope(0, rS, rB, 0, qrot, B.qrot)
            zmm(1, 1)
            rope(1, rS, rB, 128, krot, B.krot)
            zc = 4 if sample else 0
            E("dve", lambda e: e.tensor_tensor(
                out=kz.rearrange("p (h d) -> p h d", h=4), in0=krot.rearrange("p (h d) -> p h d", h=4),
                in1=zt[:, zc:zc + 4].unsqueeze(2).to_broadcast([128, 4, 128]), op=ALU.mult),
              reads=[B.krot, B.consts], writes=[B.kz])
            if sample:
                E("dve", lambda e: e.tensor_tensor(
                    out=kzb.rearrange("p (h d) -> p h d", h=4), in0=krot.rearrange("p (h d) -> p h d", h=4),
                    in1=zt[:, 8:12].unsqueeze(2).to_broadcast([128, 4, 128]), op=ALU.mult),
                  reads=[B.krot, B.consts], writes=[B.kzb])
            zmm(0, 2)
            E("act", lambda e: e.activation(out=vr, in_=bview(0), func=AF.Copy), reads=[bankB[0]], writes=[B.vr])
            zmm(1, 3)
            if os.environ.get('KV') == '1':
                E("act", lambda e: e.activation(out=Gs, in_=bview(0), func=AF.Exp, scale=-1.0), reads=[bankB[0]],
                  writes=[B.Gs])
            elif os.environ.get('KV') == '3':
                E("act", lambda e: e.activation(out=Gs, in_=hbuf[:, 0:512], func=AF.Copy), reads=[B.hbuf],
                  writes=[B.Gs])
            elif os.environ.get('KV') == '2':
                E("dve", lambda e: e.tensor_copy(out=Gs, in_=bview(1)), reads=[bankB[1]], writes=[B.Gs])
            else:
                E("act", lambda e: e.activation(out=Gs, in_=bview(1), func=AF.Exp, scale=-1.0), reads=[bankB[1]],
                  writes=[B.Gs])
            E("dve", lambda e: e.tensor_scalar(out=Gs, in0=Gs, scalar1=1.0, scalar2=None, op0=ALU.add),
              reads=[B.Gs], writes=[B.Gs])
            E("dve", lambda e: e.reciprocal(out=Gs, in_=Gs), reads=[B.Gs], writes=[B.Gs])
            E("dve", lambda e: e.tensor_tensor(out=Gs, in0=bview(1), in1=Gs, op=ALU.mult),
              reads=[bankB[1], B.Gs], writes=[B.Gs])
            E("dve", lambda e: e.tensor_tensor(out=Gs, in0=Gs, in1=gret, op=ALU.mult),
              reads=[B.Gs, B.consts], writes=[B.Gs])
            cut(11)
            tv = bview16(2)
            for h in range(4):
                E("pe", lambda e, h=h: e.transpose(out=tv[:, h * 128:(h + 1) * 128], in_=qrot[:, h * 128:(h + 1) * 128],
                                                   identity=ident), reads=[B.qrot, B.ident], writes=[bankB[2]])
            for h in range(4):
                E("pe", lambda e, h=h: e.transpose(out=tv[:, 512 + h * 128:512 + (h + 1) * 128],
                                                   in_=krot[:, h * 128:(h + 1) * 128], identity=ident),
                  reads=[B.krot, B.ident], writes=[bankB[2]])
            E("act", lambda e: e.activation(out=qT, in_=tv[:, 0:512], func=AF.Copy), reads=[bankB[2]], writes=[B.qT])
            E("act", lambda e: e.activation(out=kT, in_=tv[:, 512:1024], func=AF.Copy), reads=[bankB[2]],
              writes=[B.kT])
            cut(12)
            zmm(0, 4)
            E("act", lambda e: e.activation(out=scrA[:, 0:512], in_=bview(0), func=AF.Square),
              reads=[bankB[0]], writes=[B.scrA])
            E("dve", lambda e: e.tensor_reduce(out=ss8[:, 0:8], in_=scrA[:, 0:512].rearrange("p (h d) -> p h d", h=8),
                                               axis=AX.X, op=ALU.add), reads=[B.scrA], writes=[B.ss8])
            E("act", lambda e: e.activation(out=rstd8[:, 0:8], in_=ss8[:, 0:8], func=AF.Sqrt, bias=epsc[:, 0:1], scale=1.0 / 64), reads=[B.ss8, B.consts], writes=[B.rstd8])
            E("dve", lambda e: e.reciprocal(out=rstd8[:, 0:8], in_=rstd8[:, 0:8]), reads=[B.rstd8], writes=[B.rstd8])
            E("dve", lambda e: e.tensor_tensor(out=scrA[:, 512:1024].rearrange("p (h d) -> p h d", h=8),
                                               in0=bview(0).rearrange("p (h d) -> p h d", h=8),
                                               in1=rstd8[:, 0:8].unsqueeze(2).to_broadcast([128, 8, 64]),
                                               op=ALU.mult), reads=[bankB[0], B.rstd8], writes=[B.scrA])
            E("dve", lambda e: e.tensor_tensor(out=qn.rearrange("p (h d) -> p h d", h=8),
                                               in0=scrA[:, 512:1024].rearrange("p (h d) -> p h d", h=8),
                                               in1=gq_b, op=ALU.mult), reads=[B.scrA, B.consts], writes=[B.qn])
            for j in range(4):
                E("pe", lambda e, j=j: e.transpose(out=tv[:, j * 128:(j + 1) * 128], in_=qn[:, j * 128:(j + 1) * 128],
                                                   identity=ident), reads=[B.qn, B.ident], writes=[bankB[2]])
            tvq = tv[:, 0:512].rearrange("p (j t) -> p j t", j=4)
            E("act", lambda e: e.activation(out=qz[0][0:64], in_=tvq[0:64], func=AF.Copy), reads=[bankB[2]],
              writes=[B.qz])
            E("dve", lambda e: e.tensor_copy(out=qz[1][64:128], in_=tvq[64:128]), reads=[bankB[2]],
              writes=[B.qz], append=True)
            cut(13)
            kv_project(own_slot, want_out, out_row)
            cut(14)
            dti = 1 if sample else 0
            for h in range(4):
                E("pe", lambda e, h=h: e.matmul(out=bview(3)[:, h * 128:(h + 1) * 128],
                                                lhsT=kT[:, h * 128:(h + 1) * 128], rhs=qT[:, h * 128:(h + 1) * 128],
                                                start=True, stop=True), reads=[B.kT, B.qT], writes=[bankB[3]])
            E("dve", lambda e: e.tensor_tensor(out=sTm, in0=bview(3), in1=DTs[dti], op=ALU.mult),
              reads=[bankB[3], B.consts], writes=[B.sTm])
            nst = 2 if sample else 1
            for s in range(nst):
                xidx = 1 + s if sample else 0
                E("dve", lambda e, s=s, xidx=xidx: e.tensor_tensor(out=qxT[s], in0=qT, in1=xiS[xidx], op=ALU.mult),
                  reads=[B.qT, B.consts], writes=[B.qxT[s]])
            for h in range(4):
                hs = slice(h * 128, (h + 1) * 128)
                E("pe", lambda e, hs=hs: e.matmul(out=bview(4)[:, hs], lhsT=sTm[:, hs], rhs=vr[:, hs],
                                                  start=True, stop=False), reads=[B.sTm, B.vr], writes=[bankB[4]])
                for s in range(nst):
                    E("pe", lambda e, hs=hs, s=s: e.matmul(out=bview(4)[:, hs], lhsT=qxT[s][:, hs], rhs=Rb[s][:, hs],
                                                           start=False, stop=(s == nst - 1)),
                      reads=[B.qxT[s], B.Rb[s]], writes=[bankB[4]])
            cut(15)
            T = 64 if sample else 128
            for s in range(nst):
                kzs, kzB = (kzb, B.kzb) if (sample and s == 1) else (kz, B.kz)
                for h in range(4):
                    hs = slice(h * 128, (h + 1) * 128)
                    E("pe", lambda e, hs=hs, kzs=kzs: e.matmul(out=bview(5)[:, hs], lhsT=kzs[:, hs], rhs=vr[:, hs],
                                                               start=True, stop=True),
                      reads=[kzB, B.vr], writes=[bankB[5]])
                for h in range(4):
                    hs = slice(h * 128, (h + 1) * 128)
                    gam = float(np.exp(T * LG[h]))
                    E("dve", lambda e, hs=hs, gam=gam, s=s: e.scalar_tensor_tensor(
                        out=Rf[s][:, hs], in0=Rf[s][:, hs], scalar=gam, in1=bview(5)[:, hs], op0=ALU.mult,
                        op1=ALU.add), reads=[B.Rf[s], bankB[5]], writes=[B.Rf[s]], append=(h > 0))
                if sample:
                    E("pool", lambda e, s=s: e.dma_start(out=RS[seqs[s]].rearrange("h p v -> p h v"),
                                                         in_=Rf[s].rearrange("p (h v) -> p h v", h=4)),
                      reads=[B.Rf[s]], dma=True, final=True)
                else:
                    E("act", lambda e, s=s: e.activation(out=Rb[s], in_=Rf[s], func=AF.Copy),
                      reads=[B.Rf[s]], writes=[B.Rb[s]])
            cut(16)
            for h in range(4):
                hs = slice(h * 128, (h + 1) * 128)
                E("act", lambda e, hs=hs, h=h: e.activation(out=scrA[:, hs], in_=bview(4)[:, hs], func=AF.Square,
                                                            accum_out=ss4[:, h:h + 1]),
                  reads=[bankB[4]], writes=[B.scrA, B.ss4], append=(h > 0))
            E("act", lambda e: e.activation(out=rstd4, in_=ss4, func=AF.Sqrt, bias=epsc[:, 0:1], scale=1.0 / 128), reads=[B.ss4, B.consts], writes=[B.rstd4])
            E("dve", lambda e: e.reciprocal(out=rstd4, in_=rstd4), reads=[B.rstd4], writes=[B.rstd4])
            for h in range(4):
                hs = slice(h * 128, (h + 1) * 128)
                E("dve", lambda e, hs=hs, h=h: e.scalar_tensor_tensor(
                    out=mixed[:, hs], in0=bview(4)[:, hs], scalar=rstd4[:, h:h + 1], in1=Gs[:, hs], op0=ALU.mult,
                    op1=ALU.mult), reads=[bankB[4], B.rstd4, B.Gs], writes=[B.mixed], append=True)
            cut(17)
            attention(keys)
            cut(18)
            for k in range(8):
                E("pe", lambda e, k=k: e.transpose(out=tv[:, k * 128:(k + 1) * 128], in_=mixed[:, k * 128:(k + 1) * 128],
                                                   identity=ident), reads=[B.mixed, B.ident], writes=[bankB[2]])
            E("act", lambda e: e.activation(out=mixedT, in_=tv.rearrange("p (k t) -> p k t", k=8), func=AF.Copy),
              reads=[bankB[2]], writes=[B.mixedT])
            for nb in range(2):
                for k in range(8):
                    E("pe", lambda e, k=k, nb=nb: e.matmul(out=bview(nb), lhsT=mixedT[:, k, :],
                                                           rhs=w_out_sb[:, k, nb * 512:(nb + 1) * 512],
                                                           start=(k == 0), stop=(k == 7)),
                      reads=[B.mixedT, B.w_out], writes=[bankB[nb]])
                E("dve", lambda e, nb=nb: e.tensor_tensor(out=hbuf[:, nb * 512:(nb + 1) * 512], in0=bview(nb),
                                                          in1=xs[slot][:, nb * 512:(nb + 1) * 512], op=ALU.add),
                  reads=[bankB[nb], B.xs[slot]], writes=[B.hbuf], append=(nb == 1))
            E("pool", lambda e: e.dma_start(out=HS[ti * 128:(ti + 1) * 128, :], in_=hbuf), reads=[B.hbuf],
              writes=[B.HSt[ti]], dma=True)

        zero_col = hmask[:, 1:2]
        halo_col = hmask[:, 0:1]

        slot_ctr = 0
        for st in range(NST if (STAGE >= 2 and not os.environ.get('KNOSAMP')) else 0):
            ti = NPT + st
            seqs = (2 * st, 2 * st + 1)
            own = 5
            keys = []
            for s in range(2):
                for j in range(4):
                    sl = (s * 4 + j) % 5
                    keys.append((sl, biasS[:, (s * 4 + j) * 1024:(s * 4 + j + 1) * 1024], B.biasS, zero_col,
                                 (lambda sq=seqs[s], j=j, sl=sl: load_cache_tile(sq, j, sl))))
            keys.append((own, biasS[:, 8 * 1024:9 * 1024], B.biasS, zero_col, None))
            tileA(ti, 512 + TPC + st * 128, st % 2, own, keys, True, True, (4 + st) * 128, seqs=seqs)

        E("dve", lambda e: e.tensor_copy(out=Rf[0], in_=Sacc), reads=[B.Sacc], writes=[B.Rf[0]])
        E("act", lambda e: e.activation(out=Rb[0], in_=Sacc, func=AF.Copy), reads=[B.Sacc], writes=[B.Rb[0]])
        for _i in range(int(os.environ.get('KEXTRA', '0'))):
            if _i % 16 == 0 and os.environ.get('KEPOCH'):
                pg.new_epoch()
            if os.environ.get('KEXD'):
                E("dve", lambda e: e.tensor_copy(out=junk, in_=hbuf), reads=[B.hbuf], writes=[B.junk])
            elif os.environ.get('KEXP'):
                E("act", lambda e: e.activation(out=junk[:, 0:512], in_=bview(0), func=AF.Copy), reads=[bankB[0]], writes=[B.junk])
            else:
                E("act", lambda e: e.activation(out=junk, in_=hbuf, func=AF.Copy), reads=[B.hbuf], writes=[B.junk])
        for g in range(4 if STAGE >= 3 else 0):
            slot = g % 2
            front(g * 128, slot, 0, X, xs[slot], B.xs[slot], gfold=True)
            kv_project(g % NSLOT, False, 0)
        for i in range((NPT if STAGE >= 5 else 1) if STAGE >= 4 else 0):
            gq = 4 + i
            keys = []
            for j in range(5):
                g = gq - 4 + j
                keys.append((g % NSLOT, biasP[:, j * 1024:(j + 1) * 1024], B.biasP, halo_col if g < 4 else zero_col,
                             None))
            want = i >= NPT - 4
            tileA(i, 512 + i * 128, i % 2, gq % NSLOT, keys, False, want, (i - (NPT - 4)) * 128 if want else 0)

        if STAGE >= 5:
            E("pool", lambda e: e.dma_start(out=RP.rearrange("h p v -> p h v"),
                                            in_=Rf[0].rearrange("p (h v) -> p h v", h=4)),
              reads=[B.Rf[0]], dma=True, final=True)
        def wload(dst, src, buf):
            E("pool", lambda e: e.dma_start(out=dst, in_=src), writes=[buf], dma=True, append=True)

        allA = [B.w_in, B.w_out, B.biasP, B.biasS, B.xb, B.uT, B.qrot, B.krot, B.kz, B.vr, B.qn, B.kn, B.qT, B.kT,
                B.sTm, B.qnT, B.qz, B.kzb, B.xb2[1], B.uT2[1], B.mixed, B.mixedT, B.kcb] + B.pT + B.knT + B.vaug + B.qxT + B.Rb
        allAf = [B.Gs, B.scrB, B.scrC, B.kvout, B.Sacc] + B.Rf + B.ropeS
        allBp = [B.gpl, B.uTB, B.actT, B.ubB, B.u2T, B.ppT, B.silu_s] + B.pS
        first = [True]

        def wload_all(dst, src, buf):
            extra = (allA + allAf) if first[0] else []
            first[0] = False
            E("pool", lambda e: e.dma_start(out=dst, in_=src), writes=[buf] + extra, dma=True, append=not extra)

        if STAGE < 6:
            pg.muted = True
        E("pool", lambda e: e.memset(dmy[:, 1:2], 0.0), writes=allA + allAf + allBp + [B.dummy, B.consts])
        barrier_deps = [B.dummy]
        Wg_v = Wg.rearrange("(k p) n -> p k n", p=128)
        Wu_v = Wu.rearrange("(k p) n -> p k n", p=128)
        Wd_v = Wd.rearrange("(f p) n -> p f n", p=128)
        Wpg_v = Wpg.rearrange("(k p) n -> p k n", p=128)
        Wpp_v = Wpp.rearrange("(k p) n -> p k n", p=128)
        for k in range(8):
            E("pool", lambda e, k=k: e.dma_start(out=wg_sb[:, k, :], in_=Wg_v[:, k, :]), reads=barrier_deps,
              writes=[B.wg], dma=True, append=True)
            E("pool", lambda e, k=k: e.dma_start(out=wu_sb[:, k, :], in_=Wu_v[:, k, :]), reads=barrier_deps,
              writes=[B.wu], dma=True, append=True)
        for f in range(NFC):
            E("pool", lambda e, f=f: e.dma_start(out=wd_sb[:, f, :], in_=Wd_v[:, f, :]), reads=barrier_deps,
              writes=[B.wd], dma=True, append=True)
        for k in range(8):
            E("pool", lambda e, k=k: e.dma_start(out=wpg_sb[:, k, :], in_=Wpg_v[:, k, :]), reads=barrier_deps,
              writes=[B.wpg], dma=True, append=True)
        for k in range(2):
            E("pool", lambda e, k=k: e.dma_start(out=wpp_sb[:, k, :], in_=Wpp_v[:, k, :]), reads=barrier_deps,
              writes=[B.wpp], dma=True, append=True)

        NBLK = NT // 2
        for blk in range((NBLK if STAGE >= 7 else 1) if STAGE >= 6 else 0):
            tiles = (2 * blk, 2 * blk + 1)
            for q, ti in enumerate(tiles):
                E("sp", lambda e, q=q, ti=ti: e.dma_start(out=xs[q], in_=HS[ti * 128:(ti + 1) * 128, :]),
                  reads=[B.HSt[ti], B.dummy], writes=[B.xs[q]], dma=True)
                E("sp", lambda e, q=q, ti=ti: e.dma_start(out=pS[q], in_=PPd[ti * 128:(ti + 1) * 128, :]),
                  reads=[B.dummy], writes=[B.pS[q]], dma=True)
                front(0, q, 8, None, xs[q], B.xs[q], xb_t=ubB, xb_B=B.ubB, uT_t=uTB, uT_B=B.uTB, tb=0, tok_off=q * 128)
            for f in range(NFC):
                gb = 1 + (f % 2) * 2
                ub_ = gb + 1
                for (bk, wsb, wB) in ((gb, wg_sb, B.wg), (ub_, wu_sb, B.wu)):
                    for k in range(8):
                        E("pe", lambda e, k=k, bk=bk, wsb=wsb, f=f: e.matmul(
                            out=bview(bk)[:, 0:256], lhsT=wsb[:, k, f * 128:(f + 1) * 128],
                            rhs=uTB[:, k, :], start=(k == 0), stop=(k == 7)),
                          reads=[B.uTB, wB], writes=[bankB[bk]])
                E("act", lambda e, gb=gb: e.activation(out=silu_s, in_=bview(gb)[:, 0:256], func=AF.Sigmoid),
                  reads=[bankB[gb]], writes=[B.silu_s])
                E("dve", lambda e, gb=gb: e.tensor_tensor(out=silu_s, in0=bview(gb)[:, 0:256], in1=silu_s,
                                                          op=ALU.mult),
                  reads=[bankB[gb], B.silu_s], writes=[B.silu_s])
                E("dve", lambda e, ub_=ub_, f=f: e.tensor_tensor(out=actT[:, f, :], in0=bview(ub_)[:, 0:256],
                                                                 in1=silu_s, op=ALU.mult),
                  reads=[bankB[ub_], B.silu_s], writes=[B.actT], append=(f > 0))
            for q, ti in enumerate(tiles):
                for nb in range(2):
                    bk = 5 + nb
                    for f in range(NFC):
                        E("pe", lambda e, f=f, nb=nb, bk=bk, q=q: e.matmul(
                            out=bview(bk), lhsT=actT[:, f, q * 128:(q + 1) * 128],
                            rhs=wd_sb[:, f, nb * 512:(nb + 1) * 512], start=(f == 0), stop=(f == NFC - 1)),
                          reads=[B.actT, B.wd], writes=[bankB[bk]])
                    E("dve", lambda e, nb=nb, bk=bk, q=q: e.tensor_tensor(
                        out=hbuf[:, nb * 512:(nb + 1) * 512], in0=bview(bk), in1=xs[q][:, nb * 512:(nb + 1) * 512],
                        op=ALU.add), reads=[bankB[bk], B.xs[q]], writes=[B.hbuf], append=(nb == 1))
                front(0, 0, 16, None, hbuf, B.hbuf, xb_t=ubB, xb_B=B.ubB, uT_t=u2T, uT_B=B.u2T, tb=0)
                E("act", lambda e, q=q: e.activation(out=ubB[:, 0:256], in_=pS[q], func=AF.Copy),
                  reads=[B.pS[q]], writes=[B.ubB])
                tv0 = bview16(0)
                for k in range(2):
                    E("pe", lambda e, k=k: e.transpose(out=tv0[:, k * 128:(k + 1) * 128],
                                                       in_=ubB[:, k * 128:(k + 1) * 128], identity=ident),
                      reads=[B.ubB, B.ident], writes=[bankB[0]])
                E("act", lambda e: e.activation(out=ppT, in_=tv0[:, 0:256].rearrange("p (k t) -> p k t", k=2),
                                                func=AF.Copy), reads=[bankB[0]], writes=[B.ppT])
                for nb in range(2):
                    gbk = 7
                    pbk = 0
                    for k in range(8):
                        E("pe", lambda e, k=k, nb=nb: e.matmul(out=bview(gbk), lhsT=u2T[:, k, :],
                                                               rhs=wpg_sb[:, k, nb * 512:(nb + 1) * 512],
                                                               start=(k == 0), stop=(k == 7)),
                          reads=[B.u2T, B.wpg], writes=[bankB[gbk]])
                    E("act", lambda e, nb=nb: e.activation(out=gpl[:, nb * 512:(nb + 1) * 512], in_=bview(gbk),
                                                           func=AF.Sigmoid), reads=[bankB[gbk]], writes=[B.gpl])
                    for k in range(2):
                        E("pe", lambda e, k=k, nb=nb: e.matmul(out=bview(pbk), lhsT=ppT[:, k, :],
                                                               rhs=wpp_sb[:, k, nb * 512:(nb + 1) * 512],
                                                               start=(k == 0), stop=(k == 1)),
                          reads=[B.ppT, B.wpp], writes=[bankB[pbk]])
                    E("dve", lambda e, nb=nb: e.tensor_tensor(out=gpl[:, nb * 512:(nb + 1) * 512], in0=bview(pbk),
                                                              in1=gpl[:, nb * 512:(nb + 1) * 512], op=ALU.mult),
                      reads=[bankB[pbk], B.gpl], writes=[B.gpl])
                    E("dve", lambda e, nb=nb: e.tensor_tensor(out=scrA[:, nb * 512:(nb + 1) * 512],
                                                              in0=gpl[:, nb * 512:(nb + 1) * 512],
                                                              in1=hbuf[:, nb * 512:(nb + 1) * 512], op=ALU.add),
                      reads=[B.gpl, B.hbuf], writes=[B.scrA], append=(nb == 1))
                E("pool", lambda e, ti=ti: e.dma_start(out=Y[ti * 128:(ti + 1) * 128, :], in_=scrA),
                  reads=[B.scrA], dma=True, final=True)

        pg.finalize()
        if os.environ.get('KVERB'):
            print('SIGCOUNTS', {e: max([o.val for o in pg.ops[e] if not o.dma and o.ext is None] + [0]) for e in ENGS}, 'dma', [d.val if d else 0 for d in pg.dma_last])
        n_ep = pg.epoch + 1
        for ep in range(n_ep):
            last = (ep == n_ep - 1)
            with nc.Block() as block:
                @block.tensor
                def _(pe):
                    pg.replay("pe", pe, sems, dma_sems, epoch=ep)

                @block.scalar
                def _(act):
                    pg.replay("act", act, sems, dma_sems, epoch=ep)

                @block.vector
                def _(dve):
                    pg.replay("dve", dve, sems, dma_sems, epoch=ep)

                @block.gpsimd
                def _(pool):
                    pg.replay("pool", pool, sems, dma_sems, epoch=ep)

                @block.sync
                def _(sp):
                    pg.replay("sp", sp, sems, dma_sems, extra_final=last, epoch=ep)
    if os.environ.get('KVERB'):
        print('ICOUNT', pg.icount, 'WCOUNT', pg.wcount)
    return nc


_NC_CACHE = {}


def _get_nc():
    if "nc" not in _NC_CACHE:
        _NC_CACHE["nc"] = build_program()
    return _NC_CACHE["nc"]


def make_in_maps(x_prompt, x_sample, cache_attn_k, cache_attn_v, state_ret, p_prompt, p_sample,
                 g_mix, w_in, g_ret_out, g_q_attn, g_k_attn, rel_bias, w_out, g_ffn,
                 w_ffn_gate, w_ffn_up, w_ffn_down, g_ple, w_ple_gate, w_ple_proj):
    f = lambda a: np.ascontiguousarray(np.asarray(a, dtype=np.float32))
    xp = f(x_prompt).reshape(SEQ, D)
    xsm = f(x_sample).reshape(32 * 64, D)
    pp = f(p_prompt).reshape(SEQ, 256)
    psm = f(p_sample).reshape(32 * 64, 256)
    ck = f(cache_attn_k).reshape(32, 512, 512)
    cv = f(cache_attn_v).reshape(32, 512, 512)
    st = f(state_ret).reshape(32, 4, 128, 128)
    gT = np.concatenate([f(g).reshape(8, 128).T for g in (g_mix, g_ffn, g_ple)], axis=1)
    gqk = np.concatenate([f(g_q_attn).reshape(1, 64), f(g_k_attn).reshape(1, 64)], axis=1)
    DTt, xit, zt = _decay_tables()
    bP, bS = _bias_tables(rel_bias)
    ident = np.eye(128, dtype=np.float32)
    shared = {
        "w_in": f(w_in).reshape(D, 3584), "w_out": f(w_out).reshape(D, D), "w_gate": f(w_ffn_gate).reshape(D, DFF),
        "w_up": f(w_ffn_up).reshape(D, DFF), "w_down": f(w_ffn_down).reshape(DFF, D),
        "w_pg": f(w_ple_gate).reshape(D, D), "w_pp": f(w_ple_proj).reshape(256, D),
        "gT": np.ascontiguousarray(gT), "gret": f(g_ret_out).reshape(1, 512), "gmixrow": f(g_mix).reshape(1, 1024), "gqk": np.ascontiguousarray(gqk),
        "DTt": DTt, "xit": xit, "zt": zt, "biasP": bP, "biasS": bS, "ident": ident,
    }
    maps = []
    for c in range(NCORES):
        halo = xp[c * TPC - 512:c * TPC] if c > 0 else np.zeros((512, D), np.float32)
        Xc = np.concatenate([halo, xp[c * TPC:(c + 1) * TPC], xsm[c * 256:(c + 1) * 256]], axis=0)
        PPc = np.concatenate([pp[c * TPC:(c + 1) * TPC], psm[c * 256:(c + 1) * 256]], axis=0)
        XPc = np.zeros((NPRE_T * 128, D), np.float32)
        if c > 0:
            XPc[NPRE_T * 128 - c * TPC:] = xp[0:c * TPC]
        ropeP, ztp = _prepass_tables(c)
        hm = np.zeros((128, 2), np.float32)
        if c == 0:
            hm[:, 0] = NEG
        m = dict(shared)
        m.update({
            "X": np.ascontiguousarray(Xc), "PP": np.ascontiguousarray(PPc),
            "CK": np.ascontiguousarray(ck[4 * c:4 * c + 4].reshape(2048, 512)),
            "CV": np.ascontiguousarray(cv[4 * c:4 * c + 4].reshape(2048, 512)),
            "ST": np.ascontiguousarray(st[4 * c:4 * c + 4]),
            "rope": _rope_tables(c), "hmask": hm, "XP": XPc, "ropeP": ropeP, "ztp": ztp,
        })
        maps.append(m)
    return maps


def assemble(results):
    y_p = np.concatenate([r["Y"][0:TPC] for r in results], axis=0).reshape(1, SEQ, D)
    y_s = np.concatenate([r["Y"][TPC:TPC + 256] for r in results], axis=0).reshape(32, 64, D)
    ret_p = results[7]["RP"].reshape(1, 1, 4, 128, 128)
    k_p = results[7]["KO"][0:512].reshape(1, 1, 512, 8, 64)
    v_p = results[7]["VO"][0:512].reshape(1, 1, 512, 8, 64)
    ret_s = np.concatenate([r["RS"] for r in results], axis=0).reshape(1, 32, 4, 128, 128)
    k_s = np.concatenate([r["KO"][512:768] for r in results], axis=0).reshape(1, 32, 64, 8, 64)
    v_s = np.concatenate([r["VO"][512:768] for r in results], axis=0).reshape(1, 32, 64, 8, 64)
    out = (y_p, y_s, ret_p, k_p, v_p, ret_s, k_s, v_s)
    return tuple(np.ascontiguousarray(a, dtype=np.float32) for a in out)


def kernel(**inputs):
    nc = _get_nc()
    in_maps = make_in_maps(**inputs)
    res = run_bass_kernel_spmd(nc, in_maps, core_ids=list(range(NCORES)))
    return assemble(res.results)
```

```python
import os
import numpy as np
from contextlib import ExitStack
import concourse.bass as bass
import concourse.mybir as mybir
from concourse.bass_utils import run_bass_kernel_spmd

F32 = mybir.dt.float32
BF16 = mybir.dt.bfloat16
ALU = mybir.AluOpType
AF = mybir.ActivationFunctionType
AX = mybir.AxisListType

NCORES = 8
D = 1024
SEQ = 16384
TPC = SEQ // NCORES
NPT = TPC // 128
NST = 2
NT = NPT + NST
DFF = 2816
NFC = DFF // 128
EPS = 1e-6
NEG = -30000.0
STAGE = int(os.environ.get("KSTAGE", "9"))
NPRE_T = 7 * 16
NPRE = int(os.environ.get("KNPRE", str(NPRE_T)))
KCUT = int(os.environ.get("KCUT", "0"))
PAST_LEN = 2048
LG = np.log1p(-np.exp2(-5.0 - np.arange(4, dtype=np.float64)))


class Op:
    __slots__ = ("eng", "fn", "deps", "signal", "val", "sem", "dma", "ext", "epoch")

    def __init__(self, eng, fn, dma):
        self.ext = None
        self.epoch = 0
        self.eng = eng
        self.fn = fn
        self.dma = dma
        self.deps = []
        self.signal = dma
        self.val = 0
        self.sem = None


class Buf:
    __slots__ = ("w", "r", "dr")

    def __init__(self):
        self.w = []
        self.r = {}
        self.dr = []


ENGS = ("pe", "act", "dve", "pool", "sp")


class Prog:
    def __init__(self, n_dma_sems):
        self.ops = {e: [] for e in ENGS}
        self.n_dma_sems = n_dma_sems
        self.dma_last = [None] * n_dma_sems
        self.dma_counts = [0, 0]
        self.epoch = 0
        self.waited = {e: {} for e in ENGS}
        self.final_waits = []

    def emit(self, eng, fn, reads=(), writes=(), dma=False, append=False, final=False):
        op = Op(eng, fn, dma)
        if getattr(self, 'muted', False):
            return op
        op.epoch = self.epoch
        deps = []
        for b in reads:
            deps += b.w
        for b in writes:
            if not append:
                deps += b.w
            deps += list(b.r.values()) + b.dr
        if dma:
            half = self.n_dma_sems // 2
            q = 0 if eng == "sp" else 1
            idx = q * half + self.dma_counts[q] % half
            self.dma_counts[q] += 1
            prev = self.dma_last[idx]
            op.sem = idx
            op.val = (prev.val if prev is not None else 0) + 16
            if prev is not None:
                deps.append(prev)
            self.dma_last[idx] = op
        seen = set()
        for d in deps:
            if id(d) in seen:
                continue
            seen.add(id(d))
            if (not d.dma) and (not dma) and d.eng == "pe" and eng == "pe":
                continue
            d.signal = True
            op.deps.append(d)
        for b in reads:
            if dma:
                b.dr.append(op)
            else:
                b.r[eng] = op
        for b in writes:
            if append:
                b.w.append(op)
            else:
                b.w = [op]
                b.r = {}
                b.dr = []
        self.ops[eng].append(op)
        if final:
            self.final_waits.append(op)
        return op

    def finalize(self):
        for e in ENGS:
            c = 0
            for op in self.ops[e]:
                if op.dma or op.ext is not None:
                    continue
                if op.signal:
                    c += 1
                    op.val = c
                    op.sem = e

    def new_epoch(self):
        self.epoch += 1

    def replay(self, eng, handle, sems, dma_sems, extra_final=False, epoch=None):
        waited = self.waited[eng]

        def wait(d):
            if d.ext is not None:
                if waited.get("ext", 0) < d.ext[1]:
                    handle.wait_ge(d.ext[0], d.ext[1])
                    waited["ext"] = d.ext[1]
                return
            key = ("d", d.sem) if d.dma else d.sem
            if waited.get(key, 0) >= d.val:
                return
            s = dma_sems[d.sem] if d.dma else sems[d.sem]
            self.wcount = getattr(self, 'wcount', {})
            self.wcount[eng] = self.wcount.get(eng, 0) + 1
            handle.wait_ge(s, d.val)
            waited[key] = d.val

        for op in self.ops[eng]:
            if epoch is not None and op.epoch != epoch:
                continue
            self.icount = getattr(self, 'icount', {})
            self.icount[eng] = self.icount.get(eng, 0) + 1
            for d in op.deps:
                wait(d)
            ins = op.fn(handle)
            if op.signal and op.ext is None:
                if op.dma:
                    ins.then_inc(dma_sems[op.sem], 16)
                else:
                    ins.then_inc(sems[op.sem], 1)
        if extra_final:
            for d in self.final_waits:
                wait(d)
            for d in self.dma_last:
                if d is not None:
                    wait(d)


def _rope_tables(core):
    half = 64
    inv = (np.float32(10000.0) ** (-np.arange(half, dtype=np.float32) / np.float32(half))).astype(np.float32)
    out = np.zeros((NT, 128, 256), np.float32)
    ks = np.float32(128.0 ** -0.5)
    for ti in range(NT):
        if ti < NPT:
            pos = (core * TPC + ti * 128 + np.arange(128)).astype(np.float32)
        else:
            pos = (PAST_LEN + (np.arange(128) % 64)).astype(np.float32)
        ang = (pos[:, None] * inv[None, :]).astype(np.float32)
        c = np.cos(ang).astype(np.float32)
        s = np.sin(ang).astype(np.float32)
        out[ti, :, 0:64] = c
        out[ti, :, 64:128] = s
        out[ti, :, 128:192] = c * ks
        out[ti, :, 192:256] = s * ks
    return out


def _prepass_tables(core):
    half = 64
    inv = (np.float32(10000.0) ** (-np.arange(half, dtype=np.float32) / np.float32(half))).astype(np.float32)
    ks = np.float32(128.0 ** -0.5)
    n0 = core * TPC
    r = np.arange(NPRE_T * 128)
    pos = (n0 - NPRE_T * 128 + r).astype(np.float32)
    ang = (pos[:, None] * inv[None, :]).astype(np.float32)
    tab = np.concatenate([np.cos(ang).astype(np.float32) * ks, np.sin(ang).astype(np.float32) * ks], axis=1)
    tab = tab.reshape(NPRE_T, 128, 128).astype(np.float32)
    zt = np.zeros((128, 4 * NPRE_T), np.float64)
    p = np.arange(128)
    for i in range(NPRE_T):
        for h in range(4):
            zt[:, 4 * i + h] = np.exp((NPRE_T * 128 - 1.0 - (i * 128 + p)) * LG[h])
    return np.ascontiguousarray(tab), zt.astype(np.float32)


def _decay_tables():
    n = np.arange(128)
    diff = n[None, :] - n[:, None]
    DT = np.zeros((2, 128, 4, 128), np.float64)
    xi = np.zeros((3, 128, 4, 128), np.float64)
    zt = np.zeros((128, 12 + 4 * NPT), np.float64)
    same = (n[None, :] // 64) == (n[:, None] // 64)
    for h in range(4):
        DT[0, :, h, :] = np.where(diff >= 0, np.exp(np.maximum(diff, 0) * LG[h]), 0.0)
        DT[1, :, h, :] = np.where((diff >= 0) & same, np.exp(np.maximum(diff, 0) * LG[h]), 0.0)
        xi[0, :, h, :] = np.exp((n + 1.0) * LG[h])[None, :]
        xs = np.exp(((n % 64) + 1.0) * LG[h])
        xi[1, :, h, :] = np.where(n < 64, xs, 0.0)[None, :]
        xi[2, :, h, :] = np.where(n >= 64, xs, 0.0)[None, :]
        zt[:, h] = np.exp((127.0 - n) * LG[h])
        zt[:, 4 + h] = np.where(n < 64, np.exp((63.0 - (n % 64)) * LG[h]), 0.0)
        zt[:, 8 + h] = np.where(n >= 64, np.exp((63.0 - (n % 64)) * LG[h]), 0.0)
        for i in range(NPT):
            zt[:, 12 + i * 4 + h] = np.exp((TPC - 1.0 - (i * 128 + n)) * LG[h])
    return (DT.reshape(2, 128, 512).astype(np.float32), xi.reshape(3, 128, 512).astype(np.float32),
            zt.astype(np.float32))


def _coef_table(core):
    co = np.zeros((2, 8, 4), np.float64)
    for cp in range(8):
        for h in range(4):
            if cp < core:
                co[0, cp, h] = np.exp(TPC * LG[h] * (core - 1 - cp))
            co[1, cp, h] = np.exp(TPC * LG[h] * (7 - cp))
    return np.broadcast_to(co.reshape(1, 64), (128, 64)).astype(np.float32).copy()


def _bias_tables(rel_bias):
    rb = np.asarray(rel_bias, np.float32).reshape(8, 513)
    l = np.arange(128)[:, None]
    t = np.arange(128)[None, :]
    bp = np.zeros((128, 5, 8, 128), np.float32)
    for j in range(5):
        d = 128 * (4 - j) + t - l
        idx = np.clip(d, -256, 256) + 256
        kc = 2 * j + (l >= 64)
        qc = 8 + (t >= 64)
        valid = (kc >= qc - 8) & (kc <= qc)
        for h in range(8):
            bp[:, j, h, :] = np.where(valid, rb[h][idx], NEG)
    bs = np.zeros((128, 9, 8, 128), np.float32)
    for j in range(4):
        da = np.clip(128 * (4 - j) + t - l, -256, 256) + 256
        db = np.clip(128 * (4 - j) + (t - 64) - l, -256, 256) + 256
        for h in range(8):
            bs[:, j, h, :] = np.where(t < 64, rb[h][da], NEG)
            bs[:, 4 + j, h, :] = np.where(t >= 64, rb[h][db], NEG)
    do = np.clip(t - l, -256, 256) + 256
    vo = ((t < 64) & (l < 64)) | ((t >= 64) & (l >= 64))
    for h in range(8):
        bs[:, 8, h, :] = np.where(vo, rb[h][do], NEG)
    return bp.reshape(128, 5 * 1024), bs.reshape(128, 9 * 1024)


def build_program():
    nc = bass.Bass("TRN2", target_bir_lowering=False)

    def din(name, shape):
        return nc.dram_tensor(name, list(shape), F32, kind="ExternalInput").ap()

    def dout(name, shape):
        return nc.dram_tensor(name, list(shape), F32, kind="ExternalOutput").ap()

    X = din("X", [512 + TPC + 256, D])
    PPd = din("PP", [NT * 128, 256])
    CK = din("CK", [4 * 512, 512])
    CV = din("CV", [4 * 512, 512])
    STd = din("ST", [4, 4, 128, 128])
    Win = din("w_in", [D, 3584])
    Wout = din("w_out", [D, D])
    Wg = din("w_gate", [D, DFF])
    Wu = din("w_up", [D, DFF])
    Wd = din("w_down", [DFF, D])
    Wpg = din("w_pg", [D, D])
    Wpp = din("w_pp", [256, D])
    gT_d = din("gT", [128, 24])
    gret_d = din("gret", [1, 512])
    gqk_d = din("gqk", [1, 128])
    rope_d = din("rope", [NT, 128, 256])
    DT_d = din("DTt", [2, 128, 512])
    xi_d = din("xit", [3, 128, 512])
    zt_d = din("zt", [128, 12 + 4 * NPT])
    XP = din("XP", [NPRE_T * 128, D])
    ropeP_d = din("ropeP", [NPRE_T, 128, 128])
    ztp_d = din("ztp", [128, 4 * NPRE_T])
    bP_d = din("biasP", [128, 5 * 1024])
    bS_d = din("biasS", [128, 9 * 1024])
    hm_d = din("hmask", [128, 2])
    id_d = din("ident", [128, 128])

    Y = dout("Y", [NT * 128, D])
    KO = dout("KO", [6 * 128, 512])
    VO = dout("VO", [6 * 128, 512])
    RS = dout("RS", [4, 4, 128, 128])
    RP = dout("RP", [4, 128, 128])

    HS = nc.dram_tensor("HS", [NT * 128, D], F32)

    NDS = 40
    pg = Prog(NDS)
    E = pg.emit

    with ExitStack() as es:
        WR_t = es.enter_context(nc.sbuf_tensor("WR", [128, 77824], BF16))
        NF = 13200
        FR_t = es.enter_context(nc.sbuf_tensor("FR", [128, NF], F32))
        banks = [es.enter_context(nc.psum_tensor(f"bank{i}", [128, 512], F32)) for i in range(8)]
        bankB = [Buf() for _ in range(8)]
        sems = {e: es.enter_context(nc.semaphore(f"s_{e}")) for e in ("pe", "act", "dve", "pool")}
        dma_sems = [es.enter_context(nc.semaphore(f"d{i}")) for i in range(NDS)]

        class Carve:
            def __init__(self, t, lo, hi):
                self.t, self.off, self.hi = t, lo, hi

            def take(self, n):
                a = self.t[:, self.off:self.off + n]
                self.off += n
                assert self.off <= self.hi, (self.off, self.hi)
                return a

        w_in_sb = WR_t[:, 0:28672].rearrange("p (k n) -> p k n", k=8)
        w_out_sb = WR_t[:, 28672:36864].rearrange("p (k n) -> p k n", k=8)
        ca = Carve(WR_t, 36864, 77824)
        biasP = ca.take(5 * 1024)
        biasS = ca.take(9 * 1024)
        pT = [ca.take(1024) for _ in range(3)]
        NSLOT = 6
        knT = [ca.take(512).rearrange("p (j t) -> p j t", j=4) for _ in range(NSLOT)]
        vaug_flat = [ca.take(528) for _ in range(NSLOT)]
        vaug = [v.rearrange("p (h c) -> p h c", c=66) for v in vaug_flat]
        xb = ca.take(1024)
        uT = ca.take(1024).rearrange("p (k t) -> p k t", k=8)
        xb2 = [xb, ca.take(1024)]
        uT2 = [uT, ca.take(1024).rearrange("p (k t) -> p k t", k=8)]
        qrot = ca.take(512)
        krot = ca.take(512)
        kz = ca.take(512)
        vr = ca.take(512)
        qn = ca.take(512)
        kn = ca.take(512)
        qT = ca.take(512)
        kT = ca.take(512)
        qxT = [ca.take(512), ca.take(512)]
        sTm = ca.take(512)
        Rb = [ca.take(512), ca.take(512)]
        qnT = ca.take(512).rearrange("p (j t) -> p j t", j=4)
        qz = [ca.take(512).rearrange("p (j t) -> p j t", j=4) for _ in range(2)]
        kzb = ca.take(512)
        mixed = ca.take(1024)
        mixedT = ca.take(1024).rearrange("p (k t) -> p k t", k=8)
        kcb = ca.take(512)

        wg_sb = WR_t[:, 0:22528].rearrange("p (k n) -> p k n", k=8)
        wu_sb = WR_t[:, 22528:45056].rearrange("p (k n) -> p k n", k=8)
        wd_sb = WR_t[:, 45056:67584].rearrange("p (f n) -> p f n", f=NFC)
        wpg_sb = WR_t[:, 67584:75776].rearrange("p (k n) -> p k n", k=8)
        wpp_sb = WR_t[:, 75776:77824].rearrange("p (k n) -> p k n", k=2)

        cf = Carve(FR_t, 0, NF)
        xs = [cf.take(1024), cf.take(1024)]
        hbuf = cf.take(1024)
        scrA = cf.take(1024)
        gT = cf.take(24)
        gret = cf.take(512)
        gqk = cf.take(128)
        zt = cf.take(12 + 4 * NPT)
        ztp = cf.take(4 * NPRE_T)
        hmask = cf.take(2)
        ss1 = cf.take(2)
        rstd1 = cf.take(2)
        ss8 = cf.take(16)
        rstd8 = cf.take(16)
        ss4 = cf.take(4)
        rstd4 = cf.take(4)
        rec8 = cf.take(8)
        dmy = cf.take(2)
        epsc = cf.take(2)
        ident = cf.take(64).bitcast(BF16)
        junk = cf.take(512).bitcast(BF16)
        persist_end = cf.off
        ropeS = [cf.take(256), cf.take(256)]
        Gs = cf.take(512)
        scrB = cf.take(512)
        scrC = cf.take(512)
        kvout = cf.take(1024)
        Rf = [cf.take(512), cf.take(512)]
        DTs = [cf.take(512), cf.take(512)]
        xiS = [cf.take(512), cf.take(512), cf.take(512)]
        Sacc = cf.take(512)
        a_end = cf.off
        cb = Carve(FR_t, persist_end, NF)
        pS = [cb.take(256), cb.take(256)]
        gpl = cb.take(1024)
        bB = cb.take(2048 // 2 + 5632 // 2 + 1024 // 2 + 1024 // 2 + 256 // 2)
        bBb = bB.bitcast(BF16)
        o = 0
        uTB = bBb[:, o:o + 2048].rearrange("p (k t) -> p k t", k=8); o += 2048
        actT = bBb[:, o:o + 5632].rearrange("p (f t) -> p f t", f=NFC); o += 5632
        ubB = bBb[:, o:o + 1024]; o += 1024
        u2T = bBb[:, o:o + 1024].rearrange("p (k t) -> p k t", k=8); o += 1024
        ppT = bBb[:, o:o + 256].rearrange("p (k t) -> p k t", k=2); o += 256
        silu_s = cb.take(256)

        class NB:
            pass
        B = NB()
        for name in ("w_in", "w_out", "wg", "wu", "wd", "wpg", "wpp", "biasP", "biasS", "xb", "uT", "qrot", "krot",
                     "kz", "vr", "qn", "kn", "qT", "kT", "sTm", "qnT", "mixed", "mixedT", "ident", "kcb", "hbuf",
                     "scrA", "consts", "ss1", "rstd1", "ss8", "rstd8", "ss4", "rstd4", "rec8", "Gs", "scrB", "scrC",
                     "kvout", "Sacc", "gpl", "uTB", "actT", "ubB", "u2T", "ppT", "silu_s", "HS", "cc_in", "cc_out",
                     "vones", "dummy", "junk", "cc2", "qz", "kzb"):
            setattr(B, name, Buf())
        B.xs = [Buf(), Buf()]
        B.xb2 = [B.xb, Buf()]
        B.uT2 = [B.uT, Buf()]
        B.pT = [Buf() for _ in range(3)]
        B.knT = [Buf() for _ in range(NSLOT)]
        B.vaug = [Buf() for _ in range(NSLOT)]
        B.qxT = [Buf(), Buf()]
        B.Rb = [Buf(), Buf()]
        B.Rf = [Buf(), Buf()]
        B.ropeS = [Buf(), Buf()]
        B.pS = [Buf(), Buf()]
        B.HSt = [Buf() for _ in range(NT)]

        def bview(i):
            return banks[i][:, 0:512]

        def bview16(i):
            return banks[i][:, 0:512].bitcast(BF16)

        def load_const(dst, src, buf, eng="sp"):
            E(eng, lambda e, dst=dst, src=src: e.dma_start(out=dst, in_=src), writes=[buf], dma=True, append=True)

        load_const(gT, gT_d, B.consts)
        load_const(gret, gret_d.partition_broadcast(128), B.consts)
        load_const(gqk, gqk_d.partition_broadcast(128), B.consts)
        load_const(zt, zt_d, B.consts)
        load_const(ztp, ztp_d, B.consts)
        load_const(hmask, hm_d, B.consts)
        for i in range(2):
            load_const(DTs[i], DT_d[i], B.consts)
        for i in range(3):
            load_const(xiS[i], xi_d[i], B.consts)
        load_const(ident, id_d, B.ident, eng="pool")
        Win_v = Win.rearrange("(k p) n -> p k n", p=128)
        for k in range(8):
            load_const(w_in_sb[:, k, :], Win_v[:, k, :], B.w_in, eng="pool")
        Wout_v = Wout.rearrange("(k p) n -> p k n", p=128)
        for k in range(8):
            load_const(w_out_sb[:, k, :], Wout_v[:, k, :], B.w_out, eng="pool")
        for j in range(5):
            load_const(biasP[:, j * 1024:(j + 1) * 1024], bP_d[:, j * 1024:(j + 1) * 1024], B.biasP, eng="pool")
        for j in range(9):
            load_const(biasS[:, j * 1024:(j + 1) * 1024], bS_d[:, j * 1024:(j + 1) * 1024], B.biasS, eng="pool")
        for s in range(NSLOT):
            E("pool", lambda e, s=s: e.memset(vaug_flat[s], 1.0), writes=[B.vaug[s]])

        E("pool", lambda e: e.memset(epsc, EPS), writes=[B.consts], append=True)
        for _q in range(2):
            E("pool", lambda e, _q=_q: e.memset(qz[_q], 0.0), writes=[B.qz], append=(_q == 1))
        gq_b = gqk[:, 0:64].unsqueeze(1).to_broadcast([128, 8, 64])
        gk_b = gqk[:, 64:128].unsqueeze(1).to_broadcast([128, 8, 64])

        def front(row0, slot, gcol, src_dram, xbuf_t, xbuf_B, xb_t=None, xb_B=None, uT_t=None, uT_B=None, tb=2,
                  ntok=128, tok_off=0):
            xb_t = xb if xb_t is None else xb_t
            xb_B = B.xb if xb_B is None else xb_B
            uT_t = uT if uT_t is None else uT_t
            uT_B = B.uT if uT_B is None else uT_B
            if src_dram is not None:
                E("sp", lambda e: e.dma_start(out=xbuf_t, in_=src_dram[row0:row0 + 128, :]), writes=[xbuf_B], dma=True)
            E("act", lambda e: e.activation(out=junk, in_=xbuf_t, func=AF.Square, accum_out=ss1[:, 0:1]),
              reads=[xbuf_B], writes=[B.junk, B.ss1])
            E("act", lambda e: e.activation(out=rstd1[:, 0:1], in_=ss1[:, 0:1], func=AF.Sqrt, bias=epsc[:, 0:1], scale=1.0 / D), reads=[B.ss1, B.consts], writes=[B.rstd1])
            E("dve", lambda e: e.reciprocal(out=rstd1[:, 0:1], in_=rstd1[:, 0:1]), reads=[B.rstd1], writes=[B.rstd1])
            E("dve", lambda e: e.tensor_scalar(out=xb_t, in0=xbuf_t, scalar1=rstd1[:, 0:1], scalar2=None,
                                               op0=ALU.mult), reads=[xbuf_B, B.rstd1], writes=[xb_B])
            tv = bview16(tb)
            for k in range(8):
                E("pe", lambda e, k=k: e.transpose(out=tv[:, k * 128:(k + 1) * 128], in_=xb_t[:, k * 128:(k + 1) * 128],
                                                   identity=ident), reads=[xb_B, B.ident], writes=[bankB[tb]])
            E("dve", lambda e: e.tensor_tensor(out=uT_t[:, :, tok_off:tok_off + 128],
                                               in0=tv.rearrange("p (k t) -> p k t", k=8),
                                               in1=gT[:, gcol:gcol + 8].unsqueeze(2).to_broadcast([128, 8, 128]),
                                               op=ALU.mult),
              reads=[bankB[tb], B.consts], writes=[uT_B])

        cur_par = [0]

        def zmm(bank_i, col):
            bv = bview(bank_i)
            uTp, uTB_ = uT2[cur_par[0]], B.uT2[cur_par[0]]
            for k in range(8):
                E("pe", lambda e, k=k: e.matmul(out=bv, lhsT=uTp[:, k, :], rhs=w_in_sb[:, k, col * 512:(col + 1) * 512],
                                                start=(k == 0), stop=(k == 7)),
                  reads=[uTB_, B.w_in], writes=[bankB[bank_i]])

        def rope(bank_i, rS, rB, coff, out_t, out_B):
            z = bview(bank_i).rearrange("p (h d) -> p h d", h=4)
            x1 = z[:, :, 0:64]
            x2 = z[:, :, 64:128]
            cosb = rS[:, coff:coff + 64].unsqueeze(1).to_broadcast([128, 4, 64])
            sinb = rS[:, coff + 64:coff + 128].unsqueeze(1).to_broadcast([128, 4, 64])
            t1 = scrB[:, 0:256].rearrange("p (h d) -> p h d", h=4)
            t2 = scrB[:, 256:512].rearrange("p (h d) -> p h d", h=4)
            t3 = scrC[:, 0:256].rearrange("p (h d) -> p h d", h=4)
            t4 = scrC[:, 256:512].rearrange("p (h d) -> p h d", h=4)
            ov = out_t.rearrange("p (h d) -> p h d", h=4)
            rd = [bankB[bank_i], rB]
            E("dve", lambda e: e.tensor_tensor(out=t1, in0=x1, in1=cosb, op=ALU.mult), reads=rd, writes=[B.scrB])
            E("dve", lambda e: e.tensor_tensor(out=t2, in0=x2, in1=sinb, op=ALU.mult), reads=rd, writes=[B.scrB],
              append=True)
            E("dve", lambda e: e.tensor_tensor(out=t3, in0=x1, in1=sinb, op=ALU.mult), reads=rd, writes=[B.scrC])
            E("dve", lambda e: e.tensor_tensor(out=t4, in0=x2, in1=cosb, op=ALU.mult), reads=rd, writes=[B.scrC],
              append=True)
            E("dve", lambda e: e.tensor_tensor(out=ov[:, :, 0:64], in0=t1, in1=t2, op=ALU.subtract),
              reads=[B.scrB], writes=[out_B])
            E("dve", lambda e: e.tensor_tensor(out=ov[:, :, 64:128], in0=t3, in1=t4, op=ALU.add),
              reads=[B.scrC], writes=[out_B], append=True)

        def load_rope(ti, slot):
            E("pool", lambda e: e.dma_start(out=ropeS[slot], in_=rope_d[ti]), writes=[B.ropeS[slot]], dma=True)

        SB = 7
        if KCUT == 1:
            pg.muted = True
        def pre_front(i):
            slot = i % 2
            E("pool", lambda e, i=i, slot=slot: e.dma_start(out=ropeS[slot][:, 0:128], in_=ropeP_d[i]),
              writes=[B.ropeS[slot]], dma=True)
            front(i * 128, slot, 0, XP, xs[slot], B.xs[slot], xb_t=xb2[slot], xb_B=B.xb2[slot], uT_t=uT2[slot],
                  uT_B=B.uT2[slot], tb=2 + slot)

        def pre_smm(i):
            for h in range(4):
                E("pe", lambda e, h=h, i=i: e.matmul(out=bview(SB)[:, h * 128:(h + 1) * 128],
                                                     lhsT=kz[:, h * 128:(h + 1) * 128],
                                                     rhs=vr[:, h * 128:(h + 1) * 128],
                                                     start=(i == 0 and h == 0), stop=(i == NPRE - 1),
                                                     skip_group_check=True),
                  reads=[B.kz, B.vr], writes=[bankB[SB]])

        pre_front(0)
        for i in range(NPRE):
            slot = i % 2
            if i + 1 < NPRE:
                pre_front(i + 1)
            cur_par[0] = slot
            zmm(0, 1)
            zmm(1, 2)
            if i > 0:
                pre_smm(i - 1)
            rope(0, ropeS[slot], B.ropeS[slot], 0, krot, B.krot)
            E("dve", lambda e, i=i: e.tensor_tensor(
                out=kz.rearrange("p (h d) -> p h d", h=4), in0=krot.rearrange("p (h d) -> p h d", h=4),
                in1=ztp[:, 4 * i:4 * i + 4].unsqueeze(2).to_broadcast([128, 4, 128]), op=ALU.mult),
              reads=[B.krot, B.consts], writes=[B.kz])
            E("dve", lambda e: e.tensor_copy(out=vr, in_=bview(1)), reads=[bankB[1]], writes=[B.vr])
        pre_smm(NPRE - 1)
        cur_par[0] = 0
        if KCUT == 3:
            pg.muted = True
        E("dve", lambda e: e.tensor_copy(out=Sacc, in_=bview(SB)), reads=[bankB[SB]], writes=[B.Sacc])

        def load_sample_state(sq, which):
            src = STd[sq].rearrange("h p v -> p h v")
            E("sp", lambda e: e.dma_start(out=Rf[which].rearrange("p (h v) -> p h v", h=4), in_=src),
              writes=[B.Rf[which]], dma=True)
            E("act", lambda e: e.activation(out=Rb[which], in_=Rf[which], func=AF.Copy),
              reads=[B.Rf[which]], writes=[B.Rb[which]])

        def kv_project(own_slot, want_out, out_row):
            zmm(0, 5)
            zmm(1, 6)
            kz_ = bview(0)
            E("act", lambda e: e.activation(out=scrA[:, 0:512], in_=kz_, func=AF.Square),
              reads=[bankB[0]], writes=[B.scrA])
            E("dve", lambda e: e.tensor_reduce(out=ss8[:, 8:16], in_=scrA[:, 0:512].rearrange("p (h d) -> p h d", h=8),
                                               axis=AX.X, op=ALU.add), reads=[B.scrA], writes=[B.ss8])
            E("act", lambda e: e.activation(out=rstd8[:, 8:16], in_=ss8[:, 8:16], func=AF.Sqrt, bias=epsc[:, 0:1], scale=1.0 / 64), reads=[B.ss8, B.consts], writes=[B.rstd8])
            E("dve", lambda e: e.reciprocal(out=rstd8[:, 8:16], in_=rstd8[:, 8:16]), reads=[B.rstd8], writes=[B.rstd8])
            E("dve", lambda e: e.tensor_tensor(out=scrA[:, 512:1024].rearrange("p (h d) -> p h d", h=8),
                                               in0=kz_.rearrange("p (h d) -> p h d", h=8),
                                               in1=rstd8[:, 8:16].unsqueeze(2).to_broadcast([128, 8, 64]),
                                               op=ALU.mult), reads=[bankB[0], B.rstd8], writes=[B.scrA])
            E("dve", lambda e: e.tensor_tensor(out=kvout[:, 0:512].rearrange("p (h d) -> p h d", h=8),
                                               in0=scrA[:, 512:1024].rearrange("p (h d) -> p h d", h=8),
                                               in1=gk_b, op=ALU.mult), reads=[B.scrA, B.consts], writes=[B.kvout])
            E("act", lambda e: e.activation(out=kn, in_=kvout[:, 0:512], func=AF.Copy),
              reads=[B.kvout], writes=[B.kn])
            E("act", lambda e: e.activation(out=kvout[:, 512:1024], in_=bview(1), func=AF.Copy),
              reads=[bankB[1]], writes=[B.kvout], append=True)
            E("act", lambda e: e.activation(out=vaug[own_slot][:, :, 0:64],
                                            in_=bview(1).rearrange("p (h d) -> p h d", h=8), func=AF.Copy),
              reads=[bankB[1]], writes=[B.vaug[own_slot]])
            tv = bview16(2)
            for j in range(4):
                E("pe", lambda e, j=j: e.transpose(out=tv[:, j * 128:(j + 1) * 128], in_=kn[:, j * 128:(j + 1) * 128],
                                                   identity=ident), reads=[B.kn, B.ident], writes=[bankB[2]])
            E("act", lambda e: e.activation(out=knT[own_slot], in_=tv[:, 0:512].rearrange("p (j t) -> p j t", j=4),
                                            func=AF.Copy), reads=[bankB[2]], writes=[B.knT[own_slot]])
            if want_out:
                E("pool", lambda e: e.dma_start(out=KO[out_row:out_row + 128, :], in_=kvout[:, 0:512]),
                  reads=[B.kvout], dma=True, final=True)
                E("pool", lambda e: e.dma_start(out=VO[out_row:out_row + 128, :], in_=kvout[:, 512:1024]),
                  reads=[B.kvout], dma=True, final=True)

        def load_cache_tile(sq, j, slot):
            r0 = sq * 512 + j * 128
            E("pool", lambda e: e.dma_start(out=kcb, in_=CK[r0:r0 + 128, :]), writes=[B.kcb], dma=True)
            E("pool", lambda e: e.dma_start(out=vaug[slot][:, :, 0:64],
                                            in_=CV[r0:r0 + 128, :].rearrange("p (h d) -> p h d", h=8)),
              writes=[B.vaug[slot]], dma=True)
            tv = bview16(2)
            for jj in range(4):
                E("pe", lambda e, jj=jj: e.transpose(out=tv[:, jj * 128:(jj + 1) * 128],
                                                     in_=kcb[:, jj * 128:(jj + 1) * 128], identity=ident),
                  reads=[B.kcb, B.ident], writes=[bankB[2]])
            E("act", lambda e: e.activation(out=knT[slot], in_=tv[:, 0:512].rearrange("p (j t) -> p j t", j=4),
                                            func=AF.Copy), reads=[bankB[2]], writes=[B.knT[slot]])

        pT_ctr = [0]

        def attention(keys):
            oA, oB = 4, 5
            nk = len(keys)
            def scores(jn):
                slot, bias_ap, bias_B, mcol, loader = keys[jn]
                if loader is not None:
                    loader()
                pair = (6, 7) if jn % 2 == 0 else (0, 1)
                for h in range(8):
                    bk = pair[h // 4]
                    E("pe", lambda e, h=h, bk=bk, slot=slot: e.matmul(
                        out=bview(bk)[:, (h % 4) * 128:(h % 4 + 1) * 128],
                        lhsT=knT[slot][:, h // 2, :], rhs=qz[h % 2][:, h // 2, :],
                        start=True, stop=True), reads=[B.knT[slot], B.qz], writes=[bankB[bk]])

            scores(0)
            for jn, (slot, bias_ap, bias_B, mcol, loader) in enumerate(keys):
                pair = (6, 7) if jn % 2 == 0 else (0, 1)
                ps = pT_ctr[0] % 3
                pT_ctr[0] += 1
                for half in range(2):
                    bk = pair[half]
                    E("dve", lambda e, bk=bk, half=half, bias_ap=bias_ap: e.scalar_tensor_tensor(
                        out=bview(bk), in0=bview(bk), scalar=0.125, in1=bias_ap[:, half * 512:(half + 1) * 512],
                        op0=ALU.mult, op1=ALU.add), reads=[bankB[bk], bias_B], writes=[bankB[bk]])
                    E("act", lambda e, bk=bk, half=half, ps=ps, mcol=mcol: e.activation(
                        out=pT[ps][:, half * 512:(half + 1) * 512], in_=bview(bk), func=AF.Exp, bias=mcol, scale=1.0),
                      reads=[bankB[bk], B.consts], writes=[B.pT[ps]], append=(half == 1))
                if jn + 1 < nk:
                    scores(jn + 1)
                for h in range(8):
                    ob = oA if h < 4 else oB
                    E("pe", lambda e, h=h, ob=ob, ps=ps, slot=slot, jn=jn: e.matmul(
                        out=bview(ob)[:, (h % 4) * 65:(h % 4) * 65 + 65], lhsT=pT[ps][:, h * 128:(h + 1) * 128],
                        rhs=vaug[slot][:, h, 0:65], start=(jn == 0 and h % 4 == 0), stop=(jn == nk - 1),
                        skip_group_check=True), reads=[B.pT[ps], B.vaug[slot]], writes=[bankB[ob]])
            cut(184)
            for half, ob in enumerate((oA, oB)):
                ov = bview(ob)[:, 0:260].rearrange("p (h c) -> p h c", c=65)
                E("dve", lambda e, ov=ov, half=half: e.reciprocal(out=rec8[:, half * 4:half * 4 + 4], in_=ov[:, :, 64]),
                  reads=[bankB[ob]], writes=[B.rec8], append=(half == 1))
            for half, ob in enumerate((oA, oB)):
                ov = bview(ob)[:, 0:260].rearrange("p (h c) -> p h c", c=65)
                E("dve", lambda e, ov=ov, half=half: e.tensor_tensor(
                    out=mixed[:, 512 + half * 256:512 + half * 256 + 256].rearrange("p (h d) -> p h d", h=4),
                    in0=ov[:, :, 0:64],
                    in1=rec8[:, half * 4:half * 4 + 4].unsqueeze(2).to_broadcast([128, 4, 64]), op=ALU.mult),
                  reads=[bankB[ob], B.rec8], writes=[B.mixed], append=True)

        tile_calls = [0]

        def cut(n):
            if KCUT == n and tile_calls[0] == int(os.environ.get('KTILE', '1')):
                pg.muted = True

        def tileA(ti, row0, slot, own_slot, keys, sample, want_out, out_row, seqs=None, prefetch=None):
            tile_calls[0] += 1
            cur_par[0] = slot
            rS, rB = ropeS[slot], B.ropeS[slot]
            cut(10)
            zmm(0, 0)
            rope(0, rS, rB, 0, qrot, B.qrot)
            zmm(1, 1)
            rope(1, rS, rB, 128, krot, B.krot)
            zc = 4 if sample else 0
            E("dve", lambda e: e.tensor_tensor(
                out=kz.rearrange("p (h d) -> p h d", h=4), in0=krot.rearrange("p (h d) -> p h d", h=4),
                in1=zt[:, zc:zc + 4].unsqueeze(2).to_broadcast([128, 4, 128]), op=ALU.mult),
              reads=[B.krot, B.consts], writes=[B.kz])
            if sample:
                E("dve", lambda e: e.tensor_tensor(
                    out=kzb.rearrange("p (h d) -> p h d", h=4), in0=krot.rearrange("p (h d) -> p h d", h=4),
                    in1=zt[:, 8:12].unsqueeze(2).to_broadcast([128, 4, 128]), op=ALU.mult),
                  reads=[B.krot, B.consts], writes=[B.kzb])
            zmm(0, 2)
            E("act", lambda e: e.activation(out=vr, in_=bview(0), func=AF.Copy), reads=[bankB[0]], writes=[B.vr])
            zmm(1, 3)
            if os.environ.get('KV') == '1':
                E("act", lambda e: e.activation(out=Gs, in_=bview(0), func=AF.Exp, scale=-1.0), reads=[bankB[0]],
                  writes=[B.Gs])
            elif os.environ.get('KV') == '3':
                E("act", lambda e: e.activation(out=Gs, in_=hbuf[:, 0:512], func=AF.Copy), reads=[B.hbuf],
                  writes=[B.Gs])
            elif os.environ.get('KV') == '2':
                E("dve", lambda e: e.tensor_copy(out=Gs, in_=bview(1)), reads=[bankB[1]], writes=[B.Gs])
            else:
                E("act", lambda e: e.activation(out=Gs, in_=bview(1), func=AF.Exp, scale=-1.0), reads=[bankB[1]],
                  writes=[B.Gs])
            E("dve", lambda e: e.tensor_scalar(out=Gs, in0=Gs, scalar1=1.0, scalar2=None, op0=ALU.add),
              reads=[B.Gs], writes=[B.Gs])
            E("dve", lambda e: e.reciprocal(out=Gs, in_=Gs), reads=[B.Gs], writes=[B.Gs])
            E("dve", lambda e: e.tensor_tensor(out=Gs, in0=bview(1), in1=Gs, op=ALU.mult),
              reads=[bankB[1], B.Gs], writes=[B.Gs])
            E("dve", lambda e: e.tensor_tensor(out=Gs, in0=Gs, in1=gret, op=ALU.mult),
              reads=[B.Gs, B.consts], writes=[B.Gs])
            cut(11)
            tv = bview16(2)
            for h in range(4):
                E("pe", lambda e, h=h: e.transpose(out=tv[:, h * 128:(h + 1) * 128], in_=qrot[:, h * 128:(h + 1) * 128],
                                                   identity=ident), reads=[B.qrot, B.ident], writes=[bankB[2]])
            for h in range(4):
                E("pe", lambda e, h=h: e.transpose(out=tv[:, 512 + h * 128:512 + (h + 1) * 128],
                                                   in_=krot[:, h * 128:(h + 1) * 128], identity=ident),
                  reads=[B.krot, B.ident], writes=[bankB[2]])
            E("act", lambda e: e.activation(out=qT, in_=tv[:, 0:512], func=AF.Copy), reads=[bankB[2]], writes=[B.qT])
            E("act", lambda e: e.activation(out=kT, in_=tv[:, 512:1024], func=AF.Copy), reads=[bankB[2]],
              writes=[B.kT])
            cut(12)
            zmm(0, 4)
            E("act", lambda e: e.activation(out=scrA[:, 0:512], in_=bview(0), func=AF.Square),
              reads=[bankB[0]], writes=[B.scrA])
            E("dve", lambda e: e.tensor_reduce(out=ss8[:, 0:8], in_=scrA[:, 0:512].rearrange("p (h d) -> p h d", h=8),
                                               axis=AX.X, op=ALU.add), reads=[B.scrA], writes=[B.ss8])
            E("act", lambda e: e.activation(out=rstd8[:, 0:8], in_=ss8[:, 0:8], func=AF.Sqrt, bias=epsc[:, 0:1], scale=1.0 / 64), reads=[B.ss8, B.consts], writes=[B.rstd8])
            E("dve", lambda e: e.reciprocal(out=rstd8[:, 0:8], in_=rstd8[:, 0:8]), reads=[B.rstd8], writes=[B.rstd8])
            E("dve", lambda e: e.tensor_tensor(out=scrA[:, 512:1024].rearrange("p (h d) -> p h d", h=8),
                                               in0=bview(0).rearrange("p (h d) -> p h d", h=8),
                                               in1=rstd8[:, 0:8].unsqueeze(2).to_broadcast([128, 8, 64]),
                                               op=ALU.mult), reads=[bankB[0], B.rstd8], writes=[B.scrA])
            E("dve", lambda e: e.tensor_tensor(out=qn.rearrange("p (h d) -> p h d", h=8),
                                               in0=scrA[:, 512:1024].rearrange("p (h d) -> p h d", h=8),
                                               in1=gq_b, op=ALU.mult), reads=[B.scrA, B.consts], writes=[B.qn])
            for j in range(4):
                E("pe", lambda e, j=j: e.transpose(out=tv[:, j * 128:(j + 1) * 128], in_=qn[:, j * 128:(j + 1) * 128],
                                                   identity=ident), reads=[B.qn, B.ident], writes=[bankB[2]])
            tvq = tv[:, 0:512].rearrange("p (j t) -> p j t", j=4)
            E("act", lambda e: e.activation(out=qz[0][0:64], in_=tvq[0:64], func=AF.Copy), reads=[bankB[2]],
              writes=[B.qz])
            E("dve", lambda e: e.tensor_copy(out=qz[1][64:128], in_=tvq[64:128]), reads=[bankB[2]],
              writes=[B.qz], append=True)
            cut(13)
            kv_project(own_slot, want_out, out_row)
            cut(14)
            dti = 1 if sample else 0
            for h in range(4):
                E("pe", lambda e, h=h: e.matmul(out=bview(3)[:, h * 128:(h + 1) * 128],
                                                lhsT=kT[:, h * 128:(h + 1) * 128], rhs=qT[:, h * 128:(h + 1) * 128],
                                                start=True, stop=True), reads=[B.kT, B.qT], writes=[bankB[3]])
            E("dve", lambda e: e.tensor_tensor(out=sTm, in0=bview(3), in1=DTs[dti], op=ALU.mult),
              reads=[bankB[3], B.consts], writes=[B.sTm])
            nst = 2 if sample else 1
            for s in range(nst):
                xidx = 1 + s if sample else 0
                E("dve", lambda e, s=s, xidx=xidx: e.tensor_tensor(out=qxT[s], in0=qT, in1=xiS[xidx], op=ALU.mult),
                  reads=[B.qT, B.consts], writes=[B.qxT[s]])
            for h in range(4):
                hs = slice(h * 128, (h + 1) * 128)
                E("pe", lambda e, hs=hs: e.matmul(out=bview(4)[:, hs], lhsT=sTm[:, hs], rhs=vr[:, hs],
                                                  start=True, stop=False), reads=[B.sTm, B.vr], writes=[bankB[4]])
                for s in range(nst):
                    E("pe", lambda e, hs=hs, s=s: e.matmul(out=bview(4)[:, hs], lhsT=qxT[s][:, hs], rhs=Rb[s][:, hs],
                                                           start=False, stop=(s == nst - 1)),
                      reads=[B.qxT[s], B.Rb[s]], writes=[bankB[4]])
            cut(15)
            T = 64 if sample else 128
            for s in range(nst):
                kzs, kzB = (kzb, B.kzb) if (sample and s == 1) else (kz, B.kz)
                for h in range(4):
                    hs = slice(h * 128, (h + 1) * 128)
                    E("pe", lambda e, hs=hs, kzs=kzs: e.matmul(out=bview(5)[:, hs], lhsT=kzs[:, hs], rhs=vr[:, hs],
                                                               start=True, stop=True),
                      reads=[kzB, B.vr], writes=[bankB[5]])
                for h in range(4):
                    hs = slice(h * 128, (h + 1) * 128)
                    gam = float(np.exp(T * LG[h]))
                    E("dve", lambda e, hs=hs, gam=gam, s=s: e.scalar_tensor_tensor(
                        out=Rf[s][:, hs], in0=Rf[s][:, hs], scalar=gam, in1=bview(5)[:, hs], op0=ALU.mult,
                        op1=ALU.add), reads=[B.Rf[s], bankB[5]], writes=[B.Rf[s]], append=(h > 0))
                if sample:
                    E("pool", lambda e, s=s: e.dma_start(out=RS[seqs[s]].rearrange("h p v -> p h v"),
                                                         in_=Rf[s].rearrange("p (h v) -> p h v", h=4)),
                      reads=[B.Rf[s]], dma=True, final=True)
                else:
                    E("act", lambda e, s=s: e.activation(out=Rb[s], in_=Rf[s], func=AF.Copy),
                      reads=[B.Rf[s]], writes=[B.Rb[s]])
            cut(16)
            for h in range(4):
                hs = slice(h * 128, (h + 1) * 128)
                E("act", lambda e, hs=hs, h=h: e.activation(out=scrA[:, hs], in_=bview(4)[:, hs], func=AF.Square,
                                                            accum_out=ss4[:, h:h + 1]),
                  reads=[bankB[4]], writes=[B.scrA, B.ss4], append=(h > 0))
            E("act", lambda e: e.activation(out=rstd4, in_=ss4, func=AF.Sqrt, bias=epsc[:, 0:1], scale=1.0 / 128), reads=[B.ss4, B.consts], writes=[B.rstd4])
            E("dve", lambda e: e.reciprocal(out=rstd4, in_=rstd4), reads=[B.rstd4], writes=[B.rstd4])
            for h in range(4):
                hs = slice(h * 128, (h + 1) * 128)
                E("dve", lambda e, hs=hs, h=h: e.scalar_tensor_tensor(
                    out=mixed[:, hs], in0=bview(4)[:, hs], scalar=rstd4[:, h:h + 1], in1=Gs[:, hs], op0=ALU.mult,
                    op1=ALU.mult), reads=[bankB[4], B.rstd4, B.Gs], writes=[B.mixed], append=True)
            cut(17)
            if prefetch is not None:
                prefetch()
            attention(keys)
            cut(18)
            for k in range(8):
                E("pe", lambda e, k=k: e.transpose(out=tv[:, k * 128:(k + 1) * 128], in_=mixed[:, k * 128:(k + 1) * 128],
                                                   identity=ident), reads=[B.mixed, B.ident], writes=[bankB[2]])
            E("act", lambda e: e.activation(out=mixedT, in_=tv.rearrange("p (k t) -> p k t", k=8), func=AF.Copy),
              reads=[bankB[2]], writes=[B.mixedT])
            for nb in range(2):
                for k in range(8):
                    E("pe", lambda e, k=k, nb=nb: e.matmul(out=bview(nb), lhsT=mixedT[:, k, :],
                                                           rhs=w_out_sb[:, k, nb * 512:(nb + 1) * 512],
                                                           start=(k == 0), stop=(k == 7)),
                      reads=[B.mixedT, B.w_out], writes=[bankB[nb]])
                E("dve", lambda e, nb=nb: e.tensor_tensor(out=hbuf[:, nb * 512:(nb + 1) * 512], in0=bview(nb),
                                                          in1=xs[slot][:, nb * 512:(nb + 1) * 512], op=ALU.add),
                  reads=[bankB[nb], B.xs[slot]], writes=[B.hbuf], append=(nb == 1))
            E("pool", lambda e: e.dma_start(out=HS[ti * 128:(ti + 1) * 128, :], in_=hbuf), reads=[B.hbuf],
              writes=[B.HSt[ti]], dma=True)

        zero_col = hmask[:, 1:2]
        halo_col = hmask[:, 0:1]

        jobs = [("sample", st) for st in range(NST)] + [("halo", g) for g in range(4)] + \
               [("prompt", i) for i in range(NPT)]

        def job_front(jx):
            kind, idx = jobs[jx]
            par = jx % 2
            if kind == "sample":
                load_rope(NPT + idx, par)
                load_sample_state(2 * idx, 0)
                load_sample_state(2 * idx + 1, 1)
                row0 = 512 + TPC + idx * 128
            elif kind == "halo":
                row0 = idx * 128
            else:
                load_rope(idx, par)
                row0 = 512 + idx * 128
            front(row0, par, 0, X, xs[par], B.xs[par], xb_t=xb2[par], xb_B=B.xb2[par], uT_t=uT2[par],
                  uT_B=B.uT2[par], tb=3)

        def job_body(jx):
            kind, idx = jobs[jx]
            par = jx % 2
            pf = (lambda: job_front(jx + 1)) if jx + 1 < len(jobs) else None
            if kind == "sample":
                st = idx
                seqs = (2 * st, 2 * st + 1)
                own = 5
                keys = []
                for s_ in range(2):
                    for j in range(4):
                        sl = (s_ * 4 + j) % 5
                        keys.append((sl, biasS[:, (s_ * 4 + j) * 1024:(s_ * 4 + j + 1) * 1024], B.biasS, zero_col,
                                     (lambda sq=seqs[s_], j=j, sl=sl: load_cache_tile(sq, j, sl))))
                keys.append((own, biasS[:, 8 * 1024:9 * 1024], B.biasS, zero_col, None))
                tileA(NPT + st, 0, par, own, keys, True, True, (4 + st) * 128, seqs=seqs, prefetch=pf)
                if st == NST - 1:
                    E("dve", lambda e: e.tensor_copy(out=Rf[0], in_=Sacc), reads=[B.Sacc], writes=[B.Rf[0]])
                    E("act", lambda e: e.activation(out=Rb[0], in_=Sacc, func=AF.Copy), reads=[B.Sacc],
                      writes=[B.Rb[0]])
            elif kind == "halo":
                cur_par[0] = par
                if pf is not None:
                    pf()
                cur_par[0] = par
                kv_project(idx % NSLOT, False, 0)
            else:
                i = idx
                gq = 4 + i
                keys = []
                for j in range(5):
                    g = gq - 4 + j
                    keys.append((g % NSLOT, biasP[:, j * 1024:(j + 1) * 1024], B.biasP,
                                 halo_col if g < 4 else zero_col, None))
                want = i >= NPT - 4
                tileA(i, 0, par, gq % NSLOT, keys, False, want, (i - (NPT - 4)) * 128 if want else 0, prefetch=pf)

        job_front(0)
        for jx in range(len(jobs)):
            job_body(jx)

        if STAGE >= 5:
            E("pool", lambda e: e.dma_start(out=RP.rearrange("h p v -> p h v"),
                                            in_=Rf[0].rearrange("p (h v) -> p h v", h=4)),
              reads=[B.Rf[0]], dma=True, final=True)
        def wload(dst, src, buf):
            E("pool", lambda e: e.dma_start(out=dst, in_=src), writes=[buf], dma=True, append=True)

        allA = [B.w_in, B.w_out, B.biasP, B.biasS, B.xb, B.uT, B.qrot, B.krot, B.kz, B.vr, B.qn, B.kn, B.qT, B.kT,
                B.sTm, B.qnT, B.qz, B.kzb, B.xb2[1], B.uT2[1], B.mixed, B.mixedT, B.kcb] + B.pT + B.knT + B.vaug + B.qxT + B.Rb
        allAf = [B.Gs, B.scrB, B.scrC, B.kvout, B.Sacc] + B.Rf + B.ropeS
        allBp = [B.gpl, B.uTB, B.actT, B.ubB, B.u2T, B.ppT, B.silu_s] + B.pS
        first = [True]

        def wload_all(dst, src, buf):
            extra = (allA + allAf) if first[0] else []
            first[0] = False
            E("pool", lambda e: e.dma_start(out=dst, in_=src), writes=[buf] + extra, dma=True, append=not extra)

        if STAGE < 6:
            pg.muted = True
        E("pool", lambda e: e.memset(dmy[:, 1:2], 0.0), writes=allA + allAf + allBp + [B.dummy, B.consts])
        barrier_deps = [B.dummy]
        Wg_v = Wg.rearrange("(k p) n -> p k n", p=128)
        Wu_v = Wu.rearrange("(k p) n -> p k n", p=128)
        Wd_v = Wd.rearrange("(f p) n -> p f n", p=128)
        Wpg_v = Wpg.rearrange("(k p) n -> p k n", p=128)
        Wpp_v = Wpp.rearrange("(k p) n -> p k n", p=128)
        for k in range(8):
            E("pool", lambda e, k=k: e.dma_start(out=wg_sb[:, k, :], in_=Wg_v[:, k, :]), reads=barrier_deps,
              writes=[B.wg], dma=True, append=True)
            E("pool", lambda e, k=k: e.dma_start(out=wu_sb[:, k, :], in_=Wu_v[:, k, :]), reads=barrier_deps,
              writes=[B.wu], dma=True, append=True)
        for f in range(NFC):
            E("pool", lambda e, f=f: e.dma_start(out=wd_sb[:, f, :], in_=Wd_v[:, f, :]), reads=barrier_deps,
              writes=[B.wd], dma=True, append=True)
        for k in range(8):
            E("pool", lambda e, k=k: e.dma_start(out=wpg_sb[:, k, :], in_=Wpg_v[:, k, :]), reads=barrier_deps,
              writes=[B.wpg], dma=True, append=True)
        for k in range(2):
            E("pool", lambda e, k=k: e.dma_start(out=wpp_sb[:, k, :], in_=Wpp_v[:, k, :]), reads=barrier_deps,
              writes=[B.wpp], dma=True, append=True)

        NBLK = NT // 2
        for blk in range((NBLK if STAGE >= 7 else 1) if STAGE >= 6 else 0):
            tiles = (2 * blk, 2 * blk + 1)
            for q, ti in enumerate(tiles):
                E("sp", lambda e, q=q, ti=ti: e.dma_start(out=xs[q], in_=HS[ti * 128:(ti + 1) * 128, :]),
                  reads=[B.HSt[ti], B.dummy], writes=[B.xs[q]], dma=True)
                E("sp", lambda e, q=q, ti=ti: e.dma_start(out=pS[q], in_=PPd[ti * 128:(ti + 1) * 128, :]),
                  reads=[B.dummy], writes=[B.pS[q]], dma=True)
                front(0, q, 8, None, xs[q], B.xs[q], xb_t=ubB, xb_B=B.ubB, uT_t=uTB, uT_B=B.uTB, tb=0, tok_off=q * 128)
            for f in range(NFC):
                gb = 1 + (f % 2) * 2
                ub_ = gb + 1
                for (bk, wsb, wB) in ((gb, wg_sb, B.wg), (ub_, wu_sb, B.wu)):
                    for k in range(8):
                        E("pe", lambda e, k=k, bk=bk, wsb=wsb, f=f: e.matmul(
                            out=bview(bk)[:, 0:256], lhsT=wsb[:, k, f * 128:(f + 1) * 128],
                            rhs=uTB[:, k, :], start=(k == 0), stop=(k == 7)),
                          reads=[B.uTB, wB], writes=[bankB[bk]])
                E("act", lambda e, gb=gb: e.activation(out=silu_s, in_=bview(gb)[:, 0:256], func=AF.Sigmoid),
                  reads=[bankB[gb]], writes=[B.silu_s])
                E("dve", lambda e, gb=gb: e.tensor_tensor(out=silu_s, in0=bview(gb)[:, 0:256], in1=silu_s,
                                                          op=ALU.mult),
                  reads=[bankB[gb], B.silu_s], writes=[B.silu_s])
                E("dve", lambda e, ub_=ub_, f=f: e.tensor_tensor(out=actT[:, f, :], in0=bview(ub_)[:, 0:256],
                                                                 in1=silu_s, op=ALU.mult),
                  reads=[bankB[ub_], B.silu_s], writes=[B.actT], append=(f > 0))
            for q, ti in enumerate(tiles):
                for nb in range(2):
                    bk = 5 + nb
                    for f in range(NFC):
                        E("pe", lambda e, f=f, nb=nb, bk=bk, q=q: e.matmul(
                            out=bview(bk), lhsT=actT[:, f, q * 128:(q + 1) * 128],
                            rhs=wd_sb[:, f, nb * 512:(nb + 1) * 512], start=(f == 0), stop=(f == NFC - 1)),
                          reads=[B.actT, B.wd], writes=[bankB[bk]])
                    E("dve", lambda e, nb=nb, bk=bk, q=q: e.tensor_tensor(
                        out=hbuf[:, nb * 512:(nb + 1) * 512], in0=bview(bk), in1=xs[q][:, nb * 512:(nb + 1) * 512],
                        op=ALU.add), reads=[bankB[bk], B.xs[q]], writes=[B.hbuf], append=(nb == 1))
                front(0, 0, 16, None, hbuf, B.hbuf, xb_t=ubB, xb_B=B.ubB, uT_t=u2T, uT_B=B.u2T, tb=0)
                E("act", lambda e, q=q: e.activation(out=ubB[:, 0:256], in_=pS[q], func=AF.Copy),
                  reads=[B.pS[q]], writes=[B.ubB])
                tv0 = bview16(0)
                for k in range(2):
                    E("pe", lambda e, k=k: e.transpose(out=tv0[:, k * 128:(k + 1) * 128],
                                                       in_=ubB[:, k * 128:(k + 1) * 128], identity=ident),
                      reads=[B.ubB, B.ident], writes=[bankB[0]])
                E("act", lambda e: e.activation(out=ppT, in_=tv0[:, 0:256].rearrange("p (k t) -> p k t", k=2),
                                                func=AF.Copy), reads=[bankB[0]], writes=[B.ppT])
                for nb in range(2):
                    gbk = 7
                    pbk = 0
                    for k in range(8):
                        E("pe", lambda e, k=k, nb=nb: e.matmul(out=bview(gbk), lhsT=u2T[:, k, :],
                                                               rhs=wpg_sb[:, k, nb * 512:(nb + 1) * 512],
                                                               start=(k == 0), stop=(k == 7)),
                          reads=[B.u2T, B.wpg], writes=[bankB[gbk]])
                    E("act", lambda e, nb=nb: e.activation(out=gpl[:, nb * 512:(nb + 1) * 512], in_=bview(gbk),
                                                           func=AF.Sigmoid), reads=[bankB[gbk]], writes=[B.gpl])
                    for k in range(2):
                        E("pe", lambda e, k=k, nb=nb: e.matmul(out=bview(pbk), lhsT=ppT[:, k, :],
                                                               rhs=wpp_sb[:, k, nb * 512:(nb + 1) * 512],
                                                               start=(k == 0), stop=(k == 1)),
                          reads=[B.ppT, B.wpp], writes=[bankB[pbk]])
                    E("dve", lambda e, nb=nb: e.tensor_tensor(out=gpl[:, nb * 512:(nb + 1) * 512], in0=bview(pbk),
                                                              in1=gpl[:, nb * 512:(nb + 1) * 512], op=ALU.mult),
                      reads=[bankB[pbk], B.gpl], writes=[B.gpl])
                    E("dve", lambda e, nb=nb: e.tensor_tensor(out=scrA[:, nb * 512:(nb + 1) * 512],
                                                              in0=gpl[:, nb * 512:(nb + 1) * 512],
                                                              in1=hbuf[:, nb * 512:(nb + 1) * 512], op=ALU.add),
                      reads=[B.gpl, B.hbuf], writes=[B.scrA], append=(nb == 1))
                E("pool", lambda e, ti=ti: e.dma_start(out=Y[ti * 128:(ti + 1) * 128, :], in_=scrA),
                  reads=[B.scrA], dma=True, final=True)

        pg.finalize()
        if os.environ.get('KVERB'):
            print('SIGCOUNTS', {e: max([o.val for o in pg.ops[e] if not o.dma and o.ext is None] + [0]) for e in ENGS}, 'dma', [d.val if d else 0 for d in pg.dma_last])
        n_ep = pg.epoch + 1
        for ep in range(n_ep):
            last = (ep == n_ep - 1)
            with nc.Block() as block:
                @block.tensor
                def _(pe):
                    pg.replay("pe", pe, sems, dma_sems, epoch=ep)

                @block.scalar
                def _(act):
                    pg.replay("act", act, sems, dma_sems, epoch=ep)

                @block.vector
                def _(dve):
                    pg.replay("dve", dve, sems, dma_sems, epoch=ep)

                @block.gpsimd
                def _(pool):
                    pg.replay("pool", pool, sems, dma_sems, epoch=ep)

                @block.sync
                def _(sp):
                    pg.replay("sp", sp, sems, dma_sems, extra_final=last, epoch=ep)
    if os.environ.get('KVERB'):
        print('ICOUNT', pg.icount, 'WCOUNT', pg.wcount)
    return nc


_NC_CACHE = {}


def _get_nc():
    if "nc" not in _NC_CACHE:
        _NC_CACHE["nc"] = build_program()
    return _NC_CACHE["nc"]


def make_in_maps(x_prompt, x_sample, cache_attn_k, cache_attn_v, state_ret, p_prompt, p_sample,
                 g_mix, w_in, g_ret_out, g_q_attn, g_k_attn, rel_bias, w_out, g_ffn,
                 w_ffn_gate, w_ffn_up, w_ffn_down, g_ple, w_ple_gate, w_ple_proj):
    f = lambda a: np.ascontiguousarray(np.asarray(a, dtype=np.float32))
    xp = f(x_prompt).reshape(SEQ, D)
    xsm = f(x_sample).reshape(32 * 64, D)
    pp = f(p_prompt).reshape(SEQ, 256)
    psm = f(p_sample).reshape(32 * 64, 256)
    ck = f(cache_attn_k).reshape(32, 512, 512)
    cv = f(cache_attn_v).reshape(32, 512, 512)
    st = f(state_ret).reshape(32, 4, 128, 128)
    gT = np.concatenate([f(g).reshape(8, 128).T for g in (g_mix, g_ffn, g_ple)], axis=1)
    gqk = np.concatenate([f(g_q_attn).reshape(1, 64), f(g_k_attn).reshape(1, 64)], axis=1)
    DTt, xit, zt = _decay_tables()
    bP, bS = _bias_tables(rel_bias)
    ident = np.eye(128, dtype=np.float32)
    shared = {
        "w_in": f(w_in).reshape(D, 3584), "w_out": f(w_out).reshape(D, D), "w_gate": f(w_ffn_gate).reshape(D, DFF),
        "w_up": f(w_ffn_up).reshape(D, DFF), "w_down": f(w_ffn_down).reshape(DFF, D),
        "w_pg": f(w_ple_gate).reshape(D, D), "w_pp": f(w_ple_proj).reshape(256, D),
        "gT": np.ascontiguousarray(gT), "gret": f(g_ret_out).reshape(1, 512), "gqk": np.ascontiguousarray(gqk),
        "DTt": DTt, "xit": xit, "zt": zt, "biasP": bP, "biasS": bS, "ident": ident,
    }
    maps = []
    for c in range(NCORES):
        halo = xp[c * TPC - 512:c * TPC] if c > 0 else np.zeros((512, D), np.float32)
        Xc = np.concatenate([halo, xp[c * TPC:(c + 1) * TPC], xsm[c * 256:(c + 1) * 256]], axis=0)
        PPc = np.concatenate([pp[c * TPC:(c + 1) * TPC], psm[c * 256:(c + 1) * 256]], axis=0)
        XPc = np.zeros((NPRE_T * 128, D), np.float32)
        if c > 0:
            XPc[NPRE_T * 128 - c * TPC:] = xp[0:c * TPC]
        ropeP, ztp = _prepass_tables(c)
        hm = np.zeros((128, 2), np.float32)
        if c == 0:
            hm[:, 0] = NEG
        m = dict(shared)
        m.update({
            "X": np.ascontiguousarray(Xc), "PP": np.ascontiguousarray(PPc),
            "CK": np.ascontiguousarray(ck[4 * c:4 * c + 4].reshape(2048, 512)),
            "CV": np.ascontiguousarray(cv[4 * c:4 * c + 4].reshape(2048, 512)),
            "ST": np.ascontiguousarray(st[4 * c:4 * c + 4]),
            "rope": _rope_tables(c), "hmask": hm, "XP": XPc, "ropeP": ropeP, "ztp": ztp,
        })
        maps.append(m)
    return maps


def assemble(results):
    y_p = np.concatenate([r["Y"][0:TPC] for r in results], axis=0).reshape(1, SEQ, D)
    y_s = np.concatenate([r["Y"][TPC:TPC + 256] for r in results], axis=0).reshape(32, 64, D)
    ret_p = results[7]["RP"].reshape(1, 1, 4, 128, 128)
    k_p = results[7]["KO"][0:512].reshape(1, 1, 512, 8, 64)
    v_p = results[7]["VO"][0:512].reshape(1, 1, 512, 8, 64)
    ret_s = np.concatenate([r["RS"] for r in results], axis=0).reshape(1, 32, 4, 128, 128)
    k_s = np.concatenate([r["KO"][512:768] for r in results], axis=0).reshape(1, 32, 64, 8, 64)
    v_s = np.concatenate([r["VO"][512:768] for r in results], axis=0).reshape(1, 32, 64, 8, 64)
    out = (y_p, y_s, ret_p, k_p, v_p, ret_s, k_s, v_s)
    return tuple(np.ascontiguousarray(a, dtype=np.float32) for a in out)


def kernel(**inputs):
    nc = _get_nc()
    in_maps = make_in_maps(**inputs)
    res = run_bass_kernel_spmd(nc, in_maps, core_ids=list(range(NCORES)))
    return assemble(res.results)
```

```python
import os
import numpy as np
from contextlib import ExitStack
import concourse.bass as bass
import concourse.mybir as mybir
from concourse.bass_utils import run_bass_kernel_spmd

F32 = mybir.dt.float32
BF16 = mybir.dt.bfloat16
ALU = mybir.AluOpType
AF = mybir.ActivationFunctionType
AX = mybir.AxisListType

NCORES = 8
D = 1024
SEQ = 16384
TPC = SEQ // NCORES
NPT = TPC // 128
NST = 2
NT = NPT + NST
DFF = 2816
NFC = DFF // 128
EPS = 1e-6
NEG = -30000.0
STAGE = int(os.environ.get("KSTAGE", "9"))
NPRE_T = 7 * 16
NPRE = int(os.environ.get("KNPRE", str(NPRE_T)))
KCUT = int(os.environ.get("KCUT", "0"))
PAST_LEN = 2048
LG = np.log1p(-np.exp2(-5.0 - np.arange(4, dtype=np.float64)))


class Op:
    __slots__ = ("eng", "fn", "deps", "signal", "val", "sem", "dma", "ext", "epoch")

    def __init__(self, eng, fn, dma):
        self.ext = None
        self.epoch = 0
        self.eng = eng
        self.fn = fn
        self.dma = dma
        self.deps = []
        self.signal = dma
        self.val = 0
        self.sem = None


class Buf:
    __slots__ = ("w", "r", "dr")

    def __init__(self):
        self.w = []
        self.r = {}
        self.dr = []


ENGS = ("pe", "act", "dve", "pool", "sp")


class Prog:
    def __init__(self, n_dma_sems):
        self.ops = {e: [] for e in ENGS}
        self.n_dma_sems = n_dma_sems
        self.dma_last = [None] * n_dma_sems
        self.dma_counts = [0, 0]
        self.epoch = 0
        self.waited = {e: {} for e in ENGS}
        self.final_waits = []

    def emit(self, eng, fn, reads=(), writes=(), dma=False, append=False, final=False):
        op = Op(eng, fn, dma)
        if getattr(self, 'muted', False):
            return op
        op.epoch = self.epoch
        deps = []
        for b in reads:
            deps += b.w
        for b in writes:
            if not append:
                deps += b.w
            deps += list(b.r.values()) + b.dr
        if dma:
            half = self.n_dma_sems // 2
            q = 0 if eng == "sp" else 1
            idx = q * half + self.dma_counts[q] % half
            self.dma_counts[q] += 1
            prev = self.dma_last[idx]
            op.sem = idx
            op.val = (prev.val if prev is not None else 0) + 16
            if prev is not None:
                deps.append(prev)
            self.dma_last[idx] = op
        seen = set()
        for d in deps:
            if id(d) in seen:
                continue
            seen.add(id(d))
            if (not d.dma) and (not dma) and d.eng == "pe" and eng == "pe":
                continue
            d.signal = True
            op.deps.append(d)
        for b in reads:
            if dma:
                b.dr.append(op)
            else:
                b.r[eng] = op
        for b in writes:
            if append:
                b.w.append(op)
            else:
                b.w = [op]
                b.r = {}
                b.dr = []
        self.ops[eng].append(op)
        if final:
            self.final_waits.append(op)
        return op

    def finalize(self):
        for e in ENGS:
            c = 0
            for op in self.ops[e]:
                if op.dma or op.ext is not None:
                    continue
                if op.signal:
                    c += 1
                    op.val = c
                    op.sem = e

    def new_epoch(self):
        self.epoch += 1

    def replay(self, eng, handle, sems, dma_sems, extra_final=False, epoch=None):
        waited = self.waited[eng]

        def wait(d):
            if d.ext is not None:
                if waited.get("ext", 0) < d.ext[1]:
                    handle.wait_ge(d.ext[0], d.ext[1])
                    waited["ext"] = d.ext[1]
                return
            key = ("d", d.sem) if d.dma else d.sem
            if waited.get(key, 0) >= d.val:
                return
            s = dma_sems[d.sem] if d.dma else sems[d.sem]
            self.wcount = getattr(self, 'wcount', {})
            self.wcount[eng] = self.wcount.get(eng, 0) + 1
            handle.wait_ge(s, d.val)
            waited[key] = d.val

        for op in self.ops[eng]:
            if epoch is not None and op.epoch != epoch:
                continue
            self.icount = getattr(self, 'icount', {})
            self.icount[eng] = self.icount.get(eng, 0) + 1
            for d in op.deps:
                wait(d)
            ins = op.fn(handle)
            if op.signal and op.ext is None:
                if op.dma:
                    ins.then_inc(dma_sems[op.sem], 16)
                else:
                    ins.then_inc(sems[op.sem], 1)
        if extra_final:
            for d in self.final_waits:
                wait(d)
            for d in self.dma_last:
                if d is not None:
                    wait(d)


def _rope_tables(core):
    half = 64
    inv = (np.float32(10000.0) ** (-np.arange(half, dtype=np.float32) / np.float32(half))).astype(np.float32)
    out = np.zeros((NT, 128, 256), np.float32)
    ks = np.float32(128.0 ** -0.5)
    for ti in range(NT):
        if ti < NPT:
            pos = (core * TPC + ti * 128 + np.arange(128)).astype(np.float32)
        else:
            pos = (PAST_LEN + (np.arange(128) % 64)).astype(np.float32)
        ang = (pos[:, None] * inv[None, :]).astype(np.float32)
        c = np.cos(ang).astype(np.float32)
        s = np.sin(ang).astype(np.float32)
        out[ti, :, 0:64] = c
        out[ti, :, 64:128] = s
        out[ti, :, 128:192] = c * ks
        out[ti, :, 192:256] = s * ks
    return out


def _prepass_tables(core):
    half = 64
    inv = (np.float32(10000.0) ** (-np.arange(half, dtype=np.float32) / np.float32(half))).astype(np.float32)
    ks = np.float32(128.0 ** -0.5)
    n0 = core * TPC
    r = np.arange(NPRE_T * 128)
    pos = (n0 - NPRE_T * 128 + r).astype(np.float32)
    ang = (pos[:, None] * inv[None, :]).astype(np.float32)
    tab = np.concatenate([np.cos(ang).astype(np.float32) * ks, np.sin(ang).astype(np.float32) * ks], axis=1)
    tab = tab.reshape(NPRE_T, 128, 128).astype(np.float32)
    zt = np.zeros((128, 4 * NPRE_T), np.float64)
    p = np.arange(128)
    for i in range(NPRE_T):
        for h in range(4):
            zt[:, 4 * i + h] = np.exp((NPRE_T * 128 - 1.0 - (i * 128 + p)) * LG[h])
    return np.ascontiguousarray(tab), zt.astype(np.float32)


def _decay_tables():
    n = np.arange(128)
    diff = n[None, :] - n[:, None]
    DT = np.zeros((2, 128, 4, 128), np.float64)
    xi = np.zeros((3, 128, 4, 128), np.float64)
    zt = np.zeros((128, 12 + 4 * NPT), np.float64)
    same = (n[None, :] // 64) == (n[:, None] // 64)
    for h in range(4):
        DT[0, :, h, :] = np.where(diff >= 0, np.exp(np.maximum(diff, 0) * LG[h]), 0.0)
        DT[1, :, h, :] = np.where((diff >= 0) & same, np.exp(np.maximum(diff, 0) * LG[h]), 0.0)
        xi[0, :, h, :] = np.exp((n + 1.0) * LG[h])[None, :]
        xs = np.exp(((n % 64) + 1.0) * LG[h])
        xi[1, :, h, :] = np.where(n < 64, xs, 0.0)[None, :]
        xi[2, :, h, :] = np.where(n >= 64, xs, 0.0)[None, :]
        zt[:, h] = np.exp((127.0 - n) * LG[h])
        zt[:, 4 + h] = np.where(n < 64, np.exp((63.0 - (n % 64)) * LG[h]), 0.0)
        zt[:, 8 + h] = np.where(n >= 64, np.exp((63.0 - (n % 64)) * LG[h]), 0.0)
        for i in range(NPT):
            zt[:, 12 + i * 4 + h] = np.exp((TPC - 1.0 - (i * 128 + n)) * LG[h])
    return (DT.reshape(2, 128, 512).astype(np.float32), xi.reshape(3, 128, 512).astype(np.float32),
            zt.astype(np.float32))


def _coef_table(core):
    co = np.zeros((2, 8, 4), np.float64)
    for cp in range(8):
        for h in range(4):
            if cp < core:
                co[0, cp, h] = np.exp(TPC * LG[h] * (core - 1 - cp))
            co[1, cp, h] = np.exp(TPC * LG[h] * (7 - cp))
    return np.broadcast_to(co.reshape(1, 64), (128, 64)).astype(np.float32).copy()


def _bias_tables(rel_bias):
    rb = np.asarray(rel_bias, np.float32).reshape(8, 513)
    l = np.arange(128)[:, None]
    t = np.arange(128)[None, :]
    bp = np.zeros((128, 5, 8, 128), np.float32)
    for j in range(5):
        d = 128 * (4 - j) + t - l
        idx = np.clip(d, -256, 256) + 256
        kc = 2 * j + (l >= 64)
        qc = 8 + (t >= 64)
        valid = (kc >= qc - 8) & (kc <= qc)
        for h in range(8):
            bp[:, j, h, :] = np.where(valid, rb[h][idx], NEG)
    bs = np.zeros((128, 9, 8, 128), np.float32)
    for j in range(4):
        da = np.clip(128 * (4 - j) + t - l, -256, 256) + 256
        db = np.clip(128 * (4 - j) + (t - 64) - l, -256, 256) + 256
        for h in range(8):
            bs[:, j, h, :] = np.where(t < 64, rb[h][da], NEG)
            bs[:, 4 + j, h, :] = np.where(t >= 64, rb[h][db], NEG)
    do = np.clip(t - l, -256, 256) + 256
    vo = ((t < 64) & (l < 64)) | ((t >= 64) & (l >= 64))
    for h in range(8):
        bs[:, 8, h, :] = np.where(vo, rb[h][do], NEG)
    return bp.reshape(128, 5 * 1024), bs.reshape(128, 9 * 1024)


def build_program():
    nc = bass.Bass("TRN2", target_bir_lowering=False)

    def din(name, shape):
        return nc.dram_tensor(name, list(shape), F32, kind="ExternalInput").ap()

    def dout(name, shape):
        return nc.dram_tensor(name, list(shape), F32, kind="ExternalOutput").ap()

    X = din("X", [512 + TPC + 256, D])
    PPd = din("PP", [NT * 128, 256])
    CK = din("CK", [4 * 512, 512])
    CV = din("CV", [4 * 512, 512])
    STd = din("ST", [4, 4, 128, 128])
    Win = din("w_in", [D, 3584])
    Wout = din("w_out", [D, D])
    Wg = din("w_gate", [D, DFF])
    Wu = din("w_up", [D, DFF])
    Wd = din("w_down", [DFF, D])
    Wpg = din("w_pg", [D, D])
    Wpp = din("w_pp", [256, D])
    gT_d = din("gT", [128, 24])
    gret_d = din("gret", [1, 512])
    gqk_d = din("gqk", [1, 128])
    rope_d = din("rope", [NT, 128, 256])
    DT_d = din("DTt", [2, 128, 512])
    xi_d = din("xit", [3, 128, 512])
    zt_d = din("zt", [128, 12 + 4 * NPT])
    XP = din("XP", [NPRE_T * 128, D])
    ropeP_d = din("ropeP", [NPRE_T, 128, 128])
    ztp_d = din("ztp", [128, 4 * NPRE_T])
    bP_d = din("biasP", [128, 5 * 1024])
    bS_d = din("biasS", [128, 9 * 1024])
    hm_d = din("hmask", [128, 2])
    id_d = din("ident", [128, 128])

    Y = dout("Y", [NT * 128, D])
    KO = dout("KO", [6 * 128, 512])
    VO = dout("VO", [6 * 128, 512])
    RS = dout("RS", [4, 4, 128, 128])
    RP = dout("RP", [4, 128, 128])

    HS = nc.dram_tensor("HS", [NT * 128, D], F32)

    NDS = 40
    pg = Prog(NDS)
    E = pg.emit

    with ExitStack() as es:
        WR_t = es.enter_context(nc.sbuf_tensor("WR", [128, 77824], BF16))
        NF = 13200
        FR_t = es.enter_context(nc.sbuf_tensor("FR", [128, NF], F32))
        banks = [es.enter_context(nc.psum_tensor(f"bank{i}", [128, 512], F32)) for i in range(8)]
        bankB = [Buf() for _ in range(8)]
        sems = {e: es.enter_context(nc.semaphore(f"s_{e}")) for e in ("pe", "act", "dve", "pool")}
        dma_sems = [es.enter_context(nc.semaphore(f"d{i}")) for i in range(NDS)]

        class Carve:
            def __init__(self, t, lo, hi):
                self.t, self.off, self.hi = t, lo, hi

            def take(self, n):
                a = self.t[:, self.off:self.off + n]
                self.off += n
                assert self.off <= self.hi, (self.off, self.hi)
                return a

        w_in_sb = WR_t[:, 0:28672].rearrange("p (k n) -> p k n", k=8)
        w_out_sb = WR_t[:, 28672:36864].rearrange("p (k n) -> p k n", k=8)
        ca = Carve(WR_t, 36864, 77824)
        biasP = ca.take(5 * 1024)
        biasS = ca.take(9 * 1024)
        pT = [ca.take(1024) for _ in range(3)]
        NSLOT = 6
        knT = [ca.take(512).rearrange("p (j t) -> p j t", j=4) for _ in range(NSLOT)]
        vaug_flat = [ca.take(528) for _ in range(NSLOT)]
        vaug = [v.rearrange("p (h c) -> p h c", c=66) for v in vaug_flat]
        xb = ca.take(1024)
        uT = ca.take(1024).rearrange("p (k t) -> p k t", k=8)
        xb2 = [xb, ca.take(1024)]
        uT2 = [uT, ca.take(1024).rearrange("p (k t) -> p k t", k=8)]
        qrot = ca.take(512)
        krot = ca.take(512)
        kz = ca.take(512)
        vr = ca.take(512)
        qn = ca.take(512)
        kn = ca.take(512)
        qT = ca.take(512)
        kT = ca.take(512)
        qxT = [ca.take(512), ca.take(512)]
        sTm = ca.take(512)
        Rb = [ca.take(512), ca.take(512)]
        qnT = ca.take(512).rearrange("p (j t) -> p j t", j=4)
        qz = [ca.take(512).rearrange("p (j t) -> p j t", j=4) for _ in range(2)]
        kzb = ca.take(512)
        mixed = ca.take(1024)
        mixedT = ca.take(1024).rearrange("p (k t) -> p k t", k=8)
        kcb = ca.take(512)

        wg_sb = WR_t[:, 0:22528].rearrange("p (k n) -> p k n", k=8)
        wu_sb = WR_t[:, 22528:45056].rearrange("p (k n) -> p k n", k=8)
        wd_sb = WR_t[:, 45056:67584].rearrange("p (f n) -> p f n", f=NFC)
        wpg_sb = WR_t[:, 67584:75776].rearrange("p (k n) -> p k n", k=8)
        wpp_sb = WR_t[:, 75776:77824].rearrange("p (k n) -> p k n", k=2)

        cf = Carve(FR_t, 0, NF)
        xs = [cf.take(1024), cf.take(1024)]
        hbuf = cf.take(1024)
        scrA = cf.take(1024)
        gT = cf.take(24)
        gret = cf.take(512)
        gqk = cf.take(128)
        zt = cf.take(12 + 4 * NPT)
        ztp = cf.take(4 * NPRE_T)
        hmask = cf.take(2)
        ss1 = cf.take(2)
        rstd1 = cf.take(2)
        ss8 = cf.take(16)
        rstd8 = cf.take(16)
        ss4 = cf.take(4)
        rstd4 = cf.take(4)
        rec8 = cf.take(8)
        dmy = cf.take(2)
        epsc = cf.take(2)
        ident = cf.take(64).bitcast(BF16)
        junk = cf.take(512).bitcast(BF16)
        persist_end = cf.off
        ropeS = [cf.take(256), cf.take(256)]
        Gs = cf.take(512)
        scrB = cf.take(512)
        scrC = cf.take(512)
        kvout = cf.take(1024)
        Rf = [cf.take(512), cf.take(512)]
        DTs = [cf.take(512), cf.take(512)]
        xiS = [cf.take(512), cf.take(512), cf.take(512)]
        Sacc = cf.take(512)
        a_end = cf.off
        cb = Carve(FR_t, persist_end, NF)
        pS = [cb.take(256), cb.take(256)]
        gpl = cb.take(1024)
        bB = cb.take(2048 // 2 + 5632 // 2 + 1024 // 2 + 1024 // 2 + 256 // 2)
        bBb = bB.bitcast(BF16)
        o = 0
        uTB = bBb[:, o:o + 2048].rearrange("p (k t) -> p k t", k=8); o += 2048
        actT = bBb[:, o:o + 5632].rearrange("p (f t) -> p f t", f=NFC); o += 5632
        ubB = bBb[:, o:o + 1024]; o += 1024
        u2T = bBb[:, o:o + 1024].rearrange("p (k t) -> p k t", k=8); o += 1024
        ppT = bBb[:, o:o + 256].rearrange("p (k t) -> p k t", k=2); o += 256
        silu_s = cb.take(256)

        class NB:
            pass
        B = NB()
        for name in ("w_in", "w_out", "wg", "wu", "wd", "wpg", "wpp", "biasP", "biasS", "xb", "uT", "qrot", "krot",
                     "kz", "vr", "qn", "kn", "qT", "kT", "sTm", "qnT", "mixed", "mixedT", "ident", "kcb", "hbuf",
                     "scrA", "consts", "ss1", "rstd1", "ss8", "rstd8", "ss4", "rstd4", "rec8", "Gs", "scrB", "scrC",
                     "kvout", "Sacc", "gpl", "uTB", "actT", "ubB", "u2T", "ppT", "silu_s", "HS", "cc_in", "cc_out",
                     "vones", "dummy", "junk", "cc2", "qz", "kzb"):
            setattr(B, name, Buf())
        B.xs = [Buf(), Buf()]
        B.xb2 = [B.xb, Buf()]
        B.uT2 = [B.uT, Buf()]
        B.pT = [Buf() for _ in range(3)]
        B.knT = [Buf() for _ in range(NSLOT)]
        B.vaug = [Buf() for _ in range(NSLOT)]
        B.qxT = [Buf(), Buf()]
        B.Rb = [Buf(), Buf()]
        B.Rf = [Buf(), Buf()]
        B.ropeS = [Buf(), Buf()]
        B.pS = [Buf(), Buf()]
        B.HSt = [Buf() for _ in range(NT)]

        def bview(i):
            return banks[i][:, 0:512]

        def bview16(i):
            return banks[i][:, 0:512].bitcast(BF16)

        def load_const(dst, src, buf, eng="sp"):
            E(eng, lambda e, dst=dst, src=src: e.dma_start(out=dst, in_=src), writes=[buf], dma=True, append=True)

        load_const(gT, gT_d, B.consts)
        load_const(gret, gret_d.partition_broadcast(128), B.consts)
        load_const(gqk, gqk_d.partition_broadcast(128), B.consts)
        load_const(zt, zt_d, B.consts)
        load_const(ztp, ztp_d, B.consts)
        load_const(hmask, hm_d, B.consts)
        for i in range(2):
            load_const(DTs[i], DT_d[i], B.consts)
        for i in range(3):
            load_const(xiS[i], xi_d[i], B.consts)
        load_const(ident, id_d, B.ident, eng="pool")
        Win_v = Win.rearrange("(k p) n -> p k n", p=128)
        for k in range(8):
            load_const(w_in_sb[:, k, :], Win_v[:, k, :], B.w_in, eng="pool")
        Wout_v = Wout.rearrange("(k p) n -> p k n", p=128)
        for k in range(8):
            load_const(w_out_sb[:, k, :], Wout_v[:, k, :], B.w_out, eng="pool")
        for j in range(5):
            load_const(biasP[:, j * 1024:(j + 1) * 1024], bP_d[:, j * 1024:(j + 1) * 1024], B.biasP, eng="pool")
        for j in range(9):
            load_const(biasS[:, j * 1024:(j + 1) * 1024], bS_d[:, j * 1024:(j + 1) * 1024], B.biasS, eng="pool")
        for s in range(NSLOT):
            E("pool", lambda e, s=s: e.memset(vaug_flat[s], 1.0), writes=[B.vaug[s]])

        E("pool", lambda e: e.memset(epsc, EPS), writes=[B.consts], append=True)
        for _q in range(2):
            E("pool", lambda e, _q=_q: e.memset(qz[_q], 0.0), writes=[B.qz], append=(_q == 1))
        gq_b = gqk[:, 0:64].unsqueeze(1).to_broadcast([128, 8, 64])
        gk_b = gqk[:, 64:128].unsqueeze(1).to_broadcast([128, 8, 64])

        def front(row0, slot, gcol, src_dram, xbuf_t, xbuf_B, xb_t=None, xb_B=None, uT_t=None, uT_B=None, tb=2,
                  ntok=128, tok_off=0):
            xb_t = xb if xb_t is None else xb_t
            xb_B = B.xb if xb_B is None else xb_B
            uT_t = uT if uT_t is None else uT_t
            uT_B = B.uT if uT_B is None else uT_B
            if src_dram is not None:
                E("sp", lambda e: e.dma_start(out=xbuf_t, in_=src_dram[row0:row0 + 128, :]), writes=[xbuf_B], dma=True)
            E("act", lambda e: e.activation(out=junk, in_=xbuf_t, func=AF.Square, accum_out=ss1[:, 0:1]),
              reads=[xbuf_B], writes=[B.junk, B.ss1])
            E("act", lambda e: e.activation(out=rstd1[:, 0:1], in_=ss1[:, 0:1], func=AF.Sqrt, bias=epsc[:, 0:1], scale=1.0 / D), reads=[B.ss1, B.consts], writes=[B.rstd1])
            E("dve", lambda e: e.reciprocal(out=rstd1[:, 0:1], in_=rstd1[:, 0:1]), reads=[B.rstd1], writes=[B.rstd1])
            E("dve", lambda e: e.tensor_scalar(out=xb_t, in0=xbuf_t, scalar1=rstd1[:, 0:1], scalar2=None,
                                               op0=ALU.mult), reads=[xbuf_B, B.rstd1], writes=[xb_B])
            tv = bview16(tb)
            for k in range(8):
                E("pe", lambda e, k=k: e.transpose(out=tv[:, k * 128:(k + 1) * 128], in_=xb_t[:, k * 128:(k + 1) * 128],
                                                   identity=ident), reads=[xb_B, B.ident], writes=[bankB[tb]])
            E("dve", lambda e: e.tensor_tensor(out=uT_t[:, :, tok_off:tok_off + 128],
                                               in0=tv.rearrange("p (k t) -> p k t", k=8),
                                               in1=gT[:, gcol:gcol + 8].unsqueeze(2).to_broadcast([128, 8, 128]),
                                               op=ALU.mult),
              reads=[bankB[tb], B.consts], writes=[uT_B])

        cur_par = [0]

        def zmm(bank_i, col):
            bv = bview(bank_i)
            uTp, uTB_ = uT2[cur_par[0]], B.uT2[cur_par[0]]
            for k in range(8):
                E("pe", lambda e, k=k: e.matmul(out=bv, lhsT=uTp[:, k, :], rhs=w_in_sb[:, k, col * 512:(col + 1) * 512],
                                                start=(k == 0), stop=(k == 7)),
                  reads=[uTB_, B.w_in], writes=[bankB[bank_i]])

        def rope(bank_i, rS, rB, coff, out_t, out_B):
            z = bview(bank_i).rearrange("p (h d) -> p h d", h=4)
            x1 = z[:, :, 0:64]
            x2 = z[:, :, 64:128]
            cosb = rS[:, coff:coff + 64].unsqueeze(1).to_broadcast([128, 4, 64])
            sinb = rS[:, coff + 64:coff + 128].unsqueeze(1).to_broadcast([128, 4, 64])
            t1 = scrB[:, 0:256].rearrange("p (h d) -> p h d", h=4)
            t2 = scrB[:, 256:512].rearrange("p (h d) -> p h d", h=4)
            t3 = scrC[:, 0:256].rearrange("p (h d) -> p h d", h=4)
            t4 = scrC[:, 256:512].rearrange("p (h d) -> p h d", h=4)
            ov = out_t.rearrange("p (h d) -> p h d", h=4)
            rd = [bankB[bank_i], rB]
            E("dve", lambda e: e.tensor_tensor(out=t1, in0=x1, in1=cosb, op=ALU.mult), reads=rd, writes=[B.scrB])
            E("dve", lambda e: e.tensor_tensor(out=t2, in0=x2, in1=sinb, op=ALU.mult), reads=rd, writes=[B.scrB],
              append=True)
            E("dve", lambda e: e.tensor_tensor(out=t3, in0=x1, in1=sinb, op=ALU.mult), reads=rd, writes=[B.scrC])
            E("dve", lambda e: e.tensor_tensor(out=t4, in0=x2, in1=cosb, op=ALU.mult), reads=rd, writes=[B.scrC],
              append=True)
            E("dve", lambda e: e.tensor_tensor(out=ov[:, :, 0:64], in0=t1, in1=t2, op=ALU.subtract),
              reads=[B.scrB], writes=[out_B])
            E("dve", lambda e: e.tensor_tensor(out=ov[:, :, 64:128], in0=t3, in1=t4, op=ALU.add),
              reads=[B.scrC], writes=[out_B], append=True)

        def load_rope(ti, slot):
            E("pool", lambda e: e.dma_start(out=ropeS[slot], in_=rope_d[ti]), writes=[B.ropeS[slot]], dma=True)

        SB = 7
        if KCUT == 1:
            pg.muted = True
        def pre_front(i):
            slot = i % 2
            E("pool", lambda e, i=i, slot=slot: e.dma_start(out=ropeS[slot][:, 0:128], in_=ropeP_d[i]),
              writes=[B.ropeS[slot]], dma=True)
            front(i * 128, slot, 0, XP, xs[slot], B.xs[slot], xb_t=xb2[slot], xb_B=B.xb2[slot], uT_t=uT2[slot],
                  uT_B=B.uT2[slot], tb=2 + slot)

        def pre_smm(i):
            for h in range(4):
                E("pe", lambda e, h=h, i=i: e.matmul(out=bview(SB)[:, h * 128:(h + 1) * 128],
                                                     lhsT=kz[:, h * 128:(h + 1) * 128],
                                                     rhs=vr[:, h * 128:(h + 1) * 128],
                                                     start=(i == 0 and h == 0), stop=(i == NPRE - 1),
                                                     skip_group_check=True),
                  reads=[B.kz, B.vr], writes=[bankB[SB]])

        pre_front(0)
        for i in range(NPRE):
            slot = i % 2
            if i + 1 < NPRE:
                pre_front(i + 1)
            cur_par[0] = slot
            zmm(0, 1)
            zmm(1, 2)
            if i > 0:
                pre_smm(i - 1)
            rope(0, ropeS[slot], B.ropeS[slot], 0, kz, B.kz)
            E("dve", lambda e, i=i: e.tensor_tensor(
                out=vr.rearrange("p (h d) -> p h d", h=4), in0=bview(1).rearrange("p (h d) -> p h d", h=4),
                in1=ztp[:, 4 * i:4 * i + 4].unsqueeze(2).to_broadcast([128, 4, 128]), op=ALU.mult),
              reads=[bankB[1], B.consts], writes=[B.vr])
        pre_smm(NPRE - 1)
        cur_par[0] = 0
        if KCUT == 3:
            pg.muted = True
        E("dve", lambda e: e.tensor_copy(out=Sacc, in_=bview(SB)), reads=[bankB[SB]], writes=[B.Sacc])

        def load_sample_state(sq, which):
            src = STd[sq].rearrange("h p v -> p h v")
            E("sp", lambda e: e.dma_start(out=Rf[which].rearrange("p (h v) -> p h v", h=4), in_=src),
              writes=[B.Rf[which]], dma=True)
            E("act", lambda e: e.activation(out=Rb[which], in_=Rf[which], func=AF.Copy),
              reads=[B.Rf[which]], writes=[B.Rb[which]])

        def kv_project(own_slot, want_out, out_row):
            zmm(0, 5)
            zmm(1, 6)
            kz_ = bview(0)
            E("act", lambda e: e.activation(out=scrA[:, 0:512], in_=kz_, func=AF.Square),
              reads=[bankB[0]], writes=[B.scrA])
            E("dve", lambda e: e.tensor_reduce(out=ss8[:, 8:16], in_=scrA[:, 0:512].rearrange("p (h d) -> p h d", h=8),
                                               axis=AX.X, op=ALU.add), reads=[B.scrA], writes=[B.ss8])
            E("act", lambda e: e.activation(out=rstd8[:, 8:16], in_=ss8[:, 8:16], func=AF.Sqrt, bias=epsc[:, 0:1], scale=1.0 / 64), reads=[B.ss8, B.consts], writes=[B.rstd8])
            E("dve", lambda e: e.reciprocal(out=rstd8[:, 8:16], in_=rstd8[:, 8:16]), reads=[B.rstd8], writes=[B.rstd8])
            E("dve", lambda e: e.tensor_tensor(out=scrA[:, 512:1024].rearrange("p (h d) -> p h d", h=8),
                                               in0=kz_.rearrange("p (h d) -> p h d", h=8),
                                               in1=rstd8[:, 8:16].unsqueeze(2).to_broadcast([128, 8, 64]),
                                               op=ALU.mult), reads=[bankB[0], B.rstd8], writes=[B.scrA])
            E("dve", lambda e: e.tensor_tensor(out=kvout[:, 0:512].rearrange("p (h d) -> p h d", h=8),
                                               in0=scrA[:, 512:1024].rearrange("p (h d) -> p h d", h=8),
                                               in1=gk_b, op=ALU.mult), reads=[B.scrA, B.consts], writes=[B.kvout])
            E("act", lambda e: e.activation(out=kn, in_=kvout[:, 0:512], func=AF.Copy),
              reads=[B.kvout], writes=[B.kn])
            E("act", lambda e: e.activation(out=kvout[:, 512:1024], in_=bview(1), func=AF.Copy),
              reads=[bankB[1]], writes=[B.kvout], append=True)
            E("act", lambda e: e.activation(out=vaug[own_slot][:, :, 0:64],
                                            in_=bview(1).rearrange("p (h d) -> p h d", h=8), func=AF.Copy),
              reads=[bankB[1]], writes=[B.vaug[own_slot]])
            tv = bview16(2)
            for j in range(4):
                E("pe", lambda e, j=j: e.transpose(out=tv[:, j * 128:(j + 1) * 128], in_=kn[:, j * 128:(j + 1) * 128],
                                                   identity=ident), reads=[B.kn, B.ident], writes=[bankB[2]])
            E("act", lambda e: e.activation(out=knT[own_slot], in_=tv[:, 0:512].rearrange("p (j t) -> p j t", j=4),
                                            func=AF.Copy), reads=[bankB[2]], writes=[B.knT[own_slot]])
            if want_out:
                E("pool", lambda e: e.dma_start(out=KO[out_row:out_row + 128, :], in_=kvout[:, 0:512]),
                  reads=[B.kvout], dma=True, final=True)
                E("pool", lambda e: e.dma_start(out=VO[out_row:out_row + 128, :], in_=kvout[:, 512:1024]),
                  reads=[B.kvout], dma=True, final=True)

        def load_cache_tile(sq, j, slot):
            r0 = sq * 512 + j * 128
            E("pool", lambda e: e.dma_start(out=kcb, in_=CK[r0:r0 + 128, :]), writes=[B.kcb], dma=True)
            E("pool", lambda e: e.dma_start(out=vaug[slot][:, :, 0:64],
                                            in_=CV[r0:r0 + 128, :].rearrange("p (h d) -> p h d", h=8)),
              writes=[B.vaug[slot]], dma=True)
            tv = bview16(2)
            for jj in range(4):
                E("pe", lambda e, jj=jj: e.transpose(out=tv[:, jj * 128:(jj + 1) * 128],
                                                     in_=kcb[:, jj * 128:(jj + 1) * 128], identity=ident),
                  reads=[B.kcb, B.ident], writes=[bankB[2]])
            E("act", lambda e: e.activation(out=knT[slot], in_=tv[:, 0:512].rearrange("p (j t) -> p j t", j=4),
                                            func=AF.Copy), reads=[bankB[2]], writes=[B.knT[slot]])

        pT_ctr = [0]

        def attention(keys):
            oA, oB = 4, 5
            nk = len(keys)
            def scores(jn):
                slot, bias_ap, bias_B, mcol, loader = keys[jn]
                if loader is not None:
                    loader()
                pair = (6, 7) if jn % 2 == 0 else (0, 1)
                for h in range(8):
                    bk = pair[h // 4]
                    E("pe", lambda e, h=h, bk=bk, slot=slot: e.matmul(
                        out=bview(bk)[:, (h % 4) * 128:(h % 4 + 1) * 128],
                        lhsT=knT[slot][:, h // 2, :], rhs=qz[h % 2][:, h // 2, :],
                        start=True, stop=True), reads=[B.knT[slot], B.qz], writes=[bankB[bk]])

            scores(0)
            for jn, (slot, bias_ap, bias_B, mcol, loader) in enumerate(keys):
                pair = (6, 7) if jn % 2 == 0 else (0, 1)
                ps = pT_ctr[0] % 3
                pT_ctr[0] += 1
                for half in range(2):
                    bk = pair[half]
                    E("dve", lambda e, bk=bk, half=half, bias_ap=bias_ap: e.scalar_tensor_tensor(
                        out=bview(bk), in0=bview(bk), scalar=0.125, in1=bias_ap[:, half * 512:(half + 1) * 512],
                        op0=ALU.mult, op1=ALU.add), reads=[bankB[bk], bias_B], writes=[bankB[bk]])
                    E("act", lambda e, bk=bk, half=half, ps=ps, mcol=mcol: e.activation(
                        out=pT[ps][:, half * 512:(half + 1) * 512], in_=bview(bk), func=AF.Exp, bias=mcol, scale=1.0),
                      reads=[bankB[bk], B.consts], writes=[B.pT[ps]], append=(half == 1))
                if jn + 1 < nk:
                    scores(jn + 1)
                for h in range(8):
                    ob = oA if h < 4 else oB
                    E("pe", lambda e, h=h, ob=ob, ps=ps, slot=slot, jn=jn: e.matmul(
                        out=bview(ob)[:, (h % 4) * 65:(h % 4) * 65 + 65], lhsT=pT[ps][:, h * 128:(h + 1) * 128],
                        rhs=vaug[slot][:, h, 0:65], start=(jn == 0 and h % 4 == 0), stop=(jn == nk - 1),
                        skip_group_check=True), reads=[B.pT[ps], B.vaug[slot]], writes=[bankB[ob]])
            cut(184)
            for half, ob in enumerate((oA, oB)):
                ov = bview(ob)[:, 0:260].rearrange("p (h c) -> p h c", c=65)
                E("dve", lambda e, ov=ov, half=half: e.reciprocal(out=rec8[:, half * 4:half * 4 + 4], in_=ov[:, :, 64]),
                  reads=[bankB[ob]], writes=[B.rec8], append=(half == 1))
            for half, ob in enumerate((oA, oB)):
                ov = bview(ob)[:, 0:260].rearrange("p (h c) -> p h c", c=65)
                E("dve", lambda e, ov=ov, half=half: e.tensor_tensor(
                    out=mixed[:, 512 + half * 256:512 + half * 256 + 256].rearrange("p (h d) -> p h d", h=4),
                    in0=ov[:, :, 0:64],
                    in1=rec8[:, half * 4:half * 4 + 4].unsqueeze(2).to_broadcast([128, 4, 64]), op=ALU.mult),
                  reads=[bankB[ob], B.rec8], writes=[B.mixed], append=True)

        tile_calls = [0]

        def cut(n):
            if KCUT == n and tile_calls[0] == int(os.environ.get('KTILE', '1')):
                pg.muted = True

        def tileA(ti, row0, slot, own_slot, keys, sample, want_out, out_row, seqs=None, prefetch=None):
            tile_calls[0] += 1
            cur_par[0] = slot
            rS, rB = ropeS[slot], B.ropeS[slot]
            cut(10)
            zmm(0, 0)
            rope(0, rS, rB, 0, qrot, B.qrot)
            zmm(1, 1)
            rope(1, rS, rB, 128, krot, B.krot)
            zc = 4 if sample else 0
            E("dve", lambda e: e.tensor_tensor(
                out=kz.rearrange("p (h d) -> p h d", h=4), in0=krot.rearrange("p (h d) -> p h d", h=4),
                in1=zt[:, zc:zc + 4].unsqueeze(2).to_broadcast([128, 4, 128]), op=ALU.mult),
              reads=[B.krot, B.consts], writes=[B.kz])
            if sample:
                E("dve", lambda e: e.tensor_tensor(
                    out=kzb.rearrange("p (h d) -> p h d", h=4), in0=krot.rearrange("p (h d) -> p h d", h=4),
                    in1=zt[:, 8:12].unsqueeze(2).to_broadcast([128, 4, 128]), op=ALU.mult),
                  reads=[B.krot, B.consts], writes=[B.kzb])
            zmm(0, 2)
            E("act", lambda e: e.activation(out=vr, in_=bview(0), func=AF.Copy), reads=[bankB[0]], writes=[B.vr])
            zmm(1, 3)
            if os.environ.get('KV') == '1':
                E("act", lambda e: e.activation(out=Gs, in_=bview(0), func=AF.Exp, scale=-1.0), reads=[bankB[0]],
                  writes=[B.Gs])
            elif os.environ.get('KV') == '3':
                E("act", lambda e: e.activation(out=Gs, in_=hbuf[:, 0:512], func=AF.Copy), reads=[B.hbuf],
                  writes=[B.Gs])
            elif os.environ.get('KV') == '2':
                E("dve", lambda e: e.tensor_copy(out=Gs, in_=bview(1)), reads=[bankB[1]], writes=[B.Gs])
            else:
                E("act", lambda e: e.activation(out=Gs, in_=bview(1), func=AF.Exp, scale=-1.0), reads=[bankB[1]],
                  writes=[B.Gs])
            E("dve", lambda e: e.tensor_scalar(out=Gs, in0=Gs, scalar1=1.0, scalar2=None, op0=ALU.add),
              reads=[B.Gs], writes=[B.Gs])
            E("dve", lambda e: e.reciprocal(out=Gs, in_=Gs), reads=[B.Gs], writes=[B.Gs])
            E("dve", lambda e: e.tensor_tensor(out=Gs, in0=bview(1), in1=Gs, op=ALU.mult),
              reads=[bankB[1], B.Gs], writes=[B.Gs])
            E("dve", lambda e: e.tensor_tensor(out=Gs, in0=Gs, in1=gret, op=ALU.mult),
              reads=[B.Gs, B.consts], writes=[B.Gs])
            cut(11)
            tv = bview16(2)
            for h in range(4):
                E("pe", lambda e, h=h: e.transpose(out=tv[:, h * 128:(h + 1) * 128], in_=qrot[:, h * 128:(h + 1) * 128],
                                                   identity=ident), reads=[B.qrot, B.ident], writes=[bankB[2]])
            for h in range(4):
                E("pe", lambda e, h=h: e.transpose(out=tv[:, 512 + h * 128:512 + (h + 1) * 128],
                                                   in_=krot[:, h * 128:(h + 1) * 128], identity=ident),
                  reads=[B.krot, B.ident], writes=[bankB[2]])
            E("act", lambda e: e.activation(out=qT, in_=tv[:, 0:512], func=AF.Copy), reads=[bankB[2]], writes=[B.qT])
            E("act", lambda e: e.activation(out=kT, in_=tv[:, 512:1024], func=AF.Copy), reads=[bankB[2]],
              writes=[B.kT])
            cut(12)
            zmm(0, 4)
            E("act", lambda e: e.activation(out=scrA[:, 0:512], in_=bview(0), func=AF.Square),
              reads=[bankB[0]], writes=[B.scrA])
            E("dve", lambda e: e.tensor_reduce(out=ss8[:, 0:8], in_=scrA[:, 0:512].rearrange("p (h d) -> p h d", h=8),
                                               axis=AX.X, op=ALU.add), reads=[B.scrA], writes=[B.ss8])
            E("act", lambda e: e.activation(out=rstd8[:, 0:8], in_=ss8[:, 0:8], func=AF.Sqrt, bias=epsc[:, 0:1], scale=1.0 / 64), reads=[B.ss8, B.consts], writes=[B.rstd8])
            E("dve", lambda e: e.reciprocal(out=rstd8[:, 0:8], in_=rstd8[:, 0:8]), reads=[B.rstd8], writes=[B.rstd8])
            E("dve", lambda e: e.tensor_tensor(out=scrA[:, 512:1024].rearrange("p (h d) -> p h d", h=8),
                                               in0=bview(0).rearrange("p (h d) -> p h d", h=8),
                                               in1=rstd8[:, 0:8].unsqueeze(2).to_broadcast([128, 8, 64]),
                                               op=ALU.mult), reads=[bankB[0], B.rstd8], writes=[B.scrA])
            E("dve", lambda e: e.tensor_tensor(out=qn.rearrange("p (h d) -> p h d", h=8),
                                               in0=scrA[:, 512:1024].rearrange("p (h d) -> p h d", h=8),
                                               in1=gq_b, op=ALU.mult), reads=[B.scrA, B.consts], writes=[B.qn])
            for j in range(4):
                E("pe", lambda e, j=j: e.transpose(out=tv[:, j * 128:(j + 1) * 128], in_=qn[:, j * 128:(j + 1) * 128],
                                                   identity=ident), reads=[B.qn, B.ident], writes=[bankB[2]])
            tvq = tv[:, 0:512].rearrange("p (j t) -> p j t", j=4)
            E("act", lambda e: e.activation(out=qz[0][0:64], in_=tvq[0:64], func=AF.Copy), reads=[bankB[2]],
              writes=[B.qz])
            E("dve", lambda e: e.tensor_copy(out=qz[1][64:128], in_=tvq[64:128]), reads=[bankB[2]],
              writes=[B.qz], append=True)
            cut(13)
            kv_project(own_slot, want_out, out_row)
            cut(14)
            dti = 1 if sample else 0
            for h in range(4):
                E("pe", lambda e, h=h: e.matmul(out=bview(3)[:, h * 128:(h + 1) * 128],
                                                lhsT=kT[:, h * 128:(h + 1) * 128], rhs=qT[:, h * 128:(h + 1) * 128],
                                                start=True, stop=True), reads=[B.kT, B.qT], writes=[bankB[3]])
            E("dve", lambda e: e.tensor_tensor(out=sTm, in0=bview(3), in1=DTs[dti], op=ALU.mult),
              reads=[bankB[3], B.consts], writes=[B.sTm])
            nst = 2 if sample else 1
            for s in range(nst):
                xidx = 1 + s if sample else 0
                E("dve", lambda e, s=s, xidx=xidx: e.tensor_tensor(out=qxT[s], in0=qT, in1=xiS[xidx], op=ALU.mult),
                  reads=[B.qT, B.consts], writes=[B.qxT[s]])
            for h in range(4):
                hs = slice(h * 128, (h + 1) * 128)
                E("pe", lambda e, hs=hs: e.matmul(out=bview(4)[:, hs], lhsT=sTm[:, hs], rhs=vr[:, hs],
                                                  start=True, stop=False), reads=[B.sTm, B.vr], writes=[bankB[4]])
                for s in range(nst):
                    E("pe", lambda e, hs=hs, s=s: e.matmul(out=bview(4)[:, hs], lhsT=qxT[s][:, hs], rhs=Rb[s][:, hs],
                                                           start=False, stop=(s == nst - 1)),
                      reads=[B.qxT[s], B.Rb[s]], writes=[bankB[4]])
            cut(15)
            T = 64 if sample else 128
            for s in range(nst):
                kzs, kzB = (kzb, B.kzb) if (sample and s == 1) else (kz, B.kz)
                for h in range(4):
                    hs = slice(h * 128, (h + 1) * 128)
                    E("pe", lambda e, hs=hs, kzs=kzs: e.matmul(out=bview(5)[:, hs], lhsT=kzs[:, hs], rhs=vr[:, hs],
                                                               start=True, stop=True),
                      reads=[kzB, B.vr], writes=[bankB[5]])
                for h in range(4):
                    hs = slice(h * 128, (h + 1) * 128)
                    gam = float(np.exp(T * LG[h]))
                    E("dve", lambda e, hs=hs, gam=gam, s=s: e.scalar_tensor_tensor(
                        out=Rf[s][:, hs], in0=Rf[s][:, hs], scalar=gam, in1=bview(5)[:, hs], op0=ALU.mult,
                        op1=ALU.add), reads=[B.Rf[s], bankB[5]], writes=[B.Rf[s]], append=(h > 0))
                if sample:
                    E("pool", lambda e, s=s: e.dma_start(out=RS[seqs[s]].rearrange("h p v -> p h v"),
                                                         in_=Rf[s].rearrange("p (h v) -> p h v", h=4)),
                      reads=[B.Rf[s]], dma=True, final=True)
                else:
                    E("act", lambda e, s=s: e.activation(out=Rb[s], in_=Rf[s], func=AF.Copy),
                      reads=[B.Rf[s]], writes=[B.Rb[s]])
            cut(16)
            for h in range(4):
                hs = slice(h * 128, (h + 1) * 128)
                E("act", lambda e, hs=hs, h=h: e.activation(out=scrA[:, hs], in_=bview(4)[:, hs], func=AF.Square,
                                                            accum_out=ss4[:, h:h + 1]),
                  reads=[bankB[4]], writes=[B.scrA, B.ss4], append=(h > 0))
            E("act", lambda e: e.activation(out=rstd4, in_=ss4, func=AF.Sqrt, bias=epsc[:, 0:1], scale=1.0 / 128), reads=[B.ss4, B.consts], writes=[B.rstd4])
            E("dve", lambda e: e.reciprocal(out=rstd4, in_=rstd4), reads=[B.rstd4], writes=[B.rstd4])
            for h in range(4):
                hs = slice(h * 128, (h + 1) * 128)
                E("dve", lambda e, hs=hs, h=h: e.scalar_tensor_tensor(
                    out=mixed[:, hs], in0=bview(4)[:, hs], scalar=rstd4[:, h:h + 1], in1=Gs[:, hs], op0=ALU.mult,
                    op1=ALU.mult), reads=[bankB[4], B.rstd4, B.Gs], writes=[B.mixed], append=True)
            cut(17)
            if prefetch is not None:
                prefetch()
            attention(keys)
            cut(18)
            for k in range(8):
                E("pe", lambda e, k=k: e.transpose(out=tv[:, k * 128:(k + 1) * 128], in_=mixed[:, k * 128:(k + 1) * 128],
                                                   identity=ident), reads=[B.mixed, B.ident], writes=[bankB[2]])
            E("act", lambda e: e.activation(out=mixedT, in_=tv.rearrange("p (k t) -> p k t", k=8), func=AF.Copy),
              reads=[bankB[2]], writes=[B.mixedT])
            for nb in range(2):
                for k in range(8):
                    E("pe", lambda e, k=k, nb=nb: e.matmul(out=bview(nb), lhsT=mixedT[:, k, :],
                                                           rhs=w_out_sb[:, k, nb * 512:(nb + 1) * 512],
                                                           start=(k == 0), stop=(k == 7)),
                      reads=[B.mixedT, B.w_out], writes=[bankB[nb]])
                E("dve", lambda e, nb=nb: e.tensor_tensor(out=hbuf[:, nb * 512:(nb + 1) * 512], in0=bview(nb),
                                                          in1=xs[slot][:, nb * 512:(nb + 1) * 512], op=ALU.add),
                  reads=[bankB[nb], B.xs[slot]], writes=[B.hbuf], append=(nb == 1))
            E("pool", lambda e: e.dma_start(out=HS[ti * 128:(ti + 1) * 128, :], in_=hbuf), reads=[B.hbuf],
              writes=[B.HSt[ti]], dma=True)

        zero_col = hmask[:, 1:2]
        halo_col = hmask[:, 0:1]

        jobs = [("sample", st) for st in range(NST)] + [("halo", g) for g in range(4)] + \
               [("prompt", i) for i in range(NPT)]

        def job_front(jx):
            kind, idx = jobs[jx]
            par = jx % 2
            if kind == "sample":
                load_rope(NPT + idx, par)
                load_sample_state(2 * idx, 0)
                load_sample_state(2 * idx + 1, 1)
                row0 = 512 + TPC + idx * 128
            elif kind == "halo":
                row0 = idx * 128
            else:
                load_rope(idx, par)
                row0 = 512 + idx * 128
            front(row0, par, 0, X, xs[par], B.xs[par], xb_t=xb2[par], xb_B=B.xb2[par], uT_t=uT2[par],
                  uT_B=B.uT2[par], tb=3)

        def job_body(jx):
            kind, idx = jobs[jx]
            par = jx % 2
            pf = (lambda: job_front(jx + 1)) if jx + 1 < len(jobs) else None
            if kind == "sample":
                st = idx
                seqs = (2 * st, 2 * st + 1)
                own = 5
                keys = []
                for s_ in range(2):
                    for j in range(4):
                        sl = (s_ * 4 + j) % 5
                        keys.append((sl, biasS[:, (s_ * 4 + j) * 1024:(s_ * 4 + j + 1) * 1024], B.biasS, zero_col,
                                     (lambda sq=seqs[s_], j=j, sl=sl: load_cache_tile(sq, j, sl))))
                keys.append((own, biasS[:, 8 * 1024:9 * 1024], B.biasS, zero_col, None))
                tileA(NPT + st, 0, par, own, keys, True, True, (4 + st) * 128, seqs=seqs, prefetch=pf)
                if st == NST - 1:
                    E("dve", lambda e: e.tensor_copy(out=Rf[0], in_=Sacc), reads=[B.Sacc], writes=[B.Rf[0]])
                    E("act", lambda e: e.activation(out=Rb[0], in_=Sacc, func=AF.Copy), reads=[B.Sacc],
                      writes=[B.Rb[0]])
            elif kind == "halo":
                cur_par[0] = par
                if pf is not None:
                    pf()
                cur_par[0] = par
                kv_project(idx % NSLOT, False, 0)
            else:
                i = idx
                gq = 4 + i
                keys = []
                for j in range(5):
                    g = gq - 4 + j
                    keys.append((g % NSLOT, biasP[:, j * 1024:(j + 1) * 1024], B.biasP,
                                 halo_col if g < 4 else zero_col, None))
                want = i >= NPT - 4
                tileA(i, 0, par, gq % NSLOT, keys, False, want, (i - (NPT - 4)) * 128 if want else 0, prefetch=pf)

        job_front(0)
        for jx in range(len(jobs)):
            job_body(jx)

        if STAGE >= 5:
            E("pool", lambda e: e.dma_start(out=RP.rearrange("h p v -> p h v"),
                                            in_=Rf[0].rearrange("p (h v) -> p h v", h=4)),
              reads=[B.Rf[0]], dma=True, final=True)
        def wload(dst, src, buf):
            E("pool", lambda e: e.dma_start(out=dst, in_=src), writes=[buf], dma=True, append=True)

        allA = [B.w_in, B.w_out, B.biasP, B.biasS, B.xb, B.uT, B.qrot, B.krot, B.kz, B.vr, B.qn, B.kn, B.qT, B.kT,
                B.sTm, B.qnT, B.qz, B.kzb, B.xb2[1], B.uT2[1], B.mixed, B.mixedT, B.kcb] + B.pT + B.knT + B.vaug + B.qxT + B.Rb
        allAf = [B.Gs, B.scrB, B.scrC, B.kvout, B.Sacc] + B.Rf + B.ropeS
        allBp = [B.gpl, B.uTB, B.actT, B.ubB, B.u2T, B.ppT, B.silu_s] + B.pS
        first = [True]

        def wload_all(dst, src, buf):
            extra = (allA + allAf) if first[0] else []
            first[0] = False
            E("pool", lambda e: e.dma_start(out=dst, in_=src), writes=[buf] + extra, dma=True, append=not extra)

        if STAGE < 6:
            pg.muted = True
        E("pool", lambda e: e.memset(dmy[:, 1:2], 0.0), writes=allA + allAf + allBp + [B.dummy, B.consts])
        barrier_deps = [B.dummy]
        Wg_v = Wg.rearrange("(k p) n -> p k n", p=128)
        Wu_v = Wu.rearrange("(k p) n -> p k n", p=128)
        Wd_v = Wd.rearrange("(f p) n -> p f n", p=128)
        Wpg_v = Wpg.rearrange("(k p) n -> p k n", p=128)
        Wpp_v = Wpp.rearrange("(k p) n -> p k n", p=128)
        for k in range(8):
            E("pool", lambda e, k=k: e.dma_start(out=wg_sb[:, k, :], in_=Wg_v[:, k, :]), reads=barrier_deps,
              writes=[B.wg], dma=True, append=True)
            E("pool", lambda e, k=k: e.dma_start(out=wu_sb[:, k, :], in_=Wu_v[:, k, :]), reads=barrier_deps,
              writes=[B.wu], dma=True, append=True)
        for f in range(NFC):
            E("pool", lambda e, f=f: e.dma_start(out=wd_sb[:, f, :], in_=Wd_v[:, f, :]), reads=barrier_deps,
              writes=[B.wd], dma=True, append=True)
        for k in range(8):
            E("pool", lambda e, k=k: e.dma_start(out=wpg_sb[:, k, :], in_=Wpg_v[:, k, :]), reads=barrier_deps,
              writes=[B.wpg], dma=True, append=True)
        for k in range(2):
            E("pool", lambda e, k=k: e.dma_start(out=wpp_sb[:, k, :], in_=Wpp_v[:, k, :]), reads=barrier_deps,
              writes=[B.wpp], dma=True, append=True)

        NBLK = NT // 2
        for blk in range((NBLK if STAGE >= 7 else 1) if STAGE >= 6 else 0):
            tiles = (2 * blk, 2 * blk + 1)
            for q, ti in enumerate(tiles):
                E("sp", lambda e, q=q, ti=ti: e.dma_start(out=xs[q], in_=HS[ti * 128:(ti + 1) * 128, :]),
                  reads=[B.HSt[ti], B.dummy], writes=[B.xs[q]], dma=True)
                E("sp", lambda e, q=q, ti=ti: e.dma_start(out=pS[q], in_=PPd[ti * 128:(ti + 1) * 128, :]),
                  reads=[B.dummy], writes=[B.pS[q]], dma=True)
                front(0, q, 8, None, xs[q], B.xs[q], xb_t=ubB, xb_B=B.ubB, uT_t=uTB, uT_B=B.uTB, tb=0, tok_off=q * 128)
            for f in range(NFC):
                gb = 1 + (f % 2) * 2
                ub_ = gb + 1
                for (bk, wsb, wB) in ((gb, wg_sb, B.wg), (ub_, wu_sb, B.wu)):
                    for k in range(8):
                        E("pe", lambda e, k=k, bk=bk, wsb=wsb, f=f: e.matmul(
                            out=bview(bk)[:, 0:256], lhsT=wsb[:, k, f * 128:(f + 1) * 128],
                            rhs=uTB[:, k, :], start=(k == 0), stop=(k == 7)),
                          reads=[B.uTB, wB], writes=[bankB[bk]])
                E("act", lambda e, gb=gb: e.activation(out=silu_s, in_=bview(gb)[:, 0:256], func=AF.Sigmoid),
                  reads=[bankB[gb]], writes=[B.silu_s])
                E("dve", lambda e, gb=gb: e.tensor_tensor(out=silu_s, in0=bview(gb)[:, 0:256], in1=silu_s,
                                                          op=ALU.mult),
                  reads=[bankB[gb], B.silu_s], writes=[B.silu_s])
                E("dve", lambda e, ub_=ub_, f=f: e.tensor_tensor(out=actT[:, f, :], in0=bview(ub_)[:, 0:256],
                                                                 in1=silu_s, op=ALU.mult),
                  reads=[bankB[ub_], B.silu_s], writes=[B.actT], append=(f > 0))
            for q, ti in enumerate(tiles):
                for nb in range(2):
                    bk = 5 + nb
                    for f in range(NFC):
                        E("pe", lambda e, f=f, nb=nb, bk=bk, q=q: e.matmul(
                            out=bview(bk), lhsT=actT[:, f, q * 128:(q + 1) * 128],
                            rhs=wd_sb[:, f, nb * 512:(nb + 1) * 512], start=(f == 0), stop=(f == NFC - 1)),
                          reads=[B.actT, B.wd], writes=[bankB[bk]])
                    E("dve", lambda e, nb=nb, bk=bk, q=q: e.tensor_tensor(
                        out=hbuf[:, nb * 512:(nb + 1) * 512], in0=bview(bk), in1=xs[q][:, nb * 512:(nb + 1) * 512],
                        op=ALU.add), reads=[bankB[bk], B.xs[q]], writes=[B.hbuf], append=(nb == 1))
                front(0, 0, 16, None, hbuf, B.hbuf, xb_t=ubB, xb_B=B.ubB, uT_t=u2T, uT_B=B.u2T, tb=0)
                E("act", lambda e, q=q: e.activation(out=ubB[:, 0:256], in_=pS[q], func=AF.Copy),
                  reads=[B.pS[q]], writes=[B.ubB])
                tv0 = bview16(0)
                for k in range(2):
                    E("pe", lambda e, k=k: e.transpose(out=tv0[:, k * 128:(k + 1) * 128],
                                                       in_=ubB[:, k * 128:(k + 1) * 128], identity=ident),
                      reads=[B.ubB, B.ident], writes=[bankB[0]])
                E("act", lambda e: e.activation(out=ppT, in_=tv0[:, 0:256].rearrange("p (k t) -> p k t", k=2),
                                                func=AF.Copy), reads=[bankB[0]], writes=[B.ppT])
                for nb in range(2):
                    gbk = 7
                    pbk = 0
                    for k in range(8):
                        E("pe", lambda e, k=k, nb=nb: e.matmul(out=bview(gbk), lhsT=u2T[:, k, :],
                                                               rhs=wpg_sb[:, k, nb * 512:(nb + 1) * 512],
                                                               start=(k == 0), stop=(k == 7)),
                          reads=[B.u2T, B.wpg], writes=[bankB[gbk]])
                    E("act", lambda e, nb=nb: e.activation(out=gpl[:, nb * 512:(nb + 1) * 512], in_=bview(gbk),
                                                           func=AF.Sigmoid), reads=[bankB[gbk]], writes=[B.gpl])
                    for k in range(2):
                        E("pe", lambda e, k=k, nb=nb: e.matmul(out=bview(pbk), lhsT=ppT[:, k, :],
                                                               rhs=wpp_sb[:, k, nb * 512:(nb + 1) * 512],
                                                               start=(k == 0), stop=(k == 1)),
                          reads=[B.ppT, B.wpp], writes=[bankB[pbk]])
                    E("dve", lambda e, nb=nb: e.tensor_tensor(out=gpl[:, nb * 512:(nb + 1) * 512], in0=bview(pbk),
                                                              in1=gpl[:, nb * 512:(nb + 1) * 512], op=ALU.mult),
                      reads=[bankB[pbk], B.gpl], writes=[B.gpl])
                    E("dve", lambda e, nb=nb: e.tensor_tensor(out=scrA[:, nb * 512:(nb + 1) * 512],
                                                              in0=gpl[:, nb * 512:(nb + 1) * 512],
                                                              in1=hbuf[:, nb * 512:(nb + 1) * 512], op=ALU.add),
                      reads=[B.gpl, B.hbuf], writes=[B.scrA], append=(nb == 1))
                E("pool", lambda e, ti=ti: e.dma_start(out=Y[ti * 128:(ti + 1) * 128, :], in_=scrA),
                  reads=[B.scrA], dma=True, final=True)

        pg.finalize()
        if os.environ.get('KVERB'):
            print('SIGCOUNTS', {e: max([o.val for o in pg.ops[e] if not o.dma and o.ext is None] + [0]) for e in ENGS}, 'dma', [d.val if d else 0 for d in pg.dma_last])
        n_ep = pg.epoch + 1
        for ep in range(n_ep):
            last = (ep == n_ep - 1)
            with nc.Block() as block:
                @block.tensor
                def _(pe):
                    pg.replay("pe", pe, sems, dma_sems, epoch=ep)

                @block.scalar
                def _(act):
                    pg.replay("act", act, sems, dma_sems, epoch=ep)

                @block.vector
                def _(dve):
                    pg.replay("dve", dve, sems, dma_sems, epoch=ep)

                @block.gpsimd
                def _(pool):
                    pg.replay("pool", pool, sems, dma_sems, epoch=ep)

                @block.sync
                def _(sp):
                    pg.replay("sp", sp, sems, dma_sems, extra_final=last, epoch=ep)
    if os.environ.get('KVERB'):
        print('ICOUNT', pg.icount, 'WCOUNT', pg.wcount)
    return nc


_NC_CACHE = {}


def _get_nc():
    if "nc" not in _NC_CACHE:
        _NC_CACHE["nc"] = build_program()
    return _NC_CACHE["nc"]


def make_in_maps(x_prompt, x_sample, cache_attn_k, cache_attn_v, state_ret, p_prompt, p_sample,
                 g_mix, w_in, g_ret_out, g_q_attn, g_k_attn, rel_bias, w_out, g_ffn,
                 w_ffn_gate, w_ffn_up, w_ffn_down, g_ple, w_ple_gate, w_ple_proj):
    f = lambda a: np.ascontiguousarray(np.asarray(a, dtype=np.float32))
    xp = f(x_prompt).reshape(SEQ, D)
    xsm = f(x_sample).reshape(32 * 64, D)
    pp = f(p_prompt).reshape(SEQ, 256)
    psm = f(p_sample).reshape(32 * 64, 256)
    ck = f(cache_attn_k).reshape(32, 512, 512)
    cv = f(cache_attn_v).reshape(32, 512, 512)
    st = f(state_ret).reshape(32, 4, 128, 128)
    gT = np.concatenate([f(g).reshape(8, 128).T for g in (g_mix, g_ffn, g_ple)], axis=1)
    gqk = np.concatenate([f(g_q_attn).reshape(1, 64), f(g_k_attn).reshape(1, 64)], axis=1)
    DTt, xit, zt = _decay_tables()
    bP, bS = _bias_tables(rel_bias)
    ident = np.eye(128, dtype=np.float32)
    shared = {
        "w_in": f(w_in).reshape(D, 3584), "w_out": f(w_out).reshape(D, D), "w_gate": f(w_ffn_gate).reshape(D, DFF),
        "w_up": f(w_ffn_up).reshape(D, DFF), "w_down": f(w_ffn_down).reshape(DFF, D),
        "w_pg": f(w_ple_gate).reshape(D, D), "w_pp": f(w_ple_proj).reshape(256, D),
        "gT": np.ascontiguousarray(gT), "gret": f(g_ret_out).reshape(1, 512), "gqk": np.ascontiguousarray(gqk),
        "DTt": DTt, "xit": xit, "zt": zt, "biasP": bP, "biasS": bS, "ident": ident,
    }
    maps = []
    for c in range(NCORES):
        halo = xp[c * TPC - 512:c * TPC] if c > 0 else np.zeros((512, D), np.float32)
        Xc = np.concatenate([halo, xp[c * TPC:(c + 1) * TPC], xsm[c * 256:(c + 1) * 256]], axis=0)
        PPc = np.concatenate([pp[c * TPC:(c + 1) * TPC], psm[c * 256:(c + 1) * 256]], axis=0)
        XPc = np.zeros((NPRE_T * 128, D), np.float32)
        if c > 0:
            XPc[NPRE_T * 128 - c * TPC:] = xp[0:c * TPC]
        ropeP, ztp = _prepass_tables(c)
        hm = np.zeros((128, 2), np.float32)
        if c == 0:
            hm[:, 0] = NEG
        m = dict(shared)
        m.update({
            "X": np.ascontiguousarray(Xc), "PP": np.ascontiguousarray(PPc),
            "CK": np.ascontiguousarray(ck[4 * c:4 * c + 4].reshape(2048, 512)),
            "CV": np.ascontiguousarray(cv[4 * c:4 * c + 4].reshape(2048, 512)),
            "ST": np.ascontiguousarray(st[4 * c:4 * c + 4]),
            "rope": _rope_tables(c), "hmask": hm, "XP": XPc, "ropeP": ropeP, "ztp": ztp,
        })
        maps.append(m)
    return maps


def assemble(results):
    y_p = np.concatenate([r["Y"][0:TPC] for r in results], axis=0).reshape(1, SEQ, D)
    y_s = np.concatenate([r["Y"][TPC:TPC + 256] for r in results], axis=0).reshape(32, 64, D)
    ret_p = results[7]["RP"].reshape(1, 1, 4, 128, 128)
    k_p = results[7]["KO"][0:512].reshape(1, 1, 512, 8, 64)
    v_p = results[7]["VO"][0:512].reshape(1, 1, 512, 8, 64)
    ret_s = np.concatenate([r["RS"] for r in results], axis=0).reshape(1, 32, 4, 128, 128)
    k_s = np.concatenate([r["KO"][512:768] for r in results], axis=0).reshape(1, 32, 64, 8, 64)
    v_s = np.concatenate([r["VO"][512:768] for r in results], axis=0).reshape(1, 32, 64, 8, 64)
    out = (y_p, y_s, ret_p, k_p, v_p, ret_s, k_s, v_s)
    return tuple(np.ascontiguousarray(a, dtype=np.float32) for a in out)


def kernel(**inputs):
    nc = _get_nc()
    in_maps = make_in_maps(**inputs)
    res = run_bass_kernel_spmd(nc, in_maps, core_ids=list(range(NCORES)))
    return assemble(res.results)
```
